# Optimizing a Trainium2 kernel written in Bass

```python
import math
import jax, jax.numpy as jnp
from jax import lax
import numpy as np

D_MODEL = 1024
BATCH = 4
SEQ = 8192
DEPTH = 2

CHUNK = 64
Q_BLOCK = 128
D_BRANCH = D_MODEL // 2
D_MIX = 3 * D_BRANCH
ATTN_HEADS = D_BRANCH // 128
ATTN_V_DIM = D_BRANCH // ATTN_HEADS
ATTN_QK_DIM = ATTN_V_DIM // 2
GDN_HEADS = D_BRANCH // 128
GDN_HEAD_DIM = D_BRANCH // GDN_HEADS
GDN_CONV = 4
LRU_BLOCKS = D_BRANCH // 128
LRU_BLOCK_DIM = D_BRANCH // LRU_BLOCKS
LRU_CONV = 4
LRU_C = 8.0

NEG_INF = -1e30
SPLIT_SIZES = (D_BRANCH,) * 8 + (GDN_HEADS, GDN_HEADS) + (D_BRANCH, D_BRANCH)
IN_COLS = sum(SPLIT_SIZES)
SPLIT_POINTS = tuple(int(v) for v in np.cumsum(SPLIT_SIZES)[:-1])

kernel_name = "hybrid_diffattn_gdn_rglru_trunk"


def rmsnorm(x, w, eps=1e-6):
    xf = x.astype(jnp.float32)
    y = xf * lax.rsqrt(jnp.mean(xf * xf, axis=-1, keepdims=True) + eps)
    return (y * w.astype(jnp.float32)).astype(x.dtype)


def l2norm(x, eps=1e-6):
    return x * lax.rsqrt(jnp.sum(x * x, axis=-1, keepdims=True) + eps)


def causal_dwconv(x, w):
    K = w.shape[0]
    S = x.shape[1]
    xp = jnp.pad(x, ((0, 0), (K - 1, 0), (0, 0)))
    y = xp[:, 0:S] * w[0]
    for j in range(1, K):
        y = y + xp[:, j:j + S] * w[j]
    return y


def diff_attention(q, k, v, lam, lambda_init, subln_w):
    B, S, _ = q.shape
    H, dq, dv = ATTN_HEADS, ATTN_QK_DIM, ATTN_V_DIM
    q = q.reshape(B, S, H, 2, dq) * (dq ** -0.5)
    k = k.reshape(B, S, H, 2, dq)
    v = v.reshape(B, S, H, dv)
    nb = S // Q_BLOCK
    q_blocks = q.reshape(B, nb, Q_BLOCK, H, 2, dq).transpose(1, 0, 2, 3, 4, 5)
    key_chunk = jnp.arange(S) // CHUNK

    def one_block(args):
        qb, bi = args
        s = jnp.einsum('bqhmd,bkhmd->bhmqk', qb, k).astype(jnp.float32)
        q_chunk = (bi * Q_BLOCK + jnp.arange(Q_BLOCK)) // CHUNK
        mask = key_chunk[None, :] <= q_chunk[:, None]
        p = jax.nn.softmax(jnp.where(mask, s, NEG_INF), axis=-1)
        attn = p[:, :, 0] - lam * p[:, :, 1]
        return jnp.einsum('bhqk,bkhd->bqhd', attn.astype(v.dtype), v)

    o = lax.map(one_block, (q_blocks, jnp.arange(nb)))
    o = o.transpose(1, 0, 2, 3, 4).reshape(B, S, H, dv)
    o = rmsnorm(o, subln_w, 1e-5) * (1.0 - lambda_init)
    return o.reshape(B, S, H * dv)


def chunk_gated_delta_rule(q, k, v, beta, g):
    B, S, H, Dk = q.shape
    Dv = v.shape[-1]
    N = S // CHUNK

    def to_chunks(t):
        return t.reshape(B, N, CHUNK, H, -1).transpose(0, 3, 1, 2, 4)

    q, k, v = to_chunks(q), to_chunks(k), to_chunks(v)
    beta = beta.reshape(B, N, CHUNK, H).transpose(0, 3, 1, 2)
    g = jnp.cumsum(g.reshape(B, N, CHUNK, H).transpose(0, 3, 1, 2), axis=-1)
    tri = jnp.tril(jnp.ones((CHUNK, CHUNK), dtype=bool))
    strict = jnp.tril(jnp.ones((CHUNK, CHUNK), dtype=bool), -1)
    decay = jnp.exp(jnp.where(tri, g[..., :, None] - g[..., None, :], -jnp.inf))
    k_beta = k * beta[..., None]
    v_beta = v * beta[..., None]
    Lmat = jnp.where(strict, jnp.einsum('bhnid,bhnjd->bhnij', k_beta, k) * decay, 0.0)
    eye = jnp.broadcast_to(jnp.eye(CHUNK, dtype=q.dtype), Lmat.shape)
    T = lax.linalg.triangular_solve(eye + Lmat, eye, left_side=True, lower=True,
                                    unit_diagonal=True)
    u = jnp.einsum('bhnij,bhnjd->bhnid', T, v_beta)
    w = jnp.einsum('bhnij,bhnjd->bhnid', T, k_beta * jnp.exp(g)[..., None])
    a_intra = jnp.where(tri, jnp.einsum('bhnid,bhnjd->bhnij', q, k) * decay, 0.0)

    def step(state, xs):
        q_i, k_i, u_i, w_i, g_i, a_i = xs
        v_new = u_i - jnp.einsum('bhcd,bhde->bhce', w_i, state)
        o = (jnp.einsum('bhcd,bhde->bhce', q_i * jnp.exp(g_i)[..., None], state)
             + jnp.einsum('bhij,bhje->bhie', a_i, v_new))
        g_last = g_i[..., -1]
        state = (state * jnp.exp(g_last)[..., None, None]
                 + jnp.einsum('bhcd,bhce->bhde', k_i * jnp.exp(g_last[..., None] - g_i)[..., None], v_new))
        return state, o

    mv = lambda t: jnp.moveaxis(t, 2, 0)
    state0 = jnp.zeros((B, H, Dk, Dv), dtype=q.dtype)
    _, o = lax.scan(step, state0, (mv(q), mv(k), mv(u), mv(w), mv(g), mv(a_intra)))
    return o.transpose(1, 0, 3, 2, 4).reshape(B, S, H, Dv)


def gated_deltanet(q, k, v, b, a, z, conv_w, a_log, dt_bias, norm_w):
    B, S, _ = q.shape
    H, D = GDN_HEADS, GDN_HEAD_DIM
    qkv = jax.nn.silu(causal_dwconv(jnp.concatenate([q, k, v], axis=-1), conv_w)).astype(jnp.float32)
    q, k, v = jnp.split(qkv, 3, axis=-1)
    q = l2norm(q.reshape(B, S, H, D)) * (D ** -0.5)
    k = l2norm(k.reshape(B, S, H, D))
    v = v.reshape(B, S, H, D)
    beta = jax.nn.sigmoid(b.astype(jnp.float32))
    g = -jnp.exp(a_log.astype(jnp.float32)) * jax.nn.softplus(a.astype(jnp.float32) + dt_bias.astype(jnp.float32))
    o = chunk_gated_delta_rule(q, k, v, beta, g)
    o = rmsnorm(o, norm_w) * jax.nn.silu(z.reshape(B, S, H, D).astype(jnp.float32))
    return o.reshape(B, S, H * D)


def rg_lru(xb, conv_w, conv_b, gate_w, gate_b, lam_param):
    B, S, C = xb.shape
    xc = (causal_dwconv(xb, conv_w) + conv_b).astype(jnp.float32)
    xg = xc.reshape(B, S, LRU_BLOCKS, LRU_BLOCK_DIM)
    gates = (jnp.einsum('bsnd,gnde->gbsne', xg, gate_w.astype(jnp.float32))
             + gate_b.astype(jnp.float32).reshape(2, 1, 1, LRU_BLOCKS, LRU_BLOCK_DIM)).reshape(2, B, S, C)
    i_t = jax.nn.sigmoid(gates[0])
    r_t = jax.nn.sigmoid(gates[1])
    log_a = -LRU_C * r_t * jax.nn.softplus(-lam_param.astype(jnp.float32))
    a_t = jnp.exp(log_a)
    b_t = jnp.sqrt(jnp.maximum(-jnp.expm1(2.0 * log_a), 0.0)) * (i_t * xc)

    def combine(e1, e2):
        a1, b1 = e1
        a2, b2 = e2
        return a1 * a2, a2 * b1 + b2

    _, h = lax.associative_scan(combine, (a_t, b_t), axis=1)
    return h


def setup_inputs(seed: int = 0) -> dict:
    key = jax.random.key(seed)
    ks = jax.random.split(key, 20)
    f32 = jnp.float32
    nrm = lambda k, shape, s: jax.random.normal(k, shape, f32) * s
    x = jax.random.normal(ks[0], (BATCH, SEQ, D_MODEL), f32)
    norm_w = 1.0 + nrm(ks[1], (DEPTH, D_MODEL), 0.01)
    w_in = nrm(ks[2], (DEPTH, D_MODEL, IN_COLS), D_MODEL ** -0.5)
    attn_lambda = nrm(ks[3], (DEPTH, 4, ATTN_QK_DIM), 0.1)
    attn_subln_w = 1.0 + nrm(ks[4], (DEPTH, ATTN_V_DIM), 0.01)
    gdn_conv_w = nrm(ks[5], (DEPTH, GDN_CONV, 3 * D_BRANCH), GDN_CONV ** -0.5)
    gdn_a_log = jnp.log(jax.random.uniform(ks[6], (DEPTH, GDN_HEADS), f32, 1.0, 16.0))
    dt = jnp.exp(jax.random.uniform(ks[7], (DEPTH, GDN_HEADS), f32, math.log(1e-3), math.log(1e-1)))
    gdn_dt_bias = dt + jnp.log(-jnp.expm1(-dt))
    gdn_norm_w = 1.0 + nrm(ks[8], (DEPTH, GDN_HEAD_DIM), 0.01)
    lru_conv_w = nrm(ks[9], (DEPTH, LRU_CONV, D_BRANCH), LRU_CONV ** -0.5)
    lru_conv_b = nrm(ks[10], (DEPTH, D_BRANCH), 0.01)
    lru_gate_w = nrm(ks[11], (DEPTH, 2, LRU_BLOCKS, LRU_BLOCK_DIM, LRU_BLOCK_DIM), LRU_BLOCK_DIM ** -0.5)
    lru_gate_b = nrm(ks[12], (DEPTH, 2, D_BRANCH), 0.01)
    u = jax.random.uniform(ks[13], (DEPTH, D_BRANCH), f32, 0.9, 0.999)
    s = u ** (1.0 / LRU_C)
    lru_log_param = jnp.log(s) - jnp.log1p(-s)
    w_out = nrm(ks[14], (DEPTH, D_MIX, D_MODEL), D_MIX ** -0.5)
    final_norm_w = 1.0 + nrm(ks[15], (D_MODEL,), 0.01)
    return {"x": x, "norm_w": norm_w, "w_in": w_in, "attn_lambda": attn_lambda,
            "attn_subln_w": attn_subln_w, "gdn_conv_w": gdn_conv_w, "gdn_a_log": gdn_a_log,
            "gdn_dt_bias": gdn_dt_bias, "gdn_norm_w": gdn_norm_w, "lru_conv_w": lru_conv_w,
            "lru_conv_b": lru_conv_b, "lru_gate_w": lru_gate_w, "lru_gate_b": lru_gate_b,
            "lru_log_param": lru_log_param, "w_out": w_out, "final_norm_w": final_norm_w}


def reference(x, norm_w, w_in, attn_lambda, attn_subln_w, gdn_conv_w, gdn_a_log, gdn_dt_bias,
              gdn_norm_w, lru_conv_w, lru_conv_b, lru_gate_w, lru_gate_b, lru_log_param,
              w_out, final_norm_w):
    dtype = x.dtype
    for l in range(DEPTH):
        h = rmsnorm(x, norm_w[l])
        u = jnp.einsum('bsd,de->bse', h, w_in[l])
        (a_q, a_k, a_v, a_z, g_q, g_k, g_v, g_z, g_b, g_a, r_x, r_z) = jnp.split(u, SPLIT_POINTS, axis=-1)
        lambda_init = 0.8 - 0.6 * math.exp(-0.3 * l)
        lp = attn_lambda[l].astype(jnp.float32)
        lam = jnp.exp(jnp.sum(lp[0] * lp[1])) - jnp.exp(jnp.sum(lp[2] * lp[3])) + lambda_init
        y_a = diff_attention(a_q, a_k, a_v, lam, lambda_init, attn_subln_w[l]) * jax.nn.silu(a_z)
        y_g = gated_deltanet(g_q, g_k, g_v, g_b, g_a, g_z, gdn_conv_w[l], gdn_a_log[l],
                             gdn_dt_bias[l], gdn_norm_w[l])
        y_r = rg_lru(r_x, lru_conv_w[l], lru_conv_b[l], lru_gate_w[l], lru_gate_b[l],
                     lru_log_param[l]) * jax.nn.silu(r_z.astype(jnp.float32))
        y = jnp.concatenate([y_a.astype(dtype), y_g.astype(dtype), y_r.astype(dtype)], axis=-1)
        x = x + jnp.einsum('bse,ed->bsd', y, w_out[l])
    return rmsnorm(x, final_norm_w)
```

```python
from contextlib import ExitStack
import numpy as np
import ml_dtypes
import concourse.bass as bass
import concourse.mybir as mybir
from concourse.bass_utils import run_bass_kernel_spmd

F32 = mybir.dt.float32
BF16 = mybir.dt.bfloat16
AF = mybir.ActivationFunctionType
ALU = mybir.AluOpType
AX = mybir.AxisListType

ENGS = ("pe", "act", "dve", "pool", "sp")
D = 1024
NH = 2
HB = NH * 128
NCOL = 10 * HB + 2 * NH
DEPTH = 2
LAMBDA_INIT = [0.8 - 0.6 * float(np.exp(-0.3 * l)) for l in range(DEPTH)]


_uid = [0]


def uname(name):
    _uid[0] += 1
    return f"{name}_u{_uid[0]}"


class Tok:
    __slots__ = ("name", "w", "r")

    def __init__(self, name=""):
        self.name = name
        self.w = None
        self.r = {}


class DSem:
    def __init__(self, prog, name, q="sp"):
        self.q = q
        if prog.free_d[q]:
            self.key, self.cnt = prog.free_d[q].pop()
        else:
            self.key = ("d", name)
            prog.sems[self.key] = prog._alloc_sem(name)
            self.cnt = 0
        prog.dsems.append(self)


class Prog:
    def __init__(self, nc, stack):
        self.nc = nc
        self.stack = stack
        self.sems = {}
        self.dsems = []
        self.free_d = {"sp": [], "pool": []}
        self.dcur = {}
        self.phase = 0
        self.esets = {}
        self.rec = None
        self.q = {e: [] for e in ENGS}
        self._new_engine_sems()

    def _alloc_sem(self, name):
        return self.stack.enter_context(self.nc.semaphore(name))

    NSETS = 4

    def _new_engine_sems(self):
        si = self.phase % self.NSETS
        if si not in self.esets:
            self.esets[si] = {}
            for e in ENGS:
                if e == "sp":
                    continue
                k = ("e", e, si)
                self.sems[k] = self._alloc_sem(f"s_{e}_{si}")
                self.esets[si][e] = [k, 0]
        self.eset = self.esets[si]
        self.ekey = {e: v[0] for e, v in self.eset.items()}
        self.cnt = {e: v[1] for e, v in self.eset.items()}
        self.cnt0 = dict(self.cnt)
        self.known = {e: {} for e in ENGS}

    def _wait(self, eng, ev):
        if ev is None:
            return
        k, v = ev
        if eng == "pe" and k == self.ekey.get("pe"):
            return
        if k[0] == "d":
            v = max(v, self.dcur.get(k, 0))
        if self.known[eng].get(k, 0) >= v:
            return
        self.known[eng][k] = v
        self.q[eng].append(("w", k, v))

    def _deps(self, eng, reads, writes):
        for t in reads:
            self._wait(eng, t.w)
        for t in writes:
            self._wait(eng, t.w)
            for k, v in t.r.items():
                self._wait(eng, (k, v))

    def _record(self, ev, reads, writes):
        k, v = ev
        for t in reads:
            if t.r.get(k, 0) < v:
                t.r[k] = v
        for t in writes:
            t.w = ev
            t.r = {}

    def run_merged(self, streams):
        streams = [list(x) for x in streams if x]
        idx = [0] * len(streams)
        live = True
        while live:
            live = False
            for si, st_ in enumerate(streams):
                if idx[si] < len(st_):
                    it = st_[idx[si]]
                    idx[si] += 1
                    live = True
                    if it[0] == "op":
                        self.op(*it[1:])
                    else:
                        self.dma(*it[1:5], reads=it[5], writes=it[6], **it[7])

    def op(self, eng, fn, reads=(), writes=()):
        if self.rec is not None:
            self.rec.append(("op", eng, fn, list(reads), list(writes)))
            return None
        self._deps(eng, reads, writes)
        self.cnt[eng] += 1
        ev = (self.ekey[eng], self.cnt[eng])
        self.q[eng].append(("i", fn, ev[0], 1))
        self._record(ev, reads, writes)
        return ev

    def dma(self, queue, dsem, out, in_, reads=(), writes=(), **kw):
        if self.rec is not None:
            self.rec.append(("dma", queue, dsem, out, in_, list(reads), list(writes), kw))
            return None
        assert queue == dsem.q, (queue, dsem.q)
        self._deps(queue, reads, writes)
        dsem.cnt += 16
        self.dcur[dsem.key] = dsem.cnt
        ev = (dsem.key, dsem.cnt)
        self.q[queue].append(("i", lambda e: e.dma_start(out=out, in_=in_, **kw), dsem.key, 16))
        self._record(ev, reads, writes)
        return ev

    def _replay(self, engname, eng):
        for item in self.q[engname]:
            if item[0] == "w":
                eng.wait_ge(self.sems[item[1]], item[2])
            else:
                _, fn, k, inc = item
                fn(eng).then_inc(self.sems[k], inc)
        self.q[engname] = []

    def flush(self):
        finals = [(self.ekey[e], self.cnt[e]) for e in self.cnt if self.cnt[e] > self.cnt0[e]]
        finals += [(d.key, d.cnt) for d in self.dsems if d.cnt > 0]
        for e in ENGS:
            for ev in finals:
                self._wait(e, ev)
        with self.nc.Block() as block:
            @block.tensor
            def _(e):
                self._replay("pe", e)

            @block.scalar
            def _(e):
                self._replay("act", e)

            @block.vector
            def _(e):
                self._replay("dve", e)

            @block.gpsimd
            def _(e):
                self._replay("pool", e)

            @block.sync
            def _(e):
                self._replay("sp", e)
        for e in self.cnt:
            self.eset[e][1] = self.cnt[e]
        for d in self.dsems:
            self.free_d[d.q].append((d.key, d.cnt))
        self.dsems = []
        old = self.known
        self.phase += 1
        self._new_engine_sems()
        for e in ENGS:
            for k, v in old[e].items():
                self.known[e][k] = v


class Slots:
    def __init__(self, P, st, nc, name, n, shape, dtype, q="sp"):
        self.t = [st.enter_context(nc.sbuf_tensor(uname(name), shape, dtype)) for i in range(n)]
        self.tok = [Tok(f"{name}{i}") for i in range(n)]
        self.ds = [DSem(P, uname("d_" + name), q) for i in range(n)]
        self.i = 0
        self.n = n

    def next(self):
        i = self.i
        self.i = (i + 1) % self.n
        return self.t[i], self.tok[i], self.ds[i]


C_AQ, C_AK, C_AV, C_AZ = 0, HB, 2 * HB, 3 * HB
C_GQ, C_GK, C_GV, C_GZ = 4 * HB, 5 * HB, 6 * HB, 7 * HB
C_GBA = 8 * HB
C_RX, C_RZ = 8 * HB + 2 * NH, 9 * HB + 2 * NH
RG = [[0, 1], [2, 3], [4, 5], [6, 7]]


def make_dram(nc, S, dbg):
    kind = "ExternalOutput" if dbg else "Internal"
    T = {}

    def mk(name, shape, dt, k=None):
        T[name] = nc.dram_tensor(name, shape, dt, kind=k or kind).ap()
    XC = min(S, 1024)
    mk("aqk", [2 * NH, 128, S], BF16)
    mk("av", [S, HB], BF16)
    mk("az", [S, HB], BF16)
    mk("gqkv", [3 * NH, 128, S], F32)
    mk("gz", [S, HB], BF16)
    mk("gba", [2 * NH, S], F32)
    mk("rx", [NH, 128, S], F32)
    mk("rz", [NH, 128, S], BF16)
    mk("gcs", [2 * NH, S], F32)
    mk("beta", [2 * NH, S], F32)
    mk("yT", [3 * NH, 128, S], BF16)
    mk("yTg", [3 * NH, 2 * 128, S], BF16, "Internal")
    mk("xh1", [S, 512], F32)
    mk("xh2", [S, 512], F32)
    mk("x1g", [S // XC, 2 * XC, 512], F32, "Internal")
    mk("x2g", [S // XC, 2 * XC, 512], F32, "Internal")
    return T


def allgather(nc, P, pairs):
    with ExitStack() as st:
        ds = DSem(P, uname("d_cc"), "pool")
        for src, dst in pairs:
            P._deps("pool", [], [])
            ds.cnt += 1
            P.dcur[ds.key] = ds.cnt
            P.q["pool"].append(("i", lambda e, src=src, dst=dst: e.collective_compute(
                "AllGather", ALU.bypass, replica_groups=RG, ins=[src], outs=[dst]), ds.key, 1))
        P.flush()


def phase_A(nc, P, l, S, x_src, I, T):
    XC = min(S, 1024)
    NB = S // 512
    with ExitStack() as st:
        sb = lambda name, shape, dt: st.enter_context(nc.sbuf_tensor(uname(name), shape, dt))
        Wb = sb("Wb", [128, 8, NCOL], BF16)
        nw = sb("nw", [128, 8], F32)
        ident = sb("ident", [128, 128], BF16)
        xs = sb("xs", [128, 4, D], BF16)
        junk = sb("junk", [128, D], BF16)
        ss = sb("ss", [128, 4], F32)
        rstd = sb("rstd", [128, 4], F32)
        t_W, t_nw, t_id, t_xs, t_junk, t_ss, t_rstd = (Tok() for _ in range(7))
        d_c = DSem(P, uname("dA_c"))
        P.dma("sp", d_c, nw[:], I["norm_w"][l].rearrange("(k p) -> p k", p=128),
              writes=[t_nw], allow_slow_non_contiguous=True)
        P.dma("sp", d_c, ident[:], I["ident"], writes=[t_id])
        wst = Slots(P, st, nc, "wst", 2, [128, NCOL // 2], F32)
        for kc in range(8):
            for hf in range(2):
                t, tok, ds = wst.next()
                c0 = hf * (NCOL // 2)
                P.dma("sp", ds, t[:], I["w_in"][l, kc * 128:(kc + 1) * 128, c0:c0 + NCOL // 2], writes=[tok])
                eng = "dve" if hf == 0 else "pool"
                P.op(eng, lambda e, t=t, kc=kc, c0=c0: e.tensor_scalar(
                    out=Wb[:, kc, c0:c0 + NCOL // 2], in0=t[:], scalar1=nw[:, kc:kc + 1], scalar2=None,
                    op0=ALU.mult), reads=[tok, t_nw], writes=[t_W])
        xts = Slots(P, st, nc, "xt", 2, [128, 4, D], F32)
        hTs = [sb("hT", [128, 8, 512], BF16) for i in range(2)]
        t_hT = [Tok(), Tok()]
        ptr = [st.enter_context(nc.psum_tensor(uname("ptr"), [128, 1024], BF16)) for i in range(2)]
        t_ptr = [Tok(), Tok()]
        pmm = [st.enter_context(nc.psum_tensor(uname("pmm"), [128, 512], F32)) for i in range(6)]
        t_pmm = [Tok() for _ in range(6)]
        so_f = Slots(P, st, nc, "sof", 4, [128, 512], F32)
        so_b = Slots(P, st, nc, "sob", 4, [128, 512], BF16)
        pi = 0
        ev_i = 0
        xq = []

        def load_x(b):
            xt, t_xt, d_xt = xts.next()
            if x_src is not None:
                P.dma("sp", d_xt, xt[:], x_src[b * 512:(b + 1) * 512, :].rearrange("(j p) d -> p j d", p=128),
                      writes=[t_xt])
            else:
                tc_, off = (b * 512) // XC, (b * 512) % XC
                for r in range(2):
                    P.dma("sp", d_xt, xt[:, :, r * 512:(r + 1) * 512],
                          T["x1g"][tc_, r * XC + off:r * XC + off + 512, :].rearrange("(j p) d -> p j d", p=128),
                          writes=[t_xt])
            xq.append((xt, t_xt))
        load_x(0)
        for b in range(NB):
            xt, t_xt = xq.pop(0)
            if b + 1 < NB:
                load_x(b + 1)
            for j in range(4):
                P.op("act", lambda e, j=j, xt=xt: e.activation(out=junk[:], in_=xt[:, j, :], func=AF.Square,
                                                              accum_out=ss[:, j:j + 1]),
                     reads=[t_xt], writes=[t_junk, t_ss])
            P.op("dve", lambda e: e.tensor_scalar(out=rstd[:], in0=ss[:], scalar1=1.0 / D, scalar2=1e-6,
                                                  op0=ALU.mult, op1=ALU.add), reads=[t_ss], writes=[t_rstd])
            P.op("act", lambda e: e.activation(out=rstd[:], in_=rstd[:], func=AF.Sqrt), reads=[t_rstd], writes=[t_rstd])
            P.op("dve", lambda e: e.reciprocal(out=rstd[:], in_=rstd[:]), reads=[t_rstd], writes=[t_rstd])
            for j in range(4):
                if j % 2 == 0:
                    P.op("act", lambda e, j=j, xt=xt: e.activation(out=xs[:, j, :], in_=xt[:, j, :], func=AF.Copy,
                                                                  scale=rstd[:, j:j + 1]),
                         reads=[t_xt, t_rstd], writes=[t_xs])
                else:
                    P.op("dve", lambda e, j=j, xt=xt: e.tensor_scalar(out=xs[:, j, :], in0=xt[:, j, :],
                                                                     scalar1=rstd[:, j:j + 1], scalar2=None,
                                                                     op0=ALU.mult),
                         reads=[t_xt, t_rstd], writes=[t_xs])
            hT = hTs[b % 2]
            th = t_hT[b % 2]
            for kc in range(8):
                pt = ptr[(kc // 2) % 2]
                tp = t_ptr[(kc // 2) % 2]
                for j in range(4):
                    o = (kc % 2) * 512 + j * 128
                    P.op("pe", lambda e, pt=pt, o=o, j=j, kc=kc: e.transpose(
                        out=pt[:, o:o + 128], in_=xs[:, j, kc * 128:(kc + 1) * 128], identity=ident[:]),
                        reads=[t_xs, t_id], writes=[tp])
                eng = "dve" if (kc // 2) % 2 == 0 else "act"
                o = (kc % 2) * 512
                if eng == "dve":
                    P.op("dve", lambda e, pt=pt, o=o, kc=kc, hT=hT: e.tensor_copy(out=hT[:, kc, :], in_=pt[:, o:o + 512]),
                         reads=[tp], writes=[th])
                else:
                    P.op("act", lambda e, pt=pt, o=o, kc=kc, hT=hT: e.activation(out=hT[:, kc, :], in_=pt[:, o:o + 512],
                                                                               func=AF.Copy),
                         reads=[tp], writes=[th])
            tok0 = b * 512

            def emit(kind, c0, ncols, dst, func, outdt, j=None, hT=hT, th=th):
                nonlocal pi, ev_i
                ps = pmm[pi % 6]
                tps = t_pmm[pi % 6]
                pi += 1
                for kc in range(8):
                    if kind == "fm":
                        P.op("pe", lambda e, ps=ps, kc=kc: e.matmul(
                            out=ps[0:ncols, :], lhsT=Wb[:, kc, c0:c0 + ncols], rhs=hT[:, kc, :],
                            start=(kc == 0), stop=(kc == 7)), reads=[t_W, th], writes=[tps])
                    else:
                        P.op("pe", lambda e, ps=ps, kc=kc: e.matmul(
                            out=ps[:, 0:ncols], lhsT=hT[:, kc, j * 128:(j + 1) * 128], rhs=Wb[:, kc, c0:c0 + ncols],
                            start=(kc == 0), stop=(kc == 7)), reads=[t_W, th], writes=[tps])
                pool = so_f if outdt == F32 else so_b
                so, t_so, d_so = pool.next()
                src = ps[0:ncols, :] if kind == "fm" else ps[:, 0:ncols]
                dsto = so[0:ncols, :] if kind == "fm" else so[:, 0:ncols]
                if func is None and ev_i % 2 == 0:
                    P.op("dve", lambda e: e.tensor_copy(out=dsto, in_=src), reads=[tps], writes=[t_so])
                else:
                    P.op("act", lambda e: e.activation(out=dsto, in_=src, func=(func or AF.Copy)),
                         reads=[tps], writes=[t_so])
                ev_i += 1
                P.dma("sp", d_so, dst, dsto, reads=[t_so])

            for c in range(2 * NH):
                emit("fm", C_AQ + c * 128, 128, T["aqk"][c, :, tok0:tok0 + 512], None, BF16)
            for c in range(3 * NH):
                emit("fm", C_GQ + c * 128, 128, T["gqkv"][c, :, tok0:tok0 + 512], None, F32)
            emit("fm", C_GBA, 2 * NH, T["gba"][:, tok0:tok0 + 512], None, F32)
            for c in range(NH):
                emit("fm", C_RX + c * 128, 128, T["rx"][c, :, tok0:tok0 + 512], None, F32)
            for c in range(NH):
                emit("fm", C_RZ + c * 128, 128, T["rz"][c, :, tok0:tok0 + 512], AF.Silu, BF16)
            for j in range(4):
                r0 = tok0 + j * 128
                emit("tm", C_AV, HB, T["av"][r0:r0 + 128, :], None, BF16, j)
                emit("tm", C_AZ, HB, T["az"][r0:r0 + 128, :], AF.Silu, BF16, j)
                emit("tm", C_GZ, HB, T["gz"][r0:r0 + 128, :], AF.Silu, BF16, j)
        P.flush()


class Ctx:
    def __init__(self, nc, P, st):
        self.nc, self.P, self.st = nc, P, st

    def sb(self, name, shape, dt):
        return self.st.enter_context(self.nc.sbuf_tensor(uname(name), shape, dt))

    def ps(self, name, shape, dt=F32):
        return self.st.enter_context(self.nc.psum_tensor(uname(name), shape, dt))

    def slots(self, name, n, shape, dt, q="sp"):
        return Slots(self.P, self.st, self.nc, name, n, shape, dt, q)

    def dsem(self, name):
        return DSem(self.P, uname(name))


def phase_E(nc, P, l, S, x_src, x_dst, I, T):
    NB = S // 512
    NCH = 6 * NH
    with ExitStack() as st:
        C = Ctx(nc, P, st)
        Wo = C.sb("Wo", [128, NCH, 512], BF16)
        t_Wo = Tok()
        wst = C.slots("wost", 2, [128, 512], F32)
        for c in range(NCH):
            t, tok, ds = wst.next()
            P.dma("sp", ds, t[:], I["w_out"][l, c * 128:(c + 1) * 128, :], writes=[tok])
            eng = "dve" if c % 2 == 0 else "pool"
            P.op(eng, lambda e, t=t, c=c: e.tensor_copy(out=Wo[:, c, :], in_=t[:]), reads=[tok], writes=[t_Wo])
        ys = C.slots("ys", 2, [128, NCH, 512], BF16)
        xts = C.slots("xe", 2, [128, 4, 512], F32)
        xo = C.slots("xo", 3, [128, 512], F32)
        pm = [C.ps("pe", [128, 512]) for _ in range(4)]
        t_pm = [Tok() for _ in range(4)]
        pi = 0
        lq_ = []

        def load_E(b):
            y, t_y, d_y = ys.next()
            xt, t_xt, d_xt = xts.next()
            for r in range(2):
                P.dma("sp", d_y, y[:, r * 3 * NH:(r + 1) * 3 * NH, :],
                      T["yTg"][:, r * 128:(r + 1) * 128, b * 512:(b + 1) * 512].rearrange("c p t -> p c t"), writes=[t_y])
            P.dma("sp", d_xt, xt[:], x_src[b * 512:(b + 1) * 512, :].rearrange("(j p) d -> p j d", p=128), writes=[t_xt])
            lq_.append((y, t_y, xt, t_xt))
        load_E(0)
        for b in range(NB):
            y, t_y, xt, t_xt = lq_.pop(0)
            if b + 1 < NB:
                load_E(b + 1)
            for j in range(4):
                o, t_o, d_o = xo.next()
                ps, tps = pm[pi % 4], t_pm[pi % 4]
                pi += 1
                for c in range(NCH):
                    P.op("pe", lambda e, ps=ps, c=c, y=y, j=j: e.matmul(
                        out=ps[:], lhsT=y[:, c, j * 128:(j + 1) * 128], rhs=Wo[:, c, :],
                        start=(c == 0), stop=(c == NCH - 1)), reads=[t_y, t_Wo], writes=[tps])
                P.op("dve", lambda e, ps=ps, o=o, xt=xt, j=j: e.tensor_tensor(
                    out=o[:], in0=ps[:], in1=xt[:, j, :], op=ALU.add), reads=[tps, t_xt], writes=[t_o])
                r0 = b * 512 + j * 128
                P.dma("sp", d_o, x_dst[r0:r0 + 128, :], o[:], reads=[t_o])
        P.flush()


def phase_F(nc, P, S, I, T, out):
    NB = S // 512
    XC = min(S, 1024)
    with ExitStack() as st:
        C = Ctx(nc, P, st)
        fnw = C.sb("fnw", [128, 512], F32)
        t_fnw = Tok()
        d_c = C.dsem("dF_c")
        P.dma("sp", d_c, fnw[:], I["final_norm_w"].partition_broadcast(128), writes=[t_fnw])
        xg = C.slots("xg", 2, [128, 4, 1024], F32)
        xm = C.slots("xm", 2, [128, 4, 512], F32)
        xo = C.slots("xoF", 2, [128, 4, 512], F32)
        junk = C.sb("junkF", [128, 1024], BF16)
        ss = C.sb("ssF", [128, 4], F32)
        t_junk, t_ss = Tok(), Tok()
        lq_ = []

        def load_F(b):
            g, t_g, d_g = xg.next()
            m, t_m, d_m = xm.next()
            tc_, off = (b * 512) // XC, (b * 512) % XC
            for r in range(2):
                P.dma("sp", d_g, g[:, :, r * 512:(r + 1) * 512],
                      T["x2g"][tc_, r * XC + off:r * XC + off + 512, :].rearrange("(j p) d -> p j d", p=128), writes=[t_g])
            P.dma("sp", d_m, m[:], T["xh2"][b * 512:(b + 1) * 512, :].rearrange("(j p) d -> p j d", p=128), writes=[t_m])
            lq_.append((g, t_g, m, t_m))
        load_F(0)
        for b in range(NB):
            g, t_g, m, t_m = lq_.pop(0)
            if b + 1 < NB:
                load_F(b + 1)
            for j in range(4):
                P.op("act", lambda e, g=g, j=j: e.activation(out=junk[:], in_=g[:, j, :], func=AF.Square,
                                                            accum_out=ss[:, j:j + 1]), reads=[t_g], writes=[t_junk, t_ss])
            P.op("dve", lambda e: e.tensor_scalar(out=ss[:], in0=ss[:], scalar1=1.0 / D, scalar2=1e-6,
                                                  op0=ALU.mult, op1=ALU.add), reads=[t_ss], writes=[t_ss])
            P.op("act", lambda e: e.activation(out=ss[:], in_=ss[:], func=AF.Sqrt), reads=[t_ss], writes=[t_ss])
            P.op("dve", lambda e: e.reciprocal(out=ss[:], in_=ss[:]), reads=[t_ss], writes=[t_ss])
            o, t_o, d_o = xo.next()
            for j in range(4):
                P.op("dve", lambda e, o=o, m=m, j=j: e.scalar_tensor_tensor(
                    out=o[:, j, :], in0=m[:, j, :], scalar=ss[:, j:j + 1], in1=fnw[:], op0=ALU.mult, op1=ALU.mult),
                    reads=[t_m, t_ss, t_fnw], writes=[t_o])
            P.dma("sp", d_o, out[b * 512:(b + 1) * 512, :].rearrange("(j p) d -> p j d", p=128), o[:], reads=[t_o])
        P.flush()


def phase_C(nc, P, l, S, I, T):
    NB = S // 512
    with ExitStack() as st:
        C = Ctx(nc, P, st)
        d_c = C.dsem("dC_c")
        cw = C.sb("cw", [128, NH, 4], F32)
        cb = C.sb("cb", [128, NH], F32)
        gb = C.sb("gb", [128, 2, NH], F32)
        lp = C.sb("lp", [128, NH], F32)
        c8 = C.sb("c8", [128, NH], F32)
        c16 = C.sb("c16", [128, NH], F32)
        gwf = C.sb("gwf", [128, 2 * NH, 128], F32)
        gw = C.sb("gw", [128, 2 * NH, 128], BF16)
        t_c = Tok()
        for n in range(NH):
            P.dma("sp", d_c, cw[:, n, :], I["lru_conv_w"][l][:, n * 128:(n + 1) * 128].rearrange("j p -> p j"),
                  writes=[t_c], allow_slow_non_contiguous=True)
        P.dma("sp", d_c, cb[:], I["lru_conv_b"][l].rearrange("(n p) -> p n", p=128), writes=[t_c],
              allow_slow_non_contiguous=True)
        for g in range(2):
            P.dma("sp", d_c, gb[:, g, :], I["lru_gate_b"][l, g].rearrange("(n p) -> p n", p=128), writes=[t_c],
                  allow_slow_non_contiguous=True)
        P.dma("sp", d_c, lp[:], I["lru_log_param"][l].rearrange("(n p) -> p n", p=128), writes=[t_c],
              allow_slow_non_contiguous=True)
        P.dma("sp", d_c, gwf[:], I["lru_gate_w"][l].rearrange("g n d e -> d (g n) e"), writes=[t_c])
        P.op("dve", lambda e: e.tensor_copy(out=gw[:], in_=gwf[:]), reads=[t_c], writes=[t_c])
        P.op("act", lambda e: e.activation(out=c8[:], in_=lp[:], func=AF.Exp, scale=-1.0), reads=[t_c], writes=[t_c])
        P.op("act", lambda e: e.activation(out=c8[:], in_=c8[:], func=AF.Ln, bias=1.0), reads=[t_c], writes=[t_c])
        P.op("dve", lambda e: e.tensor_scalar(out=c16[:], in0=c8[:], scalar1=-16.0, scalar2=None, op0=ALU.mult),
             reads=[t_c], writes=[t_c])
        P.op("dve", lambda e: e.tensor_scalar(out=c8[:], in0=c8[:], scalar1=-8.0, scalar2=None, op0=ALU.mult),
             reads=[t_c], writes=[t_c])
        xins = [C.slots("xin", 2, [128, 515], F32) for _ in range(NH)]
        zins = [C.slots("zin", 2, [128, 512], BF16) for _ in range(NH)]
        yos = [C.slots("yo", 2, [128, 512], BF16) for _ in range(NH)]
        NW = NH
        W = []
        for i in range(NW):
            W.append(dict(
                xc=C.sb("xc", [128, 512], F32), xcb=C.sb("xcb", [128, 512], BF16),
                it=C.sb("it", [128, 512], F32), rt=C.sb("rt", [128, 512], F32),
                at=C.sb("at", [128, 512], F32), a2=C.sb("a2", [128, 512], F32),
                tok=Tok()))
        hs = [[C.sb("h", [128, 512], F32) for _ in range(2)] for n in range(NH)]
        t_h = [[Tok(), Tok()] for n in range(NH)]
        pg = [C.ps("pg", [128, 512]) for _ in range(4)]
        t_pg = [Tok() for _ in range(4)]
        lcs = [[] for _ in range(NH)]

        def load_C(b, n):
            t0 = b * 512
            lc_ = lcs[n]
            x, t_x, d_x = xins[n].next()
            z, t_z, d_z = zins[n].next()
            if b == 0:
                P.op("pool", lambda e, x=x: e.memset(x[:, 0:3], 0.0), writes=[t_x])
                P.dma("sp", d_x, x[:, 3:515], T["rx"][n, :, 0:512], writes=[t_x])
            else:
                P.dma("sp", d_x, x[:], T["rx"][n, :, t0 - 3:t0 + 512], writes=[t_x])
            P.dma("sp", d_z, z[:], T["rz"][n, :, t0:t0 + 512], writes=[t_z])
            lc_.append((x, t_x, z, t_z))
        def stream_C(n):
            load_C(0, n)
            for b in range(NB):
                t0 = b * 512
                x, t_x, z, t_z = lcs[n].pop(0)
                if b + 1 < NB:
                    load_C(b + 1, n)
                it_ = n
                w = W[n]
                tw = w["tok"]
                xc = w["xc"]
                P.op("act", lambda e, x=x, xc=xc, n=n: e.activation(
                    out=xc[:], in_=x[:, 3:515], func=AF.Identity, scale=cw[:, n, 3:4], bias=cb[:, n:n + 1]),
                    reads=[t_x, t_c], writes=[tw])
                for k in range(1, 4):
                    P.op("dve", lambda e, x=x, xc=xc, n=n, k=k: e.scalar_tensor_tensor(
                        out=xc[:], in0=x[:, 3 - k:515 - k], scalar=cw[:, n, 3 - k:4 - k], in1=xc[:],
                        op0=ALU.mult, op1=ALU.add), reads=[t_x, t_c], writes=[tw])
                P.op("pool", lambda e, w=w: e.tensor_copy(out=w["xcb"][:], in_=w["xc"][:]), reads=[tw], writes=[tw])
                p_i, tp_i = pg[(2 * it_) % 4], t_pg[(2 * it_) % 4]
                p_r, tp_r = pg[(2 * it_ + 1) % 4], t_pg[(2 * it_ + 1) % 4]
                P.op("pe", lambda e, w=w, p_i=p_i, n=n: e.matmul(out=p_i[:], lhsT=gw[:, n, :], rhs=w["xcb"][:],
                                                                start=True, stop=True), reads=[tw, t_c], writes=[tp_i])
                P.op("pe", lambda e, w=w, p_r=p_r, n=n: e.matmul(out=p_r[:], lhsT=gw[:, NH + n, :], rhs=w["xcb"][:],
                                                                start=True, stop=True), reads=[tw, t_c], writes=[tp_r])
                P.op("act", lambda e, w=w, p_i=p_i, n=n: e.activation(out=w["it"][:], in_=p_i[:], func=AF.Sigmoid,
                                                                     bias=gb[:, 0, n:n + 1]), reads=[tp_i, t_c], writes=[tw])
                P.op("act", lambda e, w=w, p_r=p_r, n=n: e.activation(out=w["rt"][:], in_=p_r[:], func=AF.Sigmoid,
                                                                     bias=gb[:, 1, n:n + 1]), reads=[tp_r, t_c], writes=[tw])
                P.op("act", lambda e, w=w, n=n: e.activation(out=w["at"][:], in_=w["rt"][:], func=AF.Exp,
                                                            scale=c8[:, n:n + 1]), reads=[tw, t_c], writes=[tw])
                P.op("act", lambda e, w=w, n=n: e.activation(out=w["a2"][:], in_=w["rt"][:], func=AF.Exp,
                                                            scale=c16[:, n:n + 1]), reads=[tw, t_c], writes=[tw])
                P.op("dve", lambda e, w=w: e.tensor_scalar(out=w["a2"][:], in0=w["a2"][:], scalar1=-1.0, scalar2=1.0,
                                                           op0=ALU.mult, op1=ALU.add), reads=[tw], writes=[tw])
                P.op("dve", lambda e, w=w: e.tensor_scalar(out=w["a2"][:], in0=w["a2"][:], scalar1=1e-30, scalar2=None,
                                                           op0=ALU.max), reads=[tw], writes=[tw])
                P.op("act", lambda e, w=w: e.activation(out=w["a2"][:], in_=w["a2"][:], func=AF.Sqrt), reads=[tw], writes=[tw])
                P.op("pool", lambda e, w=w: e.tensor_tensor(out=w["it"][:], in0=w["it"][:], in1=w["xc"][:], op=ALU.mult),
                     reads=[tw], writes=[tw])
                P.op("pool", lambda e, w=w: e.tensor_tensor(out=w["it"][:], in0=w["it"][:], in1=w["a2"][:], op=ALU.mult),
                     reads=[tw], writes=[tw])
                h, th = hs[n][b % 2], t_h[n][b % 2]
                hp, thp = hs[n][(b + 1) % 2], t_h[n][(b + 1) % 2]
                init = 0.0 if b == 0 else hp[:, 511:512]
                P.op("dve", lambda e, w=w, h=h, init=init: e.tensor_tensor_scan(
                    out=h[:], data0=w["at"][:], data1=w["it"][:], initial=init, op0=ALU.mult, op1=ALU.add),
                    reads=[tw] + ([thp] if b > 0 else []), writes=[th])
                y, t_y, d_y = yos[n].next()
                P.op("pool", lambda e, h=h, z=z, y=y: e.tensor_tensor(out=y[:], in0=h[:], in1=z[:], op=ALU.mult),
                     reads=[th, t_z], writes=[t_y])
                P.dma("sp", d_y, T["yT"][2 * NH + n, :, t0:t0 + 512], y[:], reads=[t_y])

        streams = []
        for n in range(NH):
            P.rec = []
            stream_C(n)
            streams.append(P.rec)
        P.rec = None
        P.run_merged(streams)
        P.flush()


def phase_B(nc, P, l, S, I, T):
    NB = S // 512
    NT = S // 128
    LOOK = 2
    with ExitStack() as st:
        C = Ctx(nc, P, st)
        d_c = C.dsem("dB_c")
        t_c = Tok()
        lq = C.sb("lq", [128, 256], F32)
        lj = C.sb("lj", [128, 64], F32)
        lam = C.sb("lam", [128, 4], F32)
        sw = C.sb("sw", [128, 128], F32)
        ident = C.sb("identB", [128, 128], BF16)
        zl = C.sb("zl", [1, 128], BF16)
        zr = C.sb("zr", [1, 512], BF16)
        P.dma("sp", d_c, lq[:], I["attn_lambda"][l:l + 1, :].partition_broadcast(128), writes=[t_c])
        P.dma("sp", d_c, sw[:], I["attn_subln_w"][l:l + 1, :].partition_broadcast(128), writes=[t_c])
        P.dma("sp", d_c, ident[:], I["ident"], writes=[t_c])
        P.op("pool", lambda e: e.memset(zl[:], 0.0), writes=[t_c])
        P.op("pool", lambda e: e.memset(zr[:], 0.0), writes=[t_c])
        for k in range(2):
            P.op("dve", lambda e, k=k: e.scalar_tensor_tensor(
                out=lj[:], in0=lq[:, 128 * k:128 * k + 64], scalar=1.0, in1=lq[:, 128 * k + 64:128 * k + 128],
                op0=ALU.mult, op1=ALU.mult, accum_out=lam[:, k:k + 1]), reads=[t_c], writes=[t_c])
        P.op("act", lambda e: e.activation(out=lam[:, 0:2], in_=lam[:, 0:2], func=AF.Exp), reads=[t_c], writes=[t_c])
        P.op("dve", lambda e: e.tensor_tensor(out=lam[:, 2:3], in0=lam[:, 1:2], in1=lam[:, 0:1], op=ALU.subtract),
             reads=[t_c], writes=[t_c])
        P.op("dve", lambda e: e.tensor_scalar(out=lam[:, 3:4], in0=lam[:, 2:3], scalar1=-LAMBDA_INIT[l], scalar2=None,
                                              op0=ALU.add), reads=[t_c], writes=[t_c])
        P.op("dve", lambda e: e.tensor_scalar(out=sw[:], in0=sw[:], scalar1=1.0 - LAMBDA_INIT[l], scalar2=None,
                                              op0=ALU.mult), reads=[t_c], writes=[t_c])
        neglam = lam[:, 3:4]
        kTs = C.slots("kT", 2, [128, S], BF16)
        Vxs = C.slots("Vx", 2, [128, NT, 130], BF16)
        for i in range(2):
            P.op("pool", lambda e, i=i: e.memset(Vxs.t[i][:, :, 128:130], 1.0), writes=[Vxs.tok[i]])
        qzs = [C.slots("qz0", 2, [128, 512], BF16), C.slots("qz1", 2, [128, 512], BF16)]
        for i in range(2):
            P.op("pool", lambda e, i=i: e.memset(qzs[0].t[i][64:128, :], 0.0), writes=[qzs[0].tok[i]])
            P.op("pool", lambda e, i=i: e.memset(qzs[1].t[i][0:64, :], 0.0), writes=[qzs[1].tok[i]])
        azs = C.slots("azb", 2, [128, 4, 128], BF16)
        pTs = C.slots("pT", 6, [128, 512], BF16)
        yTs = C.slots("yTs", 2, [128, 512], BF16)
        acc = [C.ps("acc", [128, 512]) for _ in range(4)]
        t_acc = [Tok() for _ in range(4)]
        pq = [C.ps("pq", [128, 512]) for _ in range(3)]
        t_pq = [Tok() for _ in range(3)]
        ptr = C.ps("ptrB", [128, 1024], BF16)
        t_ptr = Tok()
        accs = C.sb("accs", [128, 4, 512], F32)
        o = C.sb("oB", [128, 4, 128], F32)
        ybs = [C.sb("ybB", [128, 4, 128], BF16) for _ in range(2)]
        t_ybs = [Tok(), Tok()]
        rs = C.sb("rsB", [128, 8], F32)
        ssq = C.sb("ssB", [128, 4], F32)
        junk = C.sb("junkB", [128, 128], BF16)
        t_accs, t_o, t_rs, t_ss, t_junk = (Tok() for _ in range(5))

        heads = {}

        def load_head(h):
            kT, t_kT, d_kT = kTs.next()
            Vx, t_V, d_V = Vxs.next()
            P.dma("sp", d_kT, kT[:], T["aqk"][NH + h], writes=[t_kT])
            P.dma("sp", d_V, Vx[:, :, 0:128], T["av"][:, h * 128:(h + 1) * 128].rearrange("(t p) d -> p t d", p=128),
                  writes=[t_V])
            heads[h] = (kT, t_kT, Vx, t_V)

        blocks = {}

        def load_block(h, i):
            q0 = i * 512
            qz = []
            for m in range(2):
                t, tok, ds = qzs[m].next()
                P.dma("sp", ds, t[64 * m:64 * m + 64, :], T["aqk"][h, 64 * m:64 * m + 64, q0:q0 + 512], writes=[tok])
                qz.append((t, tok))
            az, t_az, d_az = azs.next()
            P.dma("sp", d_az, az[:], T["az"][q0:q0 + 512, h * 128:(h + 1) * 128].rearrange("(s p) d -> p s d", p=128),
                  writes=[t_az])
            blocks[(h, i)] = (qz, az, t_az)

        steps = []
        for h in range(NH):
            for i in range(NB):
                nj = 4 * i + 4
                for j in range(nj):
                    for m in range(2):
                        steps.append((h, i, j, m, j == 0 and m == 0, j == nj - 1 and m == 1))
        order = [(h, i) for h in range(NH) for i in range(NB)]
        load_head(0)
        load_block(0, 0)
        if len(order) > 1:
            load_block(*order[1])
        pend = {}
        deferred = []
        ep_i = [0]

        def front(k):
            h, i, j, m, first, last = steps[k]
            kT, t_kT, Vx, t_V = heads[h]
            qz, az, t_az = blocks[(h, i)]
            r = j - 4 * i
            c0 = 128 * r if r > 0 else 0
            ps, tps = pq[k % 3], t_pq[k % 3]
            qt, t_q = qz[m]
            P.op("pe", lambda e: e.matmul(out=ps[:, c0:512], lhsT=kT[:, j * 128:(j + 1) * 128], rhs=qt[:, c0:512],
                                          start=True, stop=True), reads=[t_kT, t_q], writes=[tps])
            pt, t_pt, _ = pTs.next()
            P.op("act", lambda e: e.activation(out=pt[:, c0:512], in_=ps[:, c0:512], func=AF.Exp, scale=0.125),
                 reads=[tps], writes=[t_pt])
            if r >= 0:
                P.op("pool", lambda e: e.memset(pt[64:128, 128 * r:128 * r + 64], 0.0), writes=[t_pt])
            pend[k] = (pt, t_pt)

        def epilogue1(h, i):
            qz, az, t_az = blocks[(h, i)]
            yb, t_yb = ybs[ep_i[0] % 2], t_ybs[ep_i[0] % 2]
            ep_i[0] += 1
            for a in range(4):
                if a % 2 == 0:
                    P.op("dve", lambda e, a=a: e.tensor_copy(out=accs[:, a, 0:385], in_=acc[a][:, 0:385]),
                         reads=[t_acc[a]], writes=[t_accs])
                else:
                    P.op("act", lambda e, a=a: e.activation(out=accs[:, a, 0:385], in_=acc[a][:, 0:385], func=AF.Copy),
                         reads=[t_acc[a]], writes=[t_accs])
            for s_ in range(4):
                co = (s_ % 2) * 256
                for m in range(2):
                    P.op("dve", lambda e, s_=s_, m=m, co=co: e.reciprocal(
                        out=rs[:, 4 * m + s_:4 * m + s_ + 1], in_=accs[:, 2 * m + s_ // 2, co + 128:co + 129]),
                        reads=[t_accs], writes=[t_rs])
            P.op("dve", lambda e: e.tensor_scalar(out=rs[:, 4:8], in0=rs[:, 4:8], scalar1=neglam, scalar2=None,
                                                  op0=ALU.mult), reads=[t_rs, t_c], writes=[t_rs])
            for s_ in range(4):
                co = (s_ % 2) * 256
                P.op("dve", lambda e, s_=s_, co=co: e.tensor_scalar(
                    out=o[:, s_, :], in0=accs[:, s_ // 2, co:co + 128], scalar1=rs[:, s_:s_ + 1], scalar2=None,
                    op0=ALU.mult), reads=[t_accs, t_rs], writes=[t_o])
                P.op("dve", lambda e, s_=s_, co=co: e.scalar_tensor_tensor(
                    out=o[:, s_, :], in0=accs[:, 2 + s_ // 2, co:co + 128], scalar=rs[:, 4 + s_:5 + s_], in1=o[:, s_, :],
                    op0=ALU.mult, op1=ALU.add), reads=[t_accs, t_rs], writes=[t_o])
                P.op("dve", lambda e, s_=s_: e.scalar_tensor_tensor(
                    out=junk[:], in0=o[:, s_, :], scalar=1.0, in1=o[:, s_, :], op0=ALU.mult, op1=ALU.mult,
                    accum_out=ssq[:, s_:s_ + 1]), reads=[t_o], writes=[t_junk, t_ss])
            P.op("dve", lambda e: e.tensor_scalar(out=ssq[:], in0=ssq[:], scalar1=1.0 / 128, scalar2=1e-5,
                                                  op0=ALU.mult, op1=ALU.add), reads=[t_ss], writes=[t_ss])
            P.op("act", lambda e: e.activation(out=ssq[:], in_=ssq[:], func=AF.Sqrt), reads=[t_ss], writes=[t_ss])
            P.op("dve", lambda e: e.reciprocal(out=ssq[:], in_=ssq[:]), reads=[t_ss], writes=[t_ss])
            for s_ in range(4):
                P.op("dve", lambda e, s_=s_: e.scalar_tensor_tensor(
                    out=o[:, s_, :], in0=o[:, s_, :], scalar=ssq[:, s_:s_ + 1], in1=sw[:],
                    op0=ALU.mult, op1=ALU.mult), reads=[t_o, t_ss, t_c], writes=[t_o])
                P.op("pool", lambda e, s_=s_: e.tensor_tensor(out=yb[:, s_, :], in0=o[:, s_, :], in1=az[:, s_, :],
                                                             op=ALU.mult), reads=[t_o, t_az], writes=[t_yb])

            def epilogue2():
                for s_ in range(4):
                    P.op("pe", lambda e, s_=s_: e.transpose(out=ptr[:, s_ * 128:(s_ + 1) * 128], in_=yb[:, s_, :],
                                                          identity=ident[:]), reads=[t_yb, t_c], writes=[t_ptr])
                ys, t_ys, d_ys = yTs.next()
                P.op("act", lambda e: e.activation(out=ys[:], in_=ptr[:, 0:512], func=AF.Copy),
                     reads=[t_ptr], writes=[t_ys])
                P.dma("sp", d_ys, T["yT"][h, :, i * 512:(i + 1) * 512], ys[:], reads=[t_ys])
            return epilogue2

        def back(k):
            h, i, j, m, first, last = steps[k]
            kT, t_kT, Vx, t_V = heads[h]
            r = j - 4 * i
            pt, t_pt = pend.pop(k)
            if first:
                for a in range(4):
                    P.op("pe", lambda e, a=a: e.matmul(out=acc[a][:], lhsT=zl[0:1, :], rhs=zr[0:1, :], start=True, stop=True),
                         reads=[t_c], writes=[t_acc[a]])
            for s_ in range(max(r, 0), 4):
                ai = m * 2 + s_ // 2
                co = (s_ % 2) * 256
                P.op("pe", lambda e, ai=ai, co=co, s_=s_: e.matmul(
                    out=acc[ai][:, co:co + 129], lhsT=pt[:, s_ * 128:(s_ + 1) * 128], rhs=Vx[:, j, 0:129],
                    start=False, stop=False, skip_group_check=True), reads=[t_pt, t_V], writes=[t_acc[ai]])
            if last:
                e2 = epilogue1(h, i)
                deferred.append((k + 12, e2))
                oi = order.index((h, i))
                if oi + 2 < len(order):
                    nh, ni = order[oi + 2]
                    load_block(nh, ni)
                if i == 0 and h + 1 < NH:
                    load_head(h + 1)

        n = len(steps)
        for k in range(n + LOOK):
            if k < n:
                front(k)
            if k >= LOOK:
                back(k - LOOK)
            while deferred and deferred[0][0] <= k:
                deferred.pop(0)[1]()
        for _, fn in deferred:
            fn()
        P.flush()


def phase_D(nc, P, l, S, I, T):
    NB = S // 512
    GB = min(S, 2048)
    with ExitStack() as st:
        C = Ctx(nc, P, st)
        d_c = C.dsem("dD0_c")
        t_c = Tok()
        par = C.sb("par", [2 * NH, 4], F32)
        rmask = C.sb("rmask", [2 * NH, GB], F32)
        P.op("pool", lambda e: e.memset(par[:], 0.0), writes=[t_c])
        P.dma("sp", d_c, par[NH:2 * NH, 0:1], I["gdn_a_log"][l].rearrange("(p o) -> p o", o=1), writes=[t_c])
        P.dma("sp", d_c, par[NH:2 * NH, 1:2], I["gdn_dt_bias"][l].rearrange("(p o) -> p o", o=1), writes=[t_c])
        P.dma("sp", d_c, rmask[:], I["rmask"][0:2 * NH, 0:GB], writes=[t_c])
        P.op("act", lambda e: e.activation(out=par[:, 2:3], in_=par[:, 0:1], func=AF.Exp), reads=[t_c], writes=[t_c])
        P.op("dve", lambda e: e.tensor_scalar(out=par[:, 2:3], in0=par[:, 2:3], scalar1=-1.0, scalar2=None, op0=ALU.mult),
             reads=[t_c], writes=[t_c])
        bas = C.slots("ba", 2, [2 * NH, GB], F32)
        sg = C.slots("sg", 2, [2 * NH, GB], F32)
        gg = C.slots("gg", 2, [2 * NH, GB], F32)
        gc = C.slots("gc", 2, [2 * NH, GB], F32)
        for b in range(S // GB):
            c0 = b * GB
            ba, t_ba, d_ba = bas.next()
            sgt, t_sg, d_sg = sg.next()
            g, t_g, _ = gg.next()
            gct, t_gc, d_gc = gc.next()
            P.dma("sp", d_ba, ba[:], T["gba"][:, c0:c0 + GB], writes=[t_ba])
            P.op("act", lambda e, ba=ba, sgt=sgt: e.activation(out=sgt[:], in_=ba[:], func=AF.Sigmoid), reads=[t_ba], writes=[t_sg])
            P.dma("sp", d_sg, T["beta"][:, c0:c0 + GB], sgt[:], reads=[t_sg])
            P.op("act", lambda e, ba=ba, g=g: e.activation(out=g[:], in_=ba[:], func=AF.Exp, bias=par[:, 1:2]),
                 reads=[t_ba, t_c], writes=[t_g])
            P.op("act", lambda e, g=g: e.activation(out=g[:], in_=g[:], func=AF.Ln, bias=1.0), reads=[t_g], writes=[t_g])
            P.op("dve", lambda e, g=g: e.tensor_scalar(out=g[:], in0=g[:], scalar1=par[:, 2:3], scalar2=None, op0=ALU.mult),
                 reads=[t_g, t_c], writes=[t_g])
            P.op("dve", lambda e, g=g, gct=gct: e.tensor_tensor_scan(out=gct[:], data0=rmask[:], data1=g[:], initial=0.0,
                                                                    op0=ALU.mult, op1=ALU.add), reads=[t_g, t_c], writes=[t_gc])
            P.dma("sp", d_gc, T["gcs"][:, c0:c0 + GB], gct[:], reads=[t_gc])
        P.flush()
    with ExitStack() as st:
        C = Ctx(nc, P, st)
        d_c = C.dsem("dD_c")
        t_c = Tok()
        gcw = C.sb("gcw", [128, 3 * NH, 4], F32)
        gnw = C.sb("gnw", [128, 128], F32)
        onesf = C.sb("onesf", [128, 128], F32)
        identf = C.sb("identf", [128, 128], F32)
        ident = C.sb("identD", [128, 128], BF16)
        mI = C.sb("mI", [128, 512], F32)
        mS = C.sb("mS", [128, 512], F32)
        for c in range(3 * NH):
            P.dma("sp", d_c, gcw[:, c, :], I["gdn_conv_w"][l][:, c * 128:(c + 1) * 128].rearrange("j p -> p j"),
                  writes=[t_c], allow_slow_non_contiguous=True)
        P.dma("sp", d_c, gnw[:], I["gdn_norm_w"][l:l + 1, :].partition_broadcast(128), writes=[t_c])
        P.dma("sp", d_c, onesf[:], I["onesf"], writes=[t_c])
        P.dma("sp", d_c, identf[:], I["identf"], writes=[t_c])
        P.dma("sp", d_c, ident[:], I["ident"], writes=[t_c])
        P.dma("sp", d_c, mI[:], I["masks"][0], writes=[t_c])
        P.dma("sp", d_c, mS[:], I["masks"][1], writes=[t_c])
        p1 = C.ps("p1", [128, 512]); pS = C.ps("pS", [128, 512]); po = C.ps("po", [128, 512])
        ptr = C.ps("ptrD", [128, 1024], BF16)
        t_p1 = [Tok() for _ in range(4)]; t_pS = [Tok() for _ in range(4)]; t_po = [Tok() for _ in range(4)]
        t_ptr = [Tok() for _ in range(4)]
        pp = [C.ps("pp", [128, 512]) for _ in range(4)]
        t_pp = [Tok() for _ in range(4)]
        R, H, WkH, tWH = [], [], [], []
        for h in range(NH):
            R.append(dict(
                S=C.sb("Sst", [128, 128], F32), tS=Tok(), vn=C.sb("vn", [128, 4, 128], BF16), tvn=Tok(),
                y=C.sb("yD", [128, 128], F32), yb=C.sb("ybD", [128, 128], BF16), ss=C.sb("ssD", [128, 1], F32),
                ty=Tok(), junk=C.sb("junkD", [128, 128], BF16)))
            P.op("pool", lambda e, h=h: e.memset(R[h]["S"][:], 0.0), writes=[R[h]["tS"]])
            H.append([dict(
                UW=C.sb("UW", [128, 4, 256], F32), wT=C.sb("wT", [128, 512], F32), qg=C.sb("qg", [128, 512], F32),
                AT=C.sb("AT", [128, 4, 128], BF16), kd=C.sb("kd", [128, 4, 128], BF16),
                egl=C.sb("egl", [128, 8], F32), tD=Tok()) for _ in range(2)])
            WkH.append(dict(
                xs=[C.sb("xsD", [128, 512], F32) for _ in range(3)], sq=C.sb("sqD", [128, 512], F32),
                rn=C.sb("rnD", [128, 512], F32), kTb=C.sb("kTb", [128, 512], BF16), qTb=C.sb("qTb", [128, 512], BF16),
                egb=C.sb("egb", [128, 512], F32), dec=C.sb("dec", [128, 512], F32), dI=C.sb("dI", [128, 512], F32),
                dS=C.sb("dSm", [128, 512], F32), A=[C.sb("Am", [128, 512], F32) for _ in range(2)],
                B=[C.sb("Bm", [128, 512], F32) for _ in range(2)], X=C.sb("Xm", [128, 4, 256], F32),
                sd=C.sb("sd", [128, 16], F32)))
            tWH.append({k: Tok() for k in ("xs0", "xs1", "xs2", "sq", "rn", "kTb", "qTb", "egb", "dec", "dI", "dS",
                                           "A0", "A1", "B0", "B1", "X", "sd")})
        gzs = C.slots("gzb", 4, [128, 4, 128], BF16)
        yTs = C.slots("yTsD", 4, [128, 512], BF16)
        xin = C.slots("xinD", 6, [128, 515], F32)
        scs = C.slots("sc", 4, [128, 24], F32)
        gbrs = C.slots("gbr", 4, [128, 512], F32)
        ppi = [0, 0]

        def prep(b, h):
            t0 = b * 512
            Hh = H[h][b % 2]
            Wk = WkH[h]
            tW = tWH[h]

            def bank():
                i = 2 * h + ppi[h] % 2
                ppi[h] += 1
                return pp[i], t_pp[i]
            sc, t_sc, d_sc = scs.next()
            gbr, t_gbr, d_gbr = gbrs.next()
            grow = T["gcs"][NH + h:NH + h + 1, :]
            brow = T["beta"][h:h + 1, :]
            P.dma("sp", d_gbr, gbr[:], grow[:, t0:t0 + 512].partition_broadcast(128), writes=[t_gbr])
            P.dma("sp", d_sc, sc[:, 0:4], T["gcs"][NH + h, t0:t0 + 512].rearrange("(t p) -> p t", p=128), writes=[t_sc],
                  allow_slow_non_contiguous=True)
            P.dma("sp", d_sc, sc[:, 4:8], T["beta"][h, t0:t0 + 512].rearrange("(t p) -> p t", p=128), writes=[t_sc],
                  allow_slow_non_contiguous=True)
            sd = Wk["sd"]
            P.op("act", lambda e, sc=sc: e.activation(out=sd[:, 0:4], in_=sc[:, 0:4], func=AF.Exp),
                 reads=[t_sc], writes=[tW["sd"]])
            P.op("dve", lambda e, sc=sc: e.tensor_scalar(out=sd[:, 4:8], in0=sc[:, 4:8], scalar1=-1.0, scalar2=None,
                                                        op0=ALU.mult), reads=[t_sc], writes=[tW["sd"]])
            for hf in range(2):
                P.op("dve", lambda e, sc=sc, gbr=gbr, hf=hf: e.tensor_tensor(
                    out=sd[hf * 64:(hf + 1) * 64, 8:12], in0=gbr[hf * 64:(hf + 1) * 64, hf * 64 + 63:512:128],
                    in1=sc[hf * 64:(hf + 1) * 64, 0:4], op=ALU.subtract), reads=[t_sc, t_gbr], writes=[tW["sd"]])
            P.op("act", lambda e: e.activation(out=sd[:, 8:12], in_=sd[:, 8:12], func=AF.Exp),
                 reads=[tW["sd"]], writes=[tW["sd"]])
            P.op("act", lambda e, gbr=gbr, Hh=Hh: e.activation(out=Hh["egl"][:], in_=gbr[:, 63:512:64], func=AF.Exp),
                 reads=[t_gbr], writes=[Hh["tD"]])
            for w in range(3):
                x, t_x, d_x = xin.next()
                ch = w * NH + h
                if b == 0:
                    P.op("pool", lambda e, x=x: e.memset(x[:, 0:3], 0.0), writes=[t_x])
                    P.dma("sp", d_x, x[:, 3:515], T["gqkv"][ch, :, 0:512], writes=[t_x])
                else:
                    P.dma("sp", d_x, x[:], T["gqkv"][ch, :, t0 - 3:t0 + 512], writes=[t_x])
                xs = Wk["xs"][w]
                tx = tW[f"xs{w}"]
                P.op("act", lambda e, x=x, xs=xs, ch=ch: e.activation(out=xs[:], in_=x[:, 3:515], func=AF.Copy,
                                                                     scale=gcw[:, ch, 3:4]), reads=[t_x, t_c], writes=[tx])
                for k in range(1, 4):
                    P.op("dve", lambda e, x=x, xs=xs, ch=ch, k=k: e.scalar_tensor_tensor(
                        out=xs[:], in0=x[:, 3 - k:515 - k], scalar=gcw[:, ch, 3 - k:4 - k], in1=xs[:],
                        op0=ALU.mult, op1=ALU.add), reads=[t_x, t_c], writes=[tx])
                P.op("act", lambda e, xs=xs: e.activation(out=xs[:], in_=xs[:], func=AF.Silu), reads=[tx], writes=[tx])
            xq, xk, xv = Wk["xs"]
            for w, xx in ((0, xq), (1, xk)):
                tx = tW[f"xs{w}"]
                P.op("act", lambda e, xx=xx: e.activation(out=Wk["sq"][:], in_=xx[:], func=AF.Square),
                     reads=[tx], writes=[tW["sq"]])
                pb, tpb = bank()
                P.op("pe", lambda e, pb=pb: e.matmul(out=pb[:], lhsT=onesf[:], rhs=Wk["sq"][:], start=True, stop=True),
                     reads=[tW["sq"], t_c], writes=[tpb])
                P.op("dve", lambda e, pb=pb: e.tensor_scalar(out=Wk["rn"][:], in0=pb[:], scalar1=1e-6, scalar2=None,
                                                            op0=ALU.add), reads=[tpb], writes=[tW["rn"]])
                P.op("act", lambda e: e.activation(out=Wk["rn"][:], in_=Wk["rn"][:], func=AF.Sqrt),
                     reads=[tW["rn"]], writes=[tW["rn"]])
                P.op("dve", lambda e: e.reciprocal(out=Wk["rn"][:], in_=Wk["rn"][:]), reads=[tW["rn"]], writes=[tW["rn"]])
                if w == 0:
                    P.op("dve", lambda e, xx=xx: e.scalar_tensor_tensor(
                        out=xx[:], in0=xx[:], scalar=128.0 ** -0.5, in1=Wk["rn"][:], op0=ALU.mult, op1=ALU.mult),
                        reads=[tx, tW["rn"]], writes=[tx])
                else:
                    P.op("dve", lambda e, xx=xx: e.tensor_tensor(out=xx[:], in0=xx[:], in1=Wk["rn"][:], op=ALU.mult),
                         reads=[tx, tW["rn"]], writes=[tx])
            P.op("pool", lambda e: e.tensor_copy(out=Wk["qTb"][:], in_=xq[:]), reads=[tW["xs0"]], writes=[tW["qTb"]])
            P.op("pool", lambda e: e.tensor_copy(out=Wk["kTb"][:], in_=xk[:]), reads=[tW["xs1"]], writes=[tW["kTb"]])
            P.op("act", lambda e, gbr=gbr: e.activation(out=Wk["egb"][:], in_=gbr[:], func=AF.Exp),
                 reads=[t_gbr], writes=[tW["egb"]])
            P.op("pool", lambda e, Hh=Hh: e.tensor_tensor(out=Hh["qg"][:], in0=xq[:], in1=Wk["egb"][:], op=ALU.mult),
                 reads=[tW["xs0"], tW["egb"]], writes=[Hh["tD"]])
            X = Wk["X"]
            pv, tpv = bank()
            for t in range(4):
                P.op("pe", lambda e, pv=pv, t=t: e.transpose(out=pv[:, t * 128:(t + 1) * 128],
                                                            in_=xv[:, t * 128:(t + 1) * 128], identity=identf[:]),
                     reads=[tW["xs2"], t_c], writes=[tpv])
            P.op("act", lambda e, pv=pv: e.activation(out=X[:, :, 0:128], in_=pv[:].rearrange("p (t d) -> p t d", t=4),
                                                     func=AF.Copy), reads=[tpv], writes=[tW["X"]])
            pk, tpk = bank()
            for t in range(4):
                P.op("pe", lambda e, pk=pk, t=t: e.transpose(out=pk[:, t * 128:(t + 1) * 128],
                                                            in_=xk[:, t * 128:(t + 1) * 128], identity=identf[:]),
                     reads=[tW["xs1"], t_c], writes=[tpk])
            for t in range(4):
                P.op("act", lambda e, pk=pk, t=t: e.activation(out=X[:, t, 128:256], in_=pk[:, t * 128:(t + 1) * 128],
                                                              func=AF.Copy, scale=sd[:, t:t + 1]),
                     reads=[tpk, tW["sd"]], writes=[tW["X"]])
                P.op("act", lambda e, pk=pk, t=t, Hh=Hh: e.activation(
                    out=Hh["kd"][:, t, :], in_=pk[:, t * 128:(t + 1) * 128], func=AF.Copy, scale=sd[:, 8 + t:9 + t]),
                    reads=[tpk, tW["sd"]], writes=[Hh["tD"]])
            for t in range(4):
                P.op("dve", lambda e, t=t, gbr=gbr, sc=sc: e.tensor_scalar(
                    out=Wk["dec"][:, t * 128:(t + 1) * 128], in0=gbr[:, t * 128:(t + 1) * 128], scalar1=sc[:, t:t + 1],
                    scalar2=0.0, op0=ALU.subtract, op1=ALU.min), reads=[t_gbr, t_sc], writes=[tW["dec"]])
            P.op("act", lambda e: e.activation(out=Wk["dec"][:], in_=Wk["dec"][:], func=AF.Exp),
                 reads=[tW["dec"]], writes=[tW["dec"]])
            P.op("pool", lambda e: e.tensor_tensor(out=Wk["dI"][:], in0=Wk["dec"][:], in1=mI[:], op=ALU.mult),
                 reads=[tW["dec"], t_c], writes=[tW["dI"]])
            P.op("pool", lambda e: e.tensor_tensor(out=Wk["dS"][:], in0=Wk["dec"][:], in1=mS[:], op=ALU.mult),
                 reads=[tW["dec"], t_c], writes=[tW["dS"]])
            pkk, tpkk = bank()
            pqk, tpqk = bank()
            for t in range(4):
                sl = slice(t * 128, (t + 1) * 128)
                P.op("pe", lambda e, sl=sl, pkk=pkk: e.matmul(out=pkk[:, sl], lhsT=Wk["kTb"][:, sl], rhs=Wk["kTb"][:, sl],
                                                             start=True, stop=True), reads=[tW["kTb"]], writes=[tpkk])
                P.op("pe", lambda e, sl=sl, pqk=pqk: e.matmul(out=pqk[:, sl], lhsT=Wk["kTb"][:, sl], rhs=Wk["qTb"][:, sl],
                                                             start=True, stop=True), reads=[tW["kTb"], tW["qTb"]], writes=[tpqk])
            P.op("dve", lambda e, pqk=pqk, Hh=Hh: e.tensor_tensor(
                out=Hh["AT"][:].rearrange("p t d -> p (t d)"), in0=pqk[:], in1=Wk["dI"][:], op=ALU.mult),
                reads=[tpqk, tW["dI"]], writes=[Hh["tD"]])
            A, B = Wk["A"], Wk["B"]
            for t in range(4):
                sl = slice(t * 128, (t + 1) * 128)
                P.op("dve", lambda e, sl=sl, t=t, pkk=pkk: e.scalar_tensor_tensor(
                    out=A[0][:, sl], in0=pkk[:, sl], scalar=sd[:, 4 + t:5 + t], in1=Wk["dS"][:, sl],
                    op0=ALU.mult, op1=ALU.mult), reads=[tpkk, tW["sd"], tW["dS"]], writes=[tW["A0"]])
            pbt, tpbt = bank()
            for t in range(4):
                sl = slice(t * 128, (t + 1) * 128)
                P.op("pe", lambda e, sl=sl, pbt=pbt: e.transpose(out=pbt[:, sl], in_=A[0][:, sl], identity=identf[:]),
                     reads=[tW["A0"], t_c], writes=[tpbt])
            P.op("act", lambda e, pbt=pbt: e.activation(out=B[0][:], in_=pbt[:], func=AF.Copy),
                 reads=[tpbt], writes=[tW["B0"]])

            def xupd(cur):
                px0, tpx0 = bank()
                px1, tpx1 = bank()
                for t in range(4):
                    px, tpx = (px0, tpx0) if t < 2 else (px1, tpx1)
                    o_ = (t % 2) * 256
                    P.op("pe", lambda e, px=px, o_=o_, t=t: e.matmul(
                        out=px[:, o_:o_ + 256], lhsT=A[cur][:, t * 128:(t + 1) * 128], rhs=X[:, t, :],
                        start=True, stop=True), reads=[tW[f"A{cur}"], tW["X"]], writes=[tpx])
                for hf, (px, tpx) in enumerate(((px0, tpx0), (px1, tpx1))):
                    P.op("dve", lambda e, px=px, hf=hf: e.tensor_tensor(
                        out=X[:, 2 * hf:2 * hf + 2, :].rearrange("p t d -> p (t d)"),
                        in0=X[:, 2 * hf:2 * hf + 2, :].rearrange("p t d -> p (t d)"), in1=px[:], op=ALU.add),
                        reads=[tpx, tW["X"]], writes=[tW["X"]])

            cur = 0
            xupd(cur)
            for lev in range(5):
                nxt = 1 - cur
                pa, tpa = bank()
                for t in range(4):
                    sl = slice(t * 128, (t + 1) * 128)
                    P.op("pe", lambda e, sl=sl, pa=pa, cur=cur: e.matmul(out=pa[:, sl], lhsT=B[cur][:, sl], rhs=A[cur][:, sl],
                                                                        start=True, stop=True),
                         reads=[tW[f"A{cur}"], tW[f"B{cur}"]], writes=[tpa])
                if lev < 4:
                    pb2, tpb2 = bank()
                    for t in range(4):
                        sl = slice(t * 128, (t + 1) * 128)
                        P.op("pe", lambda e, sl=sl, pb2=pb2, cur=cur: e.matmul(
                            out=pb2[:, sl], lhsT=A[cur][:, sl], rhs=B[cur][:, sl], start=True, stop=True),
                            reads=[tW[f"A{cur}"], tW[f"B{cur}"]], writes=[tpb2])
                P.op("act", lambda e, pa=pa, nxt=nxt: e.activation(out=A[nxt][:], in_=pa[:], func=AF.Copy),
                     reads=[tpa], writes=[tW[f"A{nxt}"]])
                if lev < 4:
                    P.op("dve", lambda e, pb2=pb2, nxt=nxt: e.tensor_copy(out=B[nxt][:], in_=pb2[:]),
                         reads=[tpb2], writes=[tW[f"B{nxt}"]])
                cur = nxt
                xupd(cur)
            for t in range(4):
                P.op("act", lambda e, t=t, Hh=Hh, sc=sc: e.activation(out=Hh["UW"][:, t, :], in_=X[:, t, :], func=AF.Copy,
                                                                    scale=sc[:, 4 + t:5 + t]),
                     reads=[tW["X"], t_sc], writes=[Hh["tD"]])
            pw, tpw = bank()
            for t in range(4):
                P.op("pe", lambda e, t=t, pw=pw, Hh=Hh: e.transpose(out=pw[:, t * 128:(t + 1) * 128],
                                                                  in_=Hh["UW"][:, t, 128:256], identity=identf[:]),
                     reads=[Hh["tD"], t_c], writes=[tpw])
            P.op("dve", lambda e, pw=pw, Hh=Hh: e.tensor_copy(out=Hh["wT"][:], in_=pw[:]), reads=[tpw], writes=[Hh["tD"]])

        def rec(b):
            t0 = b * 512
            gzl = []
            for h in range(NH):
                gz, t_gz, d_gz = gzs.next()
                P.dma("sp", d_gz, gz[:], T["gz"][t0:t0 + 512, h * 128:(h + 1) * 128].rearrange("(s p) d -> p s d", p=128),
                      writes=[t_gz])
                ys_, t_ys, d_ys = yTs.next()
                gzl.append((gz, t_gz, ys_, t_ys, d_ys))
            for c in range(8):
                t = c // 2
                r0 = (c % 2) * 64
                rs_ = slice(r0, r0 + 64)
                for h in range(NH):
                    Hh = dict(H[h][b % 2])
                    Hh.update(R[h])
                    hs = slice(h * 128, (h + 1) * 128)
                    cs = slice(t * 128 + r0, t * 128 + r0 + 64)
                    P.op("pe", lambda e, Hh=Hh, hs=hs, cs=cs, rs_=rs_: e.matmul(
                        out=p1[rs_, hs], lhsT=Hh["wT"][:, cs], rhs=Hh["S"][:], start=True, stop=True),
                        reads=[Hh["tD"], Hh["tS"]], writes=[t_p1[h]])
                    P.op("dve", lambda e, Hh=Hh, hs=hs, rs_=rs_, t=t: e.tensor_tensor(
                        out=Hh["vn"][rs_, t, :], in0=Hh["UW"][rs_, t, 0:128], in1=p1[rs_, hs], op=ALU.subtract),
                        reads=[Hh["tD"], t_p1[h]], writes=[Hh["tvn"]])
                    P.op("pe", lambda e, Hh=Hh, hs=hs, cs=cs, rs_=rs_: e.matmul(
                        out=po[rs_, hs], lhsT=Hh["qg"][:, cs], rhs=Hh["S"][:], start=True, stop=False),
                        reads=[Hh["tD"], Hh["tS"]], writes=[t_po[h]])
                    P.op("pe", lambda e, Hh=Hh, hs=hs, rs_=rs_, t=t, r0=r0: e.matmul(
                        out=po[rs_, hs], lhsT=Hh["AT"][rs_, t, r0:r0 + 64], rhs=Hh["vn"][rs_, t, :], start=False, stop=True),
                        reads=[Hh["tD"], Hh["tvn"]], writes=[t_po[h]])
                    P.op("pe", lambda e, Hh=Hh, hs=hs, rs_=rs_, t=t: e.matmul(
                        out=pS[:, hs], lhsT=Hh["kd"][rs_, t, :], rhs=Hh["vn"][rs_, t, :], start=True, stop=True),
                        reads=[Hh["tD"], Hh["tvn"]], writes=[t_pS[h]])
                    P.op("dve", lambda e, Hh=Hh, hs=hs, c=c: e.scalar_tensor_tensor(
                        out=Hh["S"][:], in0=Hh["S"][:], scalar=Hh["egl"][:, c:c + 1], in1=pS[:, hs],
                        op0=ALU.mult, op1=ALU.add), reads=[Hh["tD"], t_pS[h], Hh["tS"]], writes=[Hh["tS"]])
                    if c % 2 == 1:
                        gz, t_gz, ys_, t_ys, d_ys = gzl[h]
                        P.op("dve", lambda e, Hh=Hh, hs=hs: e.tensor_copy(out=Hh["y"][:], in_=po[:, hs]),
                             reads=[t_po[h]], writes=[Hh["ty"]])
                        P.op("act", lambda e, Hh=Hh, hs=hs: e.activation(out=Hh["junk"][:], in_=Hh["y"][:], func=AF.Square,
                                                                        accum_out=Hh["ss"][:]),
                             reads=[Hh["ty"]], writes=[Hh["ty"]])
                        P.op("dve", lambda e, Hh=Hh: e.tensor_scalar(out=Hh["ss"][:], in0=Hh["ss"][:], scalar1=1.0 / 128,
                                                                     scalar2=1e-6, op0=ALU.mult, op1=ALU.add),
                             reads=[Hh["ty"]], writes=[Hh["ty"]])
                        P.op("act", lambda e, Hh=Hh: e.activation(out=Hh["ss"][:], in_=Hh["ss"][:], func=AF.Sqrt),
                             reads=[Hh["ty"]], writes=[Hh["ty"]])
                        P.op("dve", lambda e, Hh=Hh: e.reciprocal(out=Hh["ss"][:], in_=Hh["ss"][:]),
                             reads=[Hh["ty"]], writes=[Hh["ty"]])
                        P.op("dve", lambda e, Hh=Hh, hs=hs: e.scalar_tensor_tensor(
                            out=Hh["y"][:], in0=Hh["y"][:], scalar=Hh["ss"][:], in1=gnw[:], op0=ALU.mult, op1=ALU.mult),
                            reads=[Hh["ty"], t_c], writes=[Hh["ty"]])
                        P.op("pool", lambda e, Hh=Hh, gz=gz, t=t: e.tensor_tensor(out=Hh["yb"][:], in0=Hh["y"][:], in1=gz[:, t, :],
                                                                                 op=ALU.mult),
                             reads=[Hh["ty"], t_gz], writes=[Hh["ty"]])
                        P.op("pe", lambda e, Hh=Hh, hs=hs: e.transpose(out=ptr[:, hs], in_=Hh["yb"][:], identity=ident[:]),
                             reads=[Hh["ty"], t_c], writes=[t_ptr[h]])
                        P.op("act", lambda e, hs=hs, ys_=ys_, t=t: e.activation(out=ys_[:, t * 128:(t + 1) * 128], in_=ptr[:, hs],
                                                                              func=AF.Copy), reads=[t_ptr[h]], writes=[t_ys])
                        if t == 3:
                            P.dma("sp", d_ys, T["yT"][NH + h, :, t0:t0 + 512], ys_[:], reads=[t_ys])

        for b in range(NB + 1):
            streams = []
            if b < NB:
                for h in range(NH):
                    P.rec = []
                    prep(b, h)
                    streams.append(P.rec)
            if b > 0:
                P.rec = []
                rec(b - 1)
                streams.append(P.rec)
            P.rec = None
            P.run_merged(streams)
        P.flush()


def declare_inputs(nc, S):
    I = {}

    def inp(name, shape, dt=F32):
        I[name] = nc.dram_tensor(name, shape, dt, kind="ExternalInput").ap()
    inp("x", [S, D])
    inp("norm_w", [2, D])
    inp("w_in", [2, D, NCOL])
    inp("attn_lambda", [2, 256])
    inp("attn_subln_w", [2, 128])
    inp("gdn_conv_w", [2, 4, 3 * HB])
    inp("gdn_a_log", [2, NH])
    inp("gdn_dt_bias", [2, NH])
    inp("gdn_norm_w", [2, 128])
    inp("lru_conv_w", [2, 4, HB])
    inp("lru_conv_b", [2, HB])
    inp("lru_gate_w", [2, 2, NH, 128, 128])
    inp("lru_gate_b", [2, 2, HB])
    inp("lru_log_param", [2, HB])
    inp("w_out", [2, 1536, 512])
    inp("final_norm_w", [1, 512])
    inp("xh", [S, 512])
    inp("ident", [128, 128], BF16)
    inp("identf", [128, 128], F32)
    inp("masks", [2, 128, 512], F32)
    inp("onesf", [128, 128], F32)
    inp("rmask", [8, 2048], F32)
    return I


def const_inputs():
    ident = np.eye(128, dtype=np.float32)
    i = np.arange(128)
    same = (i[:, None] // 64) == (i[None, :] // 64)
    m_incl = (same & (i[None, :] >= i[:, None])).astype(np.float32)
    m_strict = (same & (i[None, :] > i[:, None])).astype(np.float32)
    masks = np.stack([np.tile(m_incl, (1, 4)), np.tile(m_strict, (1, 4))])
    rmask = np.tile((np.arange(2048) % 64 != 0).astype(np.float32)[None, :], (8, 1))
    return {"ident": ident.astype(ml_dtypes.bfloat16), "identf": ident, "masks": masks,
            "onesf": np.ones((128, 128), np.float32), "rmask": rmask}


def pack_inputs(inp, b, hh, S=None):
    x = inp["x"][b] if S is None else inp["x"][b][:S]
    h2 = slice(hh * HB, (hh + 1) * HB)
    nh = slice(hh * NH, (hh + 1) * NH)
    w = inp["w_in"]
    cols = []
    for br in range(8):
        cols.append(np.arange(br * 512 + hh * HB, br * 512 + (hh + 1) * HB))
    cols.append(np.arange(4096 + hh * NH, 4096 + (hh + 1) * NH))
    cols.append(np.arange(4100 + hh * NH, 4100 + (hh + 1) * NH))
    cols.append(np.arange(4104 + hh * HB, 4104 + (hh + 1) * HB))
    cols.append(np.arange(4616 + hh * HB, 4616 + (hh + 1) * HB))
    cols = np.concatenate(cols)
    gcw = inp["gdn_conv_w"]
    gcw = np.concatenate([gcw[:, :, br * 512 + hh * HB:br * 512 + (hh + 1) * HB] for br in range(3)], axis=-1)
    rows = []
    for r in range(2):
        for br in range(3):
            rows.append(np.arange(br * 512 + r * HB, br * 512 + (r + 1) * HB))
    rows = np.concatenate(rows)
    dh = slice(hh * 512, (hh + 1) * 512)
    m = {"x": x, "xh": x[:, dh], "norm_w": inp["norm_w"], "w_in": w[:, :, cols],
         "attn_lambda": inp["attn_lambda"].reshape(2, 256), "attn_subln_w": inp["attn_subln_w"],
         "gdn_conv_w": gcw, "gdn_a_log": inp["gdn_a_log"][:, nh], "gdn_dt_bias": inp["gdn_dt_bias"][:, nh],
         "gdn_norm_w": inp["gdn_norm_w"], "lru_conv_w": inp["lru_conv_w"][:, :, h2], "lru_conv_b": inp["lru_conv_b"][:, h2],
         "lru_gate_w": inp["lru_gate_w"][:, :, nh], "lru_gate_b": inp["lru_gate_b"][:, :, h2],
         "lru_log_param": inp["lru_log_param"][:, h2], "w_out": inp["w_out"][:, rows][:, :, dh],
         "final_norm_w": inp["final_norm_w"][dh].reshape(1, 512)}
    m.update(const_inputs())
    return {k: np.ascontiguousarray(v) for k, v in m.items()}


def build(S, phases="ABCDE", dbg=False, nlayers=DEPTH):
    nc = bass.Bass("TRN2", target_bir_lowering=False)
    I = declare_inputs(nc, S)
    out = nc.dram_tensor("out", [S, 512], F32, kind="ExternalOutput").ap()
    T = make_dram(nc, S, dbg)
    XC = min(S, 1024)
    with ExitStack() as stack:
        P = Prog(nc, stack)
        for l in range(nlayers):
            if "A" in phases:
                phase_A(nc, P, l, S, I["x"] if l == 0 else None, I, T)
            if "B" in phases:
                phase_B(nc, P, l, S, I, T)
            if "C" in phases:
                phase_C(nc, P, l, S, I, T)
            if "D" in phases:
                phase_D(nc, P, l, S, I, T)
            if "E" in phases:
                allgather(nc, P, [(T["yT"][c], T["yTg"][c]) for c in range(3 * NH)])
                xh_src = I["xh"] if l == 0 else T["xh1"]
                xh_dst = T["xh1"] if l == 0 else T["xh2"]
                phase_E(nc, P, l, S, xh_src, xh_dst, I, T)
                xg = T["x1g"] if l == 0 else T["x2g"]
                allgather(nc, P, [(xh_dst[tc * XC:(tc + 1) * XC, :], xg[tc]) for tc in range(S // XC)])
                if l == DEPTH - 1:
                    phase_F(nc, P, S, I, T, out)
    return nc


def kernel(**inputs):
    inp = {k: np.asarray(v) for k, v in inputs.items()}
    B, S, _ = inp["x"].shape
    nc = build(S)
    in_maps = [pack_inputs(inp, c // 2, c % 2) for c in range(2 * B)]
    res = run_bass_kernel_spmd(nc, in_maps, core_ids=list(range(2 * B)))
    out = np.empty((B, S, D), np.float32)
    for c in range(2 * B):
        out[c // 2, :, (c % 2) * 512:(c % 2 + 1) * 512] = np.asarray(res.results[c]["out"], dtype=np.float32)
    return out
```

```python
from contextlib import ExitStack
import numpy as np
import ml_dtypes
import concourse.bass as bass
import concourse.mybir as mybir
from concourse.bass_utils import run_bass_kernel_spmd

F32 = mybir.dt.float32
BF16 = mybir.dt.bfloat16
AF = mybir.ActivationFunctionType
ALU = mybir.AluOpType
AX = mybir.AxisListType

ENGS = ("pe", "act", "dve", "pool", "sp")
D = 1024
NH = 2
HB = NH * 128
NCOL = 10 * HB + 2 * NH
DEPTH = 2
LAMBDA_INIT = [0.8 - 0.6 * float(np.exp(-0.3 * l)) for l in range(DEPTH)]


_uid = [0]


def uname(name):
    _uid[0] += 1
    return f"{name}_u{_uid[0]}"


class Tok:
    __slots__ = ("name", "w", "r")

    def __init__(self, name=""):
        self.name = name
        self.w = None
        self.r = {}


class DSem:
    def __init__(self, prog, name, q="sp"):
        self.q = q
        if prog.free_d[q]:
            self.key, self.cnt = prog.free_d[q].pop()
        else:
            self.key = ("d", name)
            prog.sems[self.key] = prog._alloc_sem(name)
            self.cnt = 0
        prog.dsems.append(self)


class Prog:
    def __init__(self, nc, stack):
        self.nc = nc
        self.stack = stack
        self.sems = {}
        self.dsems = []
        self.free_d = {"sp": [], "pool": []}
        self.dcur = {}
        self.phase = 0
        self.esets = {}
        self.rec = None
        self.q = {e: [] for e in ENGS}
        self._new_engine_sems()

    def _alloc_sem(self, name):
        return self.stack.enter_context(self.nc.semaphore(name))

    NSETS = 4

    def _new_engine_sems(self):
        si = self.phase % self.NSETS
        if si not in self.esets:
            self.esets[si] = {}
            for e in ENGS:
                if e == "sp":
                    continue
                k = ("e", e, si)
                self.sems[k] = self._alloc_sem(f"s_{e}_{si}")
                self.esets[si][e] = [k, 0]
        self.eset = self.esets[si]
        self.ekey = {e: v[0] for e, v in self.eset.items()}
        self.cnt = {e: v[1] for e, v in self.eset.items()}
        self.cnt0 = dict(self.cnt)
        self.known = {e: {} for e in ENGS}

    def _wait(self, eng, ev):
        if ev is None:
            return
        k, v = ev
        if eng == "pe" and k == self.ekey.get("pe"):
            return
        if k[0] == "d":
            v = max(v, self.dcur.get(k, 0))
        if self.known[eng].get(k, 0) >= v:
            return
        self.known[eng][k] = v
        self.q[eng].append(("w", k, v))

    def _deps(self, eng, reads, writes):
        for t in reads:
            self._wait(eng, t.w)
        for t in writes:
            self._wait(eng, t.w)
            for k, v in t.r.items():
                self._wait(eng, (k, v))

    def _record(self, ev, reads, writes):
        k, v = ev
        for t in reads:
            if t.r.get(k, 0) < v:
                t.r[k] = v
        for t in writes:
            t.w = ev
            t.r = {}

    def run_merged(self, streams):
        streams = [list(x) for x in streams if x]
        idx = [0] * len(streams)
        live = True
        while live:
            live = False
            for si, st_ in enumerate(streams):
                if idx[si] < len(st_):
                    it = st_[idx[si]]
                    idx[si] += 1
                    live = True
                    if it[0] == "op":
                        self.op(*it[1:])
                    else:
                        self.dma(*it[1:5], reads=it[5], writes=it[6], **it[7])

    def op(self, eng, fn, reads=(), writes=()):
        if self.rec is not None:
            self.rec.append(("op", eng, fn, list(reads), list(writes)))
            return None
        self._deps(eng, reads, writes)
        self.cnt[eng] += 1
        ev = (self.ekey[eng], self.cnt[eng])
        self.q[eng].append(("i", fn, ev[0], 1))
        self._record(ev, reads, writes)
        return ev

    def dma(self, queue, dsem, out, in_, reads=(), writes=(), **kw):
        if self.rec is not None:
            self.rec.append(("dma", queue, dsem, out, in_, list(reads), list(writes), kw))
            return None
        assert queue == dsem.q, (queue, dsem.q)
        self._deps(queue, reads, writes)
        dsem.cnt += 16
        self.dcur[dsem.key] = dsem.cnt
        ev = (dsem.key, dsem.cnt)
        self.q[queue].append(("i", lambda e: e.dma_start(out=out, in_=in_, **kw), dsem.key, 16))
        self._record(ev, reads, writes)
        return ev

    def _replay(self, engname, eng):
        for item in self.q[engname]:
            if item[0] == "w":
                eng.wait_ge(self.sems[item[1]], item[2])
            else:
                _, fn, k, inc = item
                fn(eng).then_inc(self.sems[k], inc)
        self.q[engname] = []

    def flush(self):
        finals = [(self.ekey[e], self.cnt[e]) for e in self.cnt if self.cnt[e] > self.cnt0[e]]
        finals += [(d.key, d.cnt) for d in self.dsems if d.cnt > 0]
        for e in ENGS:
            for ev in finals:
                self._wait(e, ev)
        with self.nc.Block() as block:
            @block.tensor
            def _(e):
                self._replay("pe", e)

            @block.scalar
            def _(e):
                self._replay("act", e)

            @block.vector
            def _(e):
                self._replay("dve", e)

            @block.gpsimd
            def _(e):
                self._replay("pool", e)

            @block.sync
            def _(e):
                self._replay("sp", e)
        for e in self.cnt:
            self.eset[e][1] = self.cnt[e]
        for d in self.dsems:
            self.free_d[d.q].append((d.key, d.cnt))
        self.dsems = []
        old = self.known
        self.phase += 1
        self._new_engine_sems()
        for e in ENGS:
            for k, v in old[e].items():
                self.known[e][k] = v


class Slots:
    def __init__(self, P, st, nc, name, n, shape, dtype, q="sp"):
        self.t = [st.enter_context(nc.sbuf_tensor(uname(name), shape, dtype)) for i in range(n)]
        self.tok = [Tok(f"{name}{i}") for i in range(n)]
        self.ds = [DSem(P, uname("d_" + name), q) for i in range(n)]
        self.i = 0
        self.n = n

    def next(self):
        i = self.i
        self.i = (i + 1) % self.n
        return self.t[i], self.tok[i], self.ds[i]


C_AQ, C_AK, C_AV, C_AZ = 0, HB, 2 * HB, 3 * HB
C_GQ, C_GK, C_GV, C_GZ = 4 * HB, 5 * HB, 6 * HB, 7 * HB
C_GBA = 8 * HB
C_RX, C_RZ = 8 * HB + 2 * NH, 9 * HB + 2 * NH
RG = [[0, 1], [2, 3], [4, 5], [6, 7]]


def make_dram(nc, S, dbg):
    kind = "ExternalOutput" if dbg else "Internal"
    T = {}

    def mk(name, shape, dt, k=None):
        T[name] = nc.dram_tensor(name, shape, dt, kind=k or kind).ap()
    XC = min(S, 1024)
    mk("aqk", [2 * NH, 128, S], BF16)
    mk("av", [S, HB], BF16)
    mk("az", [S, HB], BF16)
    mk("gqkv", [3 * NH, 128, S], F32)
    mk("gz", [S, HB], BF16)
    mk("gba", [2 * NH, S], F32)
    mk("rx", [NH, 128, S], F32)
    mk("rz", [NH, 128, S], BF16)
    mk("gcs", [2 * NH, S], F32)
    mk("beta", [2 * NH, S], F32)
    mk("yT", [3 * NH, 128, S], BF16)
    mk("yTg", [3 * NH, 2 * 128, S], BF16, "Internal")
    mk("xh1", [S, 512], F32)
    mk("xh2", [S, 512], F32)
    mk("x1g", [S // XC, 2 * XC, 512], F32, "Internal")
    mk("x2g", [S // XC, 2 * XC, 512], F32, "Internal")
    return T


def start_allgather(P, pairs, toks=None):
    for i, (src, dst) in enumerate(pairs):
        ds = DSem(P, uname("d_cc"), "pool")
        writes = [toks[i]] if toks else []
        P._deps("pool", [], writes)
        ds.cnt += 1
        P.dcur[ds.key] = ds.cnt
        P.q["pool"].append(("i", lambda e, src=src, dst=dst: e.collective_compute(
            "AllGather", ALU.bypass, replica_groups=RG, ins=[src], outs=[dst]), ds.key, 1))
        P._record((ds.key, ds.cnt), [], writes)


def allgather(nc, P, pairs):
    with ExitStack() as st:
        ds = DSem(P, uname("d_cc"), "pool")
        for src, dst in pairs:
            P._deps("pool", [], [])
            ds.cnt += 1
            P.dcur[ds.key] = ds.cnt
            P.q["pool"].append(("i", lambda e, src=src, dst=dst: e.collective_compute(
                "AllGather", ALU.bypass, replica_groups=RG, ins=[src], outs=[dst]), ds.key, 1))
        P.flush()


def phase_A(nc, P, l, S, x_src, I, T, pre=None, xg_toks=None):
    XC = min(S, 1024)
    NB = S // 512
    with ExitStack() as st:
        sb = lambda name, shape, dt: st.enter_context(nc.sbuf_tensor(uname(name), shape, dt))
        Wb = sb("Wb", [128, 8, NCOL], BF16)
        nw = sb("nw", [128, 8], F32)
        ident = sb("ident", [128, 128], BF16)
        xs = sb("xs", [128, 4, D], BF16)
        junk = sb("junk", [128, D], BF16)
        ss = sb("ss", [128, 4], F32)
        rstd = sb("rstd", [128, 4], F32)
        t_W, t_nw, t_id, t_xs, t_junk, t_ss, t_rstd = (Tok() for _ in range(7))
        d_c = DSem(P, uname("dA_c"))
        P.dma("sp", d_c, nw[:], I["norm_w"][l].rearrange("(k p) -> p k", p=128),
              writes=[t_nw], allow_slow_non_contiguous=True)
        P.dma("sp", d_c, ident[:], I["ident"], writes=[t_id])
        wst = Slots(P, st, nc, "wst", 2, [128, NCOL // 2], F32)
        for kc in range(8):
            for hf in range(2):
                t, tok, ds = wst.next()
                c0 = hf * (NCOL // 2)
                P.dma("sp", ds, t[:], I["w_in"][l, kc * 128:(kc + 1) * 128, c0:c0 + NCOL // 2], writes=[tok])
                eng = "dve" if hf == 0 else "pool"
                P.op(eng, lambda e, t=t, kc=kc, c0=c0: e.tensor_scalar(
                    out=Wb[:, kc, c0:c0 + NCOL // 2], in0=t[:], scalar1=nw[:, kc:kc + 1], scalar2=None,
                    op0=ALU.mult), reads=[tok, t_nw], writes=[t_W])
        if pre is not None:
            pre()
        xts = Slots(P, st, nc, "xt", 2, [128, 4, D], F32)
        hTs = [sb("hT", [128, 8, 512], BF16) for i in range(2)]
        t_hT = [Tok(), Tok()]
        ptr = [st.enter_context(nc.psum_tensor(uname("ptr"), [128, 1024], BF16)) for i in range(2)]
        t_ptr = [Tok(), Tok()]
        pmm = [st.enter_context(nc.psum_tensor(uname("pmm"), [128, 512], F32)) for i in range(6)]
        t_pmm = [Tok() for _ in range(6)]
        so_f = Slots(P, st, nc, "sof", 4, [128, 512], F32)
        so_b = Slots(P, st, nc, "sob", 4, [128, 512], BF16)
        pi = 0
        ev_i = 0
        xq = []

        def load_x(b):
            xt, t_xt, d_xt = xts.next()
            if x_src is not None:
                P.dma("sp", d_xt, xt[:], x_src[b * 512:(b + 1) * 512, :].rearrange("(j p) d -> p j d", p=128),
                      writes=[t_xt])
            else:
                tc_, off = (b * 512) // XC, (b * 512) % XC
                for r in range(2):
                    P.dma("sp", d_xt, xt[:, :, r * 512:(r + 1) * 512],
                          T["x1g"][tc_, r * XC + off:r * XC + off + 512, :].rearrange("(j p) d -> p j d", p=128),
                          reads=([xg_toks[tc_]] if xg_toks else []), writes=[t_xt])
            xq.append((xt, t_xt))
        load_x(0)
        for b in range(NB):
            xt, t_xt = xq.pop(0)
            if b + 1 < NB:
                load_x(b + 1)
            for j in range(4):
                P.op("act", lambda e, j=j, xt=xt: e.activation(out=junk[:], in_=xt[:, j, :], func=AF.Square,
                                                              accum_out=ss[:, j:j + 1]),
                     reads=[t_xt], writes=[t_junk, t_ss])
            P.op("dve", lambda e: e.tensor_scalar(out=rstd[:], in0=ss[:], scalar1=1.0 / D, scalar2=1e-6,
                                                  op0=ALU.mult, op1=ALU.add), reads=[t_ss], writes=[t_rstd])
            P.op("act", lambda e: e.activation(out=rstd[:], in_=rstd[:], func=AF.Sqrt), reads=[t_rstd], writes=[t_rstd])
            P.op("dve", lambda e: e.reciprocal(out=rstd[:], in_=rstd[:]), reads=[t_rstd], writes=[t_rstd])
            for j in range(4):
                if j % 2 == 0:
                    P.op("act", lambda e, j=j, xt=xt: e.activation(out=xs[:, j, :], in_=xt[:, j, :], func=AF.Copy,
                                                                  scale=rstd[:, j:j + 1]),
                         reads=[t_xt, t_rstd], writes=[t_xs])
                else:
                    P.op("dve", lambda e, j=j, xt=xt: e.tensor_scalar(out=xs[:, j, :], in0=xt[:, j, :],
                                                                     scalar1=rstd[:, j:j + 1], scalar2=None,
                                                                     op0=ALU.mult),
                         reads=[t_xt, t_rstd], writes=[t_xs])
            hT = hTs[b % 2]
            th = t_hT[b % 2]
            for kc in range(8):
                pt = ptr[(kc // 2) % 2]
                tp = t_ptr[(kc // 2) % 2]
                for j in range(4):
                    o = (kc % 2) * 512 + j * 128
                    P.op("pe", lambda e, pt=pt, o=o, j=j, kc=kc: e.transpose(
                        out=pt[:, o:o + 128], in_=xs[:, j, kc * 128:(kc + 1) * 128], identity=ident[:]),
                        reads=[t_xs, t_id], writes=[tp])
                eng = "dve" if (kc // 2) % 2 == 0 else "act"
                o = (kc % 2) * 512
                if eng == "dve":
                    P.op("dve", lambda e, pt=pt, o=o, kc=kc, hT=hT: e.tensor_copy(out=hT[:, kc, :], in_=pt[:, o:o + 512]),
                         reads=[tp], writes=[th])
                else:
                    P.op("act", lambda e, pt=pt, o=o, kc=kc, hT=hT: e.activation(out=hT[:, kc, :], in_=pt[:, o:o + 512],
                                                                               func=AF.Copy),
                         reads=[tp], writes=[th])
            tok0 = b * 512

            def emit(kind, c0, ncols, dst, func, outdt, j=None, hT=hT, th=th):
                nonlocal pi, ev_i
                ps = pmm[pi % 6]
                tps = t_pmm[pi % 6]
                pi += 1
                for kc in range(8):
                    if kind == "fm":
                        P.op("pe", lambda e, ps=ps, kc=kc: e.matmul(
                            out=ps[0:ncols, :], lhsT=Wb[:, kc, c0:c0 + ncols], rhs=hT[:, kc, :],
                            start=(kc == 0), stop=(kc == 7)), reads=[t_W, th], writes=[tps])
                    else:
                        P.op("pe", lambda e, ps=ps, kc=kc: e.matmul(
                            out=ps[:, 0:ncols], lhsT=hT[:, kc, j * 128:(j + 1) * 128], rhs=Wb[:, kc, c0:c0 + ncols],
                            start=(kc == 0), stop=(kc == 7)), reads=[t_W, th], writes=[tps])
                pool = so_f if outdt == F32 else so_b
                so, t_so, d_so = pool.next()
                src = ps[0:ncols, :] if kind == "fm" else ps[:, 0:ncols]
                dsto = so[0:ncols, :] if kind == "fm" else so[:, 0:ncols]
                if func is None and ev_i % 2 == 0:
                    P.op("dve", lambda e: e.tensor_copy(out=dsto, in_=src), reads=[tps], writes=[t_so])
                else:
                    P.op("act", lambda e: e.activation(out=dsto, in_=src, func=(func or AF.Copy)),
                         reads=[tps], writes=[t_so])
                ev_i += 1
                P.dma("sp", d_so, dst, dsto, reads=[t_so])

            for c in range(2 * NH):
                emit("fm", C_AQ + c * 128, 128, T["aqk"][c, :, tok0:tok0 + 512], None, BF16)
            for c in range(3 * NH):
                emit("fm", C_GQ + c * 128, 128, T["gqkv"][c, :, tok0:tok0 + 512], None, F32)
            emit("fm", C_GBA, 2 * NH, T["gba"][:, tok0:tok0 + 512], None, F32)
            for c in range(NH):
                emit("fm", C_RX + c * 128, 128, T["rx"][c, :, tok0:tok0 + 512], None, F32)
            for c in range(NH):
                emit("fm", C_RZ + c * 128, 128, T["rz"][c, :, tok0:tok0 + 512], AF.Silu, BF16)
            for j in range(4):
                r0 = tok0 + j * 128
                emit("tm", C_AV, HB, T["av"][r0:r0 + 128, :], None, BF16, j)
                emit("tm", C_AZ, HB, T["az"][r0:r0 + 128, :], AF.Silu, BF16, j)
                emit("tm", C_GZ, HB, T["gz"][r0:r0 + 128, :], AF.Silu, BF16, j)
        P.flush()


class Ctx:
    def __init__(self, nc, P, st):
        self.nc, self.P, self.st = nc, P, st

    def sb(self, name, shape, dt):
        return self.st.enter_context(self.nc.sbuf_tensor(uname(name), shape, dt))

    def ps(self, name, shape, dt=F32):
        return self.st.enter_context(self.nc.psum_tensor(uname(name), shape, dt))

    def slots(self, name, n, shape, dt, q="sp"):
        return Slots(self.P, self.st, self.nc, name, n, shape, dt, q)

    def dsem(self, name):
        return DSem(self.P, uname(name))


def phase_E(nc, P, l, S, x_src, x_dst, I, T):
    NB = S // 512
    NCH = 6 * NH
    with ExitStack() as st:
        C = Ctx(nc, P, st)
        Wo = C.sb("Wo", [128, NCH, 512], BF16)
        t_Wo = Tok()
        wst = C.slots("wost", 2, [128, 512], F32)
        for c in range(NCH):
            t, tok, ds = wst.next()
            P.dma("sp", ds, t[:], I["w_out"][l, c * 128:(c + 1) * 128, :], writes=[tok])
            eng = "dve" if c % 2 == 0 else "pool"
            P.op(eng, lambda e, t=t, c=c: e.tensor_copy(out=Wo[:, c, :], in_=t[:]), reads=[tok], writes=[t_Wo])
        ys = C.slots("ys", 2, [128, NCH, 512], BF16)
        xts = C.slots("xe", 2, [128, 4, 512], F32)
        xo = C.slots("xo", 3, [128, 512], F32)
        pm = [C.ps("pe", [128, 512]) for _ in range(4)]
        t_pm = [Tok() for _ in range(4)]
        pi = 0
        lq_ = []

        def load_E(b):
            y, t_y, d_y = ys.next()
            xt, t_xt, d_xt = xts.next()
            for r in range(2):
                P.dma("sp", d_y, y[:, r * 3 * NH:(r + 1) * 3 * NH, :],
                      T["yTg"][:, r * 128:(r + 1) * 128, b * 512:(b + 1) * 512].rearrange("c p t -> p c t"), writes=[t_y])
            P.dma("sp", d_xt, xt[:], x_src[b * 512:(b + 1) * 512, :].rearrange("(j p) d -> p j d", p=128), writes=[t_xt])
            lq_.append((y, t_y, xt, t_xt))
        load_E(0)
        for b in range(NB):
            y, t_y, xt, t_xt = lq_.pop(0)
            if b + 1 < NB:
                load_E(b + 1)
            for j in range(4):
                o, t_o, d_o = xo.next()
                ps, tps = pm[pi % 4], t_pm[pi % 4]
                pi += 1
                for c in range(NCH):
                    P.op("pe", lambda e, ps=ps, c=c, y=y, j=j: e.matmul(
                        out=ps[:], lhsT=y[:, c, j * 128:(j + 1) * 128], rhs=Wo[:, c, :],
                        start=(c == 0), stop=(c == NCH - 1)), reads=[t_y, t_Wo], writes=[tps])
                P.op("dve", lambda e, ps=ps, o=o, xt=xt, j=j: e.tensor_tensor(
                    out=o[:], in0=ps[:], in1=xt[:, j, :], op=ALU.add), reads=[tps, t_xt], writes=[t_o])
                r0 = b * 512 + j * 128
                P.dma("sp", d_o, x_dst[r0:r0 + 128, :], o[:], reads=[t_o])
        P.flush()


def phase_F(nc, P, S, I, T, out, pre=None, xg_toks=None):
    NB = S // 512
    XC = min(S, 1024)
    with ExitStack() as st:
        C = Ctx(nc, P, st)
        if pre is not None:
            pre()
        fnw = C.sb("fnw", [128, 512], F32)
        t_fnw = Tok()
        d_c = C.dsem("dF_c")
        P.dma("sp", d_c, fnw[:], I["final_norm_w"].partition_broadcast(128), writes=[t_fnw])
        xg = C.slots("xg", 2, [128, 4, 1024], F32)
        xm = C.slots("xm", 2, [128, 4, 512], F32)
        xo = C.slots("xoF", 2, [128, 4, 512], F32)
        junk = C.sb("junkF", [128, 1024], BF16)
        ss = C.sb("ssF", [128, 4], F32)
        t_junk, t_ss = Tok(), Tok()
        lq_ = []

        def load_F(b):
            g, t_g, d_g = xg.next()
            m, t_m, d_m = xm.next()
            tc_, off = (b * 512) // XC, (b * 512) % XC
            for r in range(2):
                P.dma("sp", d_g, g[:, :, r * 512:(r + 1) * 512],
                      T["x2g"][tc_, r * XC + off:r * XC + off + 512, :].rearrange("(j p) d -> p j d", p=128),
                      reads=([xg_toks[tc_]] if xg_toks else []), writes=[t_g])
            P.dma("sp", d_m, m[:], T["xh2"][b * 512:(b + 1) * 512, :].rearrange("(j p) d -> p j d", p=128), writes=[t_m])
            lq_.append((g, t_g, m, t_m))
        load_F(0)
        for b in range(NB):
            g, t_g, m, t_m = lq_.pop(0)
            if b + 1 < NB:
                load_F(b + 1)
            for j in range(4):
                P.op("act", lambda e, g=g, j=j: e.activation(out=junk[:], in_=g[:, j, :], func=AF.Square,
                                                            accum_out=ss[:, j:j + 1]), reads=[t_g], writes=[t_junk, t_ss])
            P.op("dve", lambda e: e.tensor_scalar(out=ss[:], in0=ss[:], scalar1=1.0 / D, scalar2=1e-6,
                                                  op0=ALU.mult, op1=ALU.add), reads=[t_ss], writes=[t_ss])
            P.op("act", lambda e: e.activation(out=ss[:], in_=ss[:], func=AF.Sqrt), reads=[t_ss], writes=[t_ss])
            P.op("dve", lambda e: e.reciprocal(out=ss[:], in_=ss[:]), reads=[t_ss], writes=[t_ss])
            o, t_o, d_o = xo.next()
            for j in range(4):
                P.op("dve", lambda e, o=o, m=m, j=j: e.scalar_tensor_tensor(
                    out=o[:, j, :], in0=m[:, j, :], scalar=ss[:, j:j + 1], in1=fnw[:], op0=ALU.mult, op1=ALU.mult),
                    reads=[t_m, t_ss, t_fnw], writes=[t_o])
            P.dma("sp", d_o, out[b * 512:(b + 1) * 512, :].rearrange("(j p) d -> p j d", p=128), o[:], reads=[t_o])
        P.flush()


def phase_C(nc, P, l, S, I, T, pre=None):
    NB = S // 512
    with ExitStack() as st:
        C = Ctx(nc, P, st)
        if pre is not None:
            pre()
        d_c = C.dsem("dC_c")
        cw = C.sb("cw", [128, NH, 4], F32)
        cb = C.sb("cb", [128, NH], F32)
        gb = C.sb("gb", [128, 2, NH], F32)
        lp = C.sb("lp", [128, NH], F32)
        c8 = C.sb("c8", [128, NH], F32)
        c16 = C.sb("c16", [128, NH], F32)
        gwf = C.sb("gwf", [128, 2 * NH, 128], F32)
        gw = C.sb("gw", [128, 2 * NH, 128], BF16)
        t_c = Tok()
        for n in range(NH):
            P.dma("sp", d_c, cw[:, n, :], I["lru_conv_w"][l][:, n * 128:(n + 1) * 128].rearrange("j p -> p j"),
                  writes=[t_c], allow_slow_non_contiguous=True)
        P.dma("sp", d_c, cb[:], I["lru_conv_b"][l].rearrange("(n p) -> p n", p=128), writes=[t_c],
              allow_slow_non_contiguous=True)
        for g in range(2):
            P.dma("sp", d_c, gb[:, g, :], I["lru_gate_b"][l, g].rearrange("(n p) -> p n", p=128), writes=[t_c],
                  allow_slow_non_contiguous=True)
        P.dma("sp", d_c, lp[:], I["lru_log_param"][l].rearrange("(n p) -> p n", p=128), writes=[t_c],
              allow_slow_non_contiguous=True)
        P.dma("sp", d_c, gwf[:], I["lru_gate_w"][l].rearrange("g n d e -> d (g n) e"), writes=[t_c])
        P.op("dve", lambda e: e.tensor_copy(out=gw[:], in_=gwf[:]), reads=[t_c], writes=[t_c])
        P.op("act", lambda e: e.activation(out=c8[:], in_=lp[:], func=AF.Exp, scale=-1.0), reads=[t_c], writes=[t_c])
        P.op("act", lambda e: e.activation(out=c8[:], in_=c8[:], func=AF.Ln, bias=1.0), reads=[t_c], writes=[t_c])
        P.op("dve", lambda e: e.tensor_scalar(out=c16[:], in0=c8[:], scalar1=-16.0, scalar2=None, op0=ALU.mult),
             reads=[t_c], writes=[t_c])
        P.op("dve", lambda e: e.tensor_scalar(out=c8[:], in0=c8[:], scalar1=-8.0, scalar2=None, op0=ALU.mult),
             reads=[t_c], writes=[t_c])
        xins = [C.slots("xin", 2, [128, 515], F32) for _ in range(NH)]
        zins = [C.slots("zin", 2, [128, 512], BF16) for _ in range(NH)]
        yos = [C.slots("yo", 2, [128, 512], BF16) for _ in range(NH)]
        NW = NH
        W = []
        for i in range(NW):
            W.append(dict(
                xc=C.sb("xc", [128, 512], F32), xcb=C.sb("xcb", [128, 512], BF16),
                it=C.sb("it", [128, 512], F32), rt=C.sb("rt", [128, 512], F32),
                at=C.sb("at", [128, 512], F32), a2=C.sb("a2", [128, 512], F32),
                tok=Tok()))
        hs = [[C.sb("h", [128, 512], F32) for _ in range(2)] for n in range(NH)]
        t_h = [[Tok(), Tok()] for n in range(NH)]
        pg = [C.ps("pg", [128, 512]) for _ in range(4)]
        t_pg = [Tok() for _ in range(4)]
        lcs = [[] for _ in range(NH)]

        def load_C(b, n):
            t0 = b * 512
            lc_ = lcs[n]
            x, t_x, d_x = xins[n].next()
            z, t_z, d_z = zins[n].next()
            if b == 0:
                P.op("pool", lambda e, x=x: e.memset(x[:, 0:3], 0.0), writes=[t_x])
                P.dma("sp", d_x, x[:, 3:515], T["rx"][n, :, 0:512], writes=[t_x])
            else:
                P.dma("sp", d_x, x[:], T["rx"][n, :, t0 - 3:t0 + 512], writes=[t_x])
            P.dma("sp", d_z, z[:], T["rz"][n, :, t0:t0 + 512], writes=[t_z])
            lc_.append((x, t_x, z, t_z))
        def stream_C(n):
            load_C(0, n)
            for b in range(NB):
                t0 = b * 512
                x, t_x, z, t_z = lcs[n].pop(0)
                if b + 1 < NB:
                    load_C(b + 1, n)
                it_ = n
                w = W[n]
                tw = w["tok"]
                xc = w["xc"]
                P.op("act", lambda e, x=x, xc=xc, n=n: e.activation(
                    out=xc[:], in_=x[:, 3:515], func=AF.Identity, scale=cw[:, n, 3:4], bias=cb[:, n:n + 1]),
                    reads=[t_x, t_c], writes=[tw])
                for k in range(1, 4):
                    P.op("dve", lambda e, x=x, xc=xc, n=n, k=k: e.scalar_tensor_tensor(
                        out=xc[:], in0=x[:, 3 - k:515 - k], scalar=cw[:, n, 3 - k:4 - k], in1=xc[:],
                        op0=ALU.mult, op1=ALU.add), reads=[t_x, t_c], writes=[tw])
                P.op("pool", lambda e, w=w: e.tensor_copy(out=w["xcb"][:], in_=w["xc"][:]), reads=[tw], writes=[tw])
                p_i, tp_i = pg[(2 * it_) % 4], t_pg[(2 * it_) % 4]
                p_r, tp_r = pg[(2 * it_ + 1) % 4], t_pg[(2 * it_ + 1) % 4]
                P.op("pe", lambda e, w=w, p_i=p_i, n=n: e.matmul(out=p_i[:], lhsT=gw[:, n, :], rhs=w["xcb"][:],
                                                                start=True, stop=True), reads=[tw, t_c], writes=[tp_i])
                P.op("pe", lambda e, w=w, p_r=p_r, n=n: e.matmul(out=p_r[:], lhsT=gw[:, NH + n, :], rhs=w["xcb"][:],
                                                                start=True, stop=True), reads=[tw, t_c], writes=[tp_r])
                P.op("act", lambda e, w=w, p_i=p_i, n=n: e.activation(out=w["it"][:], in_=p_i[:], func=AF.Sigmoid,
                                                                     bias=gb[:, 0, n:n + 1]), reads=[tp_i, t_c], writes=[tw])
                P.op("act", lambda e, w=w, p_r=p_r, n=n: e.activation(out=w["rt"][:], in_=p_r[:], func=AF.Sigmoid,
                                                                     bias=gb[:, 1, n:n + 1]), reads=[tp_r, t_c], writes=[tw])
                P.op("act", lambda e, w=w, n=n: e.activation(out=w["at"][:], in_=w["rt"][:], func=AF.Exp,
                                                            scale=c8[:, n:n + 1]), reads=[tw, t_c], writes=[tw])
                P.op("act", lambda e, w=w, n=n: e.activation(out=w["a2"][:], in_=w["rt"][:], func=AF.Exp,
                                                            scale=c16[:, n:n + 1]), reads=[tw, t_c], writes=[tw])
                P.op("dve", lambda e, w=w: e.tensor_scalar(out=w["a2"][:], in0=w["a2"][:], scalar1=-1.0, scalar2=1.0,
                                                           op0=ALU.mult, op1=ALU.add), reads=[tw], writes=[tw])
                P.op("dve", lambda e, w=w: e.tensor_scalar(out=w["a2"][:], in0=w["a2"][:], scalar1=1e-30, scalar2=None,
                                                           op0=ALU.max), reads=[tw], writes=[tw])
                P.op("act", lambda e, w=w: e.activation(out=w["a2"][:], in_=w["a2"][:], func=AF.Sqrt), reads=[tw], writes=[tw])
                P.op("pool", lambda e, w=w: e.tensor_tensor(out=w["it"][:], in0=w["it"][:], in1=w["xc"][:], op=ALU.mult),
                     reads=[tw], writes=[tw])
                P.op("pool", lambda e, w=w: e.tensor_tensor(out=w["it"][:], in0=w["it"][:], in1=w["a2"][:], op=ALU.mult),
                     reads=[tw], writes=[tw])
                h, th = hs[n][b % 2], t_h[n][b % 2]
                hp, thp = hs[n][(b + 1) % 2], t_h[n][(b + 1) % 2]
                init = 0.0 if b == 0 else hp[:, 511:512]
                P.op("dve", lambda e, w=w, h=h, init=init: e.tensor_tensor_scan(
                    out=h[:], data0=w["at"][:], data1=w["it"][:], initial=init, op0=ALU.mult, op1=ALU.add),
                    reads=[tw] + ([thp] if b > 0 else []), writes=[th])
                y, t_y, d_y = yos[n].next()
                P.op("pool", lambda e, h=h, z=z, y=y: e.tensor_tensor(out=y[:], in0=h[:], in1=z[:], op=ALU.mult),
                     reads=[th, t_z], writes=[t_y])
                P.dma("sp", d_y, T["yT"][2 * NH + n, :, t0:t0 + 512], y[:], reads=[t_y])

        streams = []
        for n in range(NH):
            P.rec = []
            stream_C(n)
            streams.append(P.rec)
        P.rec = None
        P.run_merged(streams)
        P.flush()


def phase_B(nc, P, l, S, I, T):
    NB = S // 512
    NT = S // 128
    LOOK = 2
    with ExitStack() as st:
        C = Ctx(nc, P, st)
        d_c = C.dsem("dB_c")
        t_c = Tok()
        lq = C.sb("lq", [128, 256], F32)
        lj = C.sb("lj", [128, 64], F32)
        lam = C.sb("lam", [128, 4], F32)
        sw = C.sb("sw", [128, 128], F32)
        ident = C.sb("identB", [128, 128], BF16)
        zl = C.sb("zl", [1, 128], BF16)
        zr = C.sb("zr", [1, 512], BF16)
        P.dma("sp", d_c, lq[:], I["attn_lambda"][l:l + 1, :].partition_broadcast(128), writes=[t_c])
        P.dma("sp", d_c, sw[:], I["attn_subln_w"][l:l + 1, :].partition_broadcast(128), writes=[t_c])
        P.dma("sp", d_c, ident[:], I["ident"], writes=[t_c])
        P.op("pool", lambda e: e.memset(zl[:], 0.0), writes=[t_c])
        P.op("pool", lambda e: e.memset(zr[:], 0.0), writes=[t_c])
        for k in range(2):
            P.op("dve", lambda e, k=k: e.scalar_tensor_tensor(
                out=lj[:], in0=lq[:, 128 * k:128 * k + 64], scalar=1.0, in1=lq[:, 128 * k + 64:128 * k + 128],
                op0=ALU.mult, op1=ALU.mult, accum_out=lam[:, k:k + 1]), reads=[t_c], writes=[t_c])
        P.op("act", lambda e: e.activation(out=lam[:, 0:2], in_=lam[:, 0:2], func=AF.Exp), reads=[t_c], writes=[t_c])
        P.op("dve", lambda e: e.tensor_tensor(out=lam[:, 2:3], in0=lam[:, 1:2], in1=lam[:, 0:1], op=ALU.subtract),
             reads=[t_c], writes=[t_c])
        P.op("dve", lambda e: e.tensor_scalar(out=lam[:, 3:4], in0=lam[:, 2:3], scalar1=-LAMBDA_INIT[l], scalar2=None,
                                              op0=ALU.add), reads=[t_c], writes=[t_c])
        P.op("dve", lambda e: e.tensor_scalar(out=sw[:], in0=sw[:], scalar1=1.0 - LAMBDA_INIT[l], scalar2=None,
                                              op0=ALU.mult), reads=[t_c], writes=[t_c])
        neglam = lam[:, 3:4]
        kTs = C.slots("kT", 2, [128, S], BF16)
        Vxs = C.slots("Vx", 2, [128, NT, 130], BF16)
        for i in range(2):
            P.op("pool", lambda e, i=i: e.memset(Vxs.t[i][:, :, 128:130], 1.0), writes=[Vxs.tok[i]])
        qzs = [C.slots("qz0", 2, [128, 512], BF16), C.slots("qz1", 2, [128, 512], BF16)]
        for i in range(2):
            P.op("pool", lambda e, i=i: e.memset(qzs[0].t[i][64:128, :], 0.0), writes=[qzs[0].tok[i]])
            P.op("pool", lambda e, i=i: e.memset(qzs[1].t[i][0:64, :], 0.0), writes=[qzs[1].tok[i]])
        azs = C.slots("azb", 2, [128, 4, 128], BF16)
        pTs = C.slots("pT", 6, [128, 512], BF16)
        yTs = C.slots("yTs", 2, [128, 512], BF16)
        acc = [C.ps("acc", [128, 512]) for _ in range(4)]
        t_acc = [Tok() for _ in range(4)]
        pq = [C.ps("pq", [128, 512]) for _ in range(3)]
        t_pq = [Tok() for _ in range(3)]
        ptr = C.ps("ptrB", [128, 1024], BF16)
        t_ptr = Tok()
        accs = C.sb("accs", [128, 4, 512], F32)
        o = C.sb("oB", [128, 4, 128], F32)
        ybs = [C.sb("ybB", [128, 4, 128], BF16) for _ in range(2)]
        t_ybs = [Tok(), Tok()]
        rs = C.sb("rsB", [128, 8], F32)
        ssq = C.sb("ssB", [128, 4], F32)
        junk = C.sb("junkB", [128, 128], BF16)
        t_accs, t_o, t_rs, t_ss, t_junk = (Tok() for _ in range(5))

        heads = {}

        def load_head(h):
            kT, t_kT, d_kT = kTs.next()
            Vx, t_V, d_V = Vxs.next()
            P.dma("sp", d_kT, kT[:], T["aqk"][NH + h], writes=[t_kT])
            P.dma("sp", d_V, Vx[:, :, 0:128], T["av"][:, h * 128:(h + 1) * 128].rearrange("(t p) d -> p t d", p=128),
                  writes=[t_V])
            heads[h] = (kT, t_kT, Vx, t_V)

        blocks = {}

        def load_block(h, i):
            q0 = i * 512
            qz = []
            for m in range(2):
                t, tok, ds = qzs[m].next()
                P.dma("sp", ds, t[64 * m:64 * m + 64, :], T["aqk"][h, 64 * m:64 * m + 64, q0:q0 + 512], writes=[tok])
                qz.append((t, tok))
            az, t_az, d_az = azs.next()
            P.dma("sp", d_az, az[:], T["az"][q0:q0 + 512, h * 128:(h + 1) * 128].rearrange("(s p) d -> p s d", p=128),
                  writes=[t_az])
            blocks[(h, i)] = (qz, az, t_az)

        steps = []
        for h in range(NH):
            for i in range(NB):
                nj = 4 * i + 4
                for j in range(nj):
                    for m in range(2):
                        steps.append((h, i, j, m, j == 0 and m == 0, j == nj - 1 and m == 1))
        order = [(h, i) for h in range(NH) for i in range(NB)]
        load_head(0)
        load_block(0, 0)
        if len(order) > 1:
            load_block(*order[1])
        pend = {}
        deferred = []
        ep_i = [0]

        def front(k):
            h, i, j, m, first, last = steps[k]
            kT, t_kT, Vx, t_V = heads[h]
            qz, az, t_az = blocks[(h, i)]
            r = j - 4 * i
            c0 = 128 * r if r > 0 else 0
            ps, tps = pq[k % 3], t_pq[k % 3]
            qt, t_q = qz[m]
            P.op("pe", lambda e: e.matmul(out=ps[:, c0:512], lhsT=kT[:, j * 128:(j + 1) * 128], rhs=qt[:, c0:512],
                                          start=True, stop=True), reads=[t_kT, t_q], writes=[tps])
            pt, t_pt, _ = pTs.next()
            P.op("act", lambda e: e.activation(out=pt[:, c0:512], in_=ps[:, c0:512], func=AF.Exp, scale=0.125),
                 reads=[tps], writes=[t_pt])
            if r >= 0:
                P.op("pool", lambda e: e.memset(pt[64:128, 128 * r:128 * r + 64], 0.0), writes=[t_pt])
            pend[k] = (pt, t_pt)

        def epilogue1(h, i):
            qz, az, t_az = blocks[(h, i)]
            yb, t_yb = ybs[ep_i[0] % 2], t_ybs[ep_i[0] % 2]
            ep_i[0] += 1
            for a in range(4):
                if a % 2 == 0:
                    P.op("dve", lambda e, a=a: e.tensor_copy(out=accs[:, a, 0:385], in_=acc[a][:, 0:385]),
                         reads=[t_acc[a]], writes=[t_accs])
                else:
                    P.op("act", lambda e, a=a: e.activation(out=accs[:, a, 0:385], in_=acc[a][:, 0:385], func=AF.Copy),
                         reads=[t_acc[a]], writes=[t_accs])
            for s_ in range(4):
                co = (s_ % 2) * 256
                for m in range(2):
                    P.op("dve", lambda e, s_=s_, m=m, co=co: e.reciprocal(
                        out=rs[:, 4 * m + s_:4 * m + s_ + 1], in_=accs[:, 2 * m + s_ // 2, co + 128:co + 129]),
                        reads=[t_accs], writes=[t_rs])
            P.op("dve", lambda e: e.tensor_scalar(out=rs[:, 4:8], in0=rs[:, 4:8], scalar1=neglam, scalar2=None,
                                                  op0=ALU.mult), reads=[t_rs, t_c], writes=[t_rs])
            for s_ in range(4):
                co = (s_ % 2) * 256
                P.op("dve", lambda e, s_=s_, co=co: e.tensor_scalar(
                    out=o[:, s_, :], in0=accs[:, s_ // 2, co:co + 128], scalar1=rs[:, s_:s_ + 1], scalar2=None,
                    op0=ALU.mult), reads=[t_accs, t_rs], writes=[t_o])
                P.op("dve", lambda e, s_=s_, co=co: e.scalar_tensor_tensor(
                    out=o[:, s_, :], in0=accs[:, 2 + s_ // 2, co:co + 128], scalar=rs[:, 4 + s_:5 + s_], in1=o[:, s_, :],
                    op0=ALU.mult, op1=ALU.add), reads=[t_accs, t_rs], writes=[t_o])
                P.op("dve", lambda e, s_=s_: e.scalar_tensor_tensor(
                    out=junk[:], in0=o[:, s_, :], scalar=1.0, in1=o[:, s_, :], op0=ALU.mult, op1=ALU.mult,
                    accum_out=ssq[:, s_:s_ + 1]), reads=[t_o], writes=[t_junk, t_ss])
            P.op("dve", lambda e: e.tensor_scalar(out=ssq[:], in0=ssq[:], scalar1=1.0 / 128, scalar2=1e-5,
                                                  op0=ALU.mult, op1=ALU.add), reads=[t_ss], writes=[t_ss])
            P.op("act", lambda e: e.activation(out=ssq[:], in_=ssq[:], func=AF.Sqrt), reads=[t_ss], writes=[t_ss])
            P.op("dve", lambda e: e.reciprocal(out=ssq[:], in_=ssq[:]), reads=[t_ss], writes=[t_ss])
            for s_ in range(4):
                P.op("dve", lambda e, s_=s_: e.scalar_tensor_tensor(
                    out=o[:, s_, :], in0=o[:, s_, :], scalar=ssq[:, s_:s_ + 1], in1=sw[:],
                    op0=ALU.mult, op1=ALU.mult), reads=[t_o, t_ss, t_c], writes=[t_o])
                P.op("pool", lambda e, s_=s_: e.tensor_tensor(out=yb[:, s_, :], in0=o[:, s_, :], in1=az[:, s_, :],
                                                             op=ALU.mult), reads=[t_o, t_az], writes=[t_yb])

            def epilogue2():
                for s_ in range(4):
                    P.op("pe", lambda e, s_=s_: e.transpose(out=ptr[:, s_ * 128:(s_ + 1) * 128], in_=yb[:, s_, :],
                                                          identity=ident[:]), reads=[t_yb, t_c], writes=[t_ptr])
                ys, t_ys, d_ys = yTs.next()
                P.op("act", lambda e: e.activation(out=ys[:], in_=ptr[:, 0:512], func=AF.Copy),
                     reads=[t_ptr], writes=[t_ys])
                P.dma("sp", d_ys, T["yT"][h, :, i * 512:(i + 1) * 512], ys[:], reads=[t_ys])
            return epilogue2

        def back(k):
            h, i, j, m, first, last = steps[k]
            kT, t_kT, Vx, t_V = heads[h]
            r = j - 4 * i
            pt, t_pt = pend.pop(k)
            if first:
                for a in range(4):
                    P.op("pe", lambda e, a=a: e.matmul(out=acc[a][:], lhsT=zl[0:1, :], rhs=zr[0:1, :], start=True, stop=True),
                         reads=[t_c], writes=[t_acc[a]])
            for s_ in range(max(r, 0), 4):
                ai = m * 2 + s_ // 2
                co = (s_ % 2) * 256
                P.op("pe", lambda e, ai=ai, co=co, s_=s_: e.matmul(
                    out=acc[ai][:, co:co + 129], lhsT=pt[:, s_ * 128:(s_ + 1) * 128], rhs=Vx[:, j, 0:129],
                    start=False, stop=False, skip_group_check=True), reads=[t_pt, t_V], writes=[t_acc[ai]])
            if last:
                e2 = epilogue1(h, i)
                deferred.append((k + 12, e2))
                oi = order.index((h, i))
                if oi + 2 < len(order):
                    nh, ni = order[oi + 2]
                    load_block(nh, ni)
                if i == 0 and h + 1 < NH:
                    load_head(h + 1)

        n = len(steps)
        for k in range(n + LOOK):
            if k < n:
                front(k)
            if k >= LOOK:
                back(k - LOOK)
            while deferred and deferred[0][0] <= k:
                deferred.pop(0)[1]()
        for _, fn in deferred:
            fn()
        P.flush()


def phase_D(nc, P, l, S, I, T, pre=None):
    NB = S // 512
    GB = min(S, 2048)
    with ExitStack() as st:
        C = Ctx(nc, P, st)
        d_c = C.dsem("dD0_c")
        t_c = Tok()
        par = C.sb("par", [2 * NH, 4], F32)
        rmask = C.sb("rmask", [2 * NH, GB], F32)
        P.op("pool", lambda e: e.memset(par[:], 0.0), writes=[t_c])
        P.dma("sp", d_c, par[NH:2 * NH, 0:1], I["gdn_a_log"][l].rearrange("(p o) -> p o", o=1), writes=[t_c])
        P.dma("sp", d_c, par[NH:2 * NH, 1:2], I["gdn_dt_bias"][l].rearrange("(p o) -> p o", o=1), writes=[t_c])
        P.dma("sp", d_c, rmask[:], I["rmask"][0:2 * NH, 0:GB], writes=[t_c])
        P.op("act", lambda e: e.activation(out=par[:, 2:3], in_=par[:, 0:1], func=AF.Exp), reads=[t_c], writes=[t_c])
        P.op("dve", lambda e: e.tensor_scalar(out=par[:, 2:3], in0=par[:, 2:3], scalar1=-1.0, scalar2=None, op0=ALU.mult),
             reads=[t_c], writes=[t_c])
        bas = C.slots("ba", 2, [2 * NH, GB], F32)
        sg = C.slots("sg", 2, [2 * NH, GB], F32)
        gg = C.slots("gg", 2, [2 * NH, GB], F32)
        gc = C.slots("gc", 2, [2 * NH, GB], F32)
        for b in range(S // GB):
            c0 = b * GB
            ba, t_ba, d_ba = bas.next()
            sgt, t_sg, d_sg = sg.next()
            g, t_g, _ = gg.next()
            gct, t_gc, d_gc = gc.next()
            P.dma("sp", d_ba, ba[:], T["gba"][:, c0:c0 + GB], writes=[t_ba])
            P.op("act", lambda e, ba=ba, sgt=sgt: e.activation(out=sgt[:], in_=ba[:], func=AF.Sigmoid), reads=[t_ba], writes=[t_sg])
            P.dma("sp", d_sg, T["beta"][:, c0:c0 + GB], sgt[:], reads=[t_sg])
            P.op("act", lambda e, ba=ba, g=g: e.activation(out=g[:], in_=ba[:], func=AF.Exp, bias=par[:, 1:2]),
                 reads=[t_ba, t_c], writes=[t_g])
            P.op("act", lambda e, g=g: e.activation(out=g[:], in_=g[:], func=AF.Ln, bias=1.0), reads=[t_g], writes=[t_g])
            P.op("dve", lambda e, g=g: e.tensor_scalar(out=g[:], in0=g[:], scalar1=par[:, 2:3], scalar2=None, op0=ALU.mult),
                 reads=[t_g, t_c], writes=[t_g])
            P.op("dve", lambda e, g=g, gct=gct: e.tensor_tensor_scan(out=gct[:], data0=rmask[:], data1=g[:], initial=0.0,
                                                                    op0=ALU.mult, op1=ALU.add), reads=[t_g, t_c], writes=[t_gc])
            P.dma("sp", d_gc, T["gcs"][:, c0:c0 + GB], gct[:], reads=[t_gc])
        P.flush()
    with ExitStack() as st:
        C = Ctx(nc, P, st)
        if pre is not None:
            pre()
        d_c = C.dsem("dD_c")
        t_c = Tok()
        gcw = C.sb("gcw", [128, 3 * NH, 4], F32)
        gnw = C.sb("gnw", [128, 128], F32)
        onesf = C.sb("onesf", [128, 128], F32)
        identf = C.sb("identf", [128, 128], F32)
        ident = C.sb("identD", [128, 128], BF16)
        mI = C.sb("mI", [128, 512], F32)
        mS = C.sb("mS", [128, 512], F32)
        for c in range(3 * NH):
            P.dma("sp", d_c, gcw[:, c, :], I["gdn_conv_w"][l][:, c * 128:(c + 1) * 128].rearrange("j p -> p j"),
                  writes=[t_c], allow_slow_non_contiguous=True)
        P.dma("sp", d_c, gnw[:], I["gdn_norm_w"][l:l + 1, :].partition_broadcast(128), writes=[t_c])
        P.dma("sp", d_c, onesf[:], I["onesf"], writes=[t_c])
        P.dma("sp", d_c, identf[:], I["identf"], writes=[t_c])
        P.dma("sp", d_c, ident[:], I["ident"], writes=[t_c])
        P.dma("sp", d_c, mI[:], I["masks"][0], writes=[t_c])
        P.dma("sp", d_c, mS[:], I["masks"][1], writes=[t_c])
        p1 = C.ps("p1", [128, 512]); pS = C.ps("pS", [128, 512]); po = C.ps("po", [128, 512])
        ptr = C.ps("ptrD", [128, 1024], BF16)
        t_p1 = [Tok() for _ in range(4)]; t_pS = [Tok() for _ in range(4)]; t_po = [Tok() for _ in range(4)]
        t_ptr = [Tok() for _ in range(4)]
        pp = [C.ps("pp", [128, 512]) for _ in range(4)]
        t_pp = [Tok() for _ in range(4)]
        R, H, WkH, tWH = [], [], [], []
        for h in range(NH):
            R.append(dict(
                S=C.sb("Sst", [128, 128], F32), tS=Tok(), vn=C.sb("vn", [128, 4, 128], BF16), tvn=Tok(),
                y=C.sb("yD", [128, 128], F32), yb=C.sb("ybD", [128, 128], BF16), ss=C.sb("ssD", [128, 1], F32),
                ty=Tok(), junk=C.sb("junkD", [128, 128], BF16)))
            P.op("pool", lambda e, h=h: e.memset(R[h]["S"][:], 0.0), writes=[R[h]["tS"]])
            H.append([dict(
                UW=C.sb("UW", [128, 4, 256], F32), wT=C.sb("wT", [128, 512], F32), qg=C.sb("qg", [128, 512], F32),
                AT=C.sb("AT", [128, 4, 128], BF16), kd=C.sb("kd", [128, 4, 128], BF16),
                egl=C.sb("egl", [128, 8], F32), tD=Tok()) for _ in range(2)])
            WkH.append(dict(
                xs=[C.sb("xsD", [128, 512], F32) for _ in range(3)], sq=C.sb("sqD", [128, 512], F32),
                rn=C.sb("rnD", [128, 512], F32), kTb=C.sb("kTb", [128, 512], BF16), qTb=C.sb("qTb", [128, 512], BF16),
                egb=C.sb("egb", [128, 512], F32), dec=C.sb("dec", [128, 512], F32), dI=C.sb("dI", [128, 512], F32),
                dS=C.sb("dSm", [128, 512], F32), A=[C.sb("Am", [128, 512], F32) for _ in range(2)],
                B=[C.sb("Bm", [128, 512], F32) for _ in range(2)], X=C.sb("Xm", [128, 4, 256], F32),
                sd=C.sb("sd", [128, 16], F32)))
            tWH.append({k: Tok() for k in ("xs0", "xs1", "xs2", "sq", "rn", "kTb", "qTb", "egb", "dec", "dI", "dS",
                                           "A0", "A1", "B0", "B1", "X", "sd")})
        gzs = C.slots("gzb", 4, [128, 4, 128], BF16)
        yTs = C.slots("yTsD", 4, [128, 512], BF16)
        xin = C.slots("xinD", 6, [128, 515], F32)
        scs = C.slots("sc", 4, [128, 24], F32)
        gbrs = C.slots("gbr", 4, [128, 512], F32)
        ppi = [0, 0]

        def prep(b, h):
            t0 = b * 512
            Hh = H[h][b % 2]
            Wk = WkH[h]
            tW = tWH[h]

            def bank():
                i = 2 * h + ppi[h] % 2
                ppi[h] += 1
                return pp[i], t_pp[i]
            sc, t_sc, d_sc = scs.next()
            gbr, t_gbr, d_gbr = gbrs.next()
            grow = T["gcs"][NH + h:NH + h + 1, :]
            brow = T["beta"][h:h + 1, :]
            P.dma("sp", d_gbr, gbr[:], grow[:, t0:t0 + 512].partition_broadcast(128), writes=[t_gbr])
            P.dma("sp", d_sc, sc[:, 0:4], T["gcs"][NH + h, t0:t0 + 512].rearrange("(t p) -> p t", p=128), writes=[t_sc],
                  allow_slow_non_contiguous=True)
            P.dma("sp", d_sc, sc[:, 4:8], T["beta"][h, t0:t0 + 512].rearrange("(t p) -> p t", p=128), writes=[t_sc],
                  allow_slow_non_contiguous=True)
            sd = Wk["sd"]
            P.op("act", lambda e, sc=sc: e.activation(out=sd[:, 0:4], in_=sc[:, 0:4], func=AF.Exp),
                 reads=[t_sc], writes=[tW["sd"]])
            P.op("dve", lambda e, sc=sc: e.tensor_scalar(out=sd[:, 4:8], in0=sc[:, 4:8], scalar1=-1.0, scalar2=None,
                                                        op0=ALU.mult), reads=[t_sc], writes=[tW["sd"]])
            for hf in range(2):
                P.op("dve", lambda e, sc=sc, gbr=gbr, hf=hf: e.tensor_tensor(
                    out=sd[hf * 64:(hf + 1) * 64, 8:12], in0=gbr[hf * 64:(hf + 1) * 64, hf * 64 + 63:512:128],
                    in1=sc[hf * 64:(hf + 1) * 64, 0:4], op=ALU.subtract), reads=[t_sc, t_gbr], writes=[tW["sd"]])
            P.op("act", lambda e: e.activation(out=sd[:, 8:12], in_=sd[:, 8:12], func=AF.Exp),
                 reads=[tW["sd"]], writes=[tW["sd"]])
            P.op("act", lambda e, gbr=gbr, Hh=Hh: e.activation(out=Hh["egl"][:], in_=gbr[:, 63:512:64], func=AF.Exp),
                 reads=[t_gbr], writes=[Hh["tD"]])
            for w in range(3):
                x, t_x, d_x = xin.next()
                ch = w * NH + h
                if b == 0:
                    P.op("pool", lambda e, x=x: e.memset(x[:, 0:3], 0.0), writes=[t_x])
                    P.dma("sp", d_x, x[:, 3:515], T["gqkv"][ch, :, 0:512], writes=[t_x])
                else:
                    P.dma("sp", d_x, x[:], T["gqkv"][ch, :, t0 - 3:t0 + 512], writes=[t_x])
                xs = Wk["xs"][w]
                tx = tW[f"xs{w}"]
                P.op("act", lambda e, x=x, xs=xs, ch=ch: e.activation(out=xs[:], in_=x[:, 3:515], func=AF.Copy,
                                                                     scale=gcw[:, ch, 3:4]), reads=[t_x, t_c], writes=[tx])
                for k in range(1, 4):
                    P.op("dve", lambda e, x=x, xs=xs, ch=ch, k=k: e.scalar_tensor_tensor(
                        out=xs[:], in0=x[:, 3 - k:515 - k], scalar=gcw[:, ch, 3 - k:4 - k], in1=xs[:],
                        op0=ALU.mult, op1=ALU.add), reads=[t_x, t_c], writes=[tx])
                P.op("act", lambda e, xs=xs: e.activation(out=xs[:], in_=xs[:], func=AF.Silu), reads=[tx], writes=[tx])
            xq, xk, xv = Wk["xs"]
            for w, xx in ((0, xq), (1, xk)):
                tx = tW[f"xs{w}"]
                P.op("act", lambda e, xx=xx: e.activation(out=Wk["sq"][:], in_=xx[:], func=AF.Square),
                     reads=[tx], writes=[tW["sq"]])
                pb, tpb = bank()
                P.op("pe", lambda e, pb=pb: e.matmul(out=pb[:], lhsT=onesf[:], rhs=Wk["sq"][:], start=True, stop=True),
                     reads=[tW["sq"], t_c], writes=[tpb])
                P.op("dve", lambda e, pb=pb: e.tensor_scalar(out=Wk["rn"][:], in0=pb[:], scalar1=1e-6, scalar2=None,
                                                            op0=ALU.add), reads=[tpb], writes=[tW["rn"]])
                P.op("act", lambda e: e.activation(out=Wk["rn"][:], in_=Wk["rn"][:], func=AF.Sqrt),
                     reads=[tW["rn"]], writes=[tW["rn"]])
                P.op("dve", lambda e: e.reciprocal(out=Wk["rn"][:], in_=Wk["rn"][:]), reads=[tW["rn"]], writes=[tW["rn"]])
                if w == 0:
                    P.op("dve", lambda e, xx=xx: e.scalar_tensor_tensor(
                        out=xx[:], in0=xx[:], scalar=128.0 ** -0.5, in1=Wk["rn"][:], op0=ALU.mult, op1=ALU.mult),
                        reads=[tx, tW["rn"]], writes=[tx])
                else:
                    P.op("dve", lambda e, xx=xx: e.tensor_tensor(out=xx[:], in0=xx[:], in1=Wk["rn"][:], op=ALU.mult),
                         reads=[tx, tW["rn"]], writes=[tx])
            P.op("pool", lambda e: e.tensor_copy(out=Wk["qTb"][:], in_=xq[:]), reads=[tW["xs0"]], writes=[tW["qTb"]])
            P.op("pool", lambda e: e.tensor_copy(out=Wk["kTb"][:], in_=xk[:]), reads=[tW["xs1"]], writes=[tW["kTb"]])
            P.op("act", lambda e, gbr=gbr: e.activation(out=Wk["egb"][:], in_=gbr[:], func=AF.Exp),
                 reads=[t_gbr], writes=[tW["egb"]])
            P.op("pool", lambda e, Hh=Hh: e.tensor_tensor(out=Hh["qg"][:], in0=xq[:], in1=Wk["egb"][:], op=ALU.mult),
                 reads=[tW["xs0"], tW["egb"]], writes=[Hh["tD"]])
            X = Wk["X"]
            pv, tpv = bank()
            for t in range(4):
                P.op("pe", lambda e, pv=pv, t=t: e.transpose(out=pv[:, t * 128:(t + 1) * 128],
                                                            in_=xv[:, t * 128:(t + 1) * 128], identity=identf[:]),
                     reads=[tW["xs2"], t_c], writes=[tpv])
            P.op("act", lambda e, pv=pv: e.activation(out=X[:, :, 0:128], in_=pv[:].rearrange("p (t d) -> p t d", t=4),
                                                     func=AF.Copy), reads=[tpv], writes=[tW["X"]])
            pk, tpk = bank()
            for t in range(4):
                P.op("pe", lambda e, pk=pk, t=t: e.transpose(out=pk[:, t * 128:(t + 1) * 128],
                                                            in_=xk[:, t * 128:(t + 1) * 128], identity=identf[:]),
                     reads=[tW["xs1"], t_c], writes=[tpk])
            for t in range(4):
                P.op("act", lambda e, pk=pk, t=t: e.activation(out=X[:, t, 128:256], in_=pk[:, t * 128:(t + 1) * 128],
                                                              func=AF.Copy, scale=sd[:, t:t + 1]),
                     reads=[tpk, tW["sd"]], writes=[tW["X"]])
                P.op("act", lambda e, pk=pk, t=t, Hh=Hh: e.activation(
                    out=Hh["kd"][:, t, :], in_=pk[:, t * 128:(t + 1) * 128], func=AF.Copy, scale=sd[:, 8 + t:9 + t]),
                    reads=[tpk, tW["sd"]], writes=[Hh["tD"]])
            for t in range(4):
                P.op("dve", lambda e, t=t, gbr=gbr, sc=sc: e.tensor_scalar(
                    out=Wk["dec"][:, t * 128:(t + 1) * 128], in0=gbr[:, t * 128:(t + 1) * 128], scalar1=sc[:, t:t + 1],
                    scalar2=0.0, op0=ALU.subtract, op1=ALU.min), reads=[t_gbr, t_sc], writes=[tW["dec"]])
            P.op("act", lambda e: e.activation(out=Wk["dec"][:], in_=Wk["dec"][:], func=AF.Exp),
                 reads=[tW["dec"]], writes=[tW["dec"]])
            P.op("pool", lambda e: e.tensor_tensor(out=Wk["dI"][:], in0=Wk["dec"][:], in1=mI[:], op=ALU.mult),
                 reads=[tW["dec"], t_c], writes=[tW["dI"]])
            P.op("pool", lambda e: e.tensor_tensor(out=Wk["dS"][:], in0=Wk["dec"][:], in1=mS[:], op=ALU.mult),
                 reads=[tW["dec"], t_c], writes=[tW["dS"]])
            pkk, tpkk = bank()
            pqk, tpqk = bank()
            for t in range(4):
                sl = slice(t * 128, (t + 1) * 128)
                P.op("pe", lambda e, sl=sl, pkk=pkk: e.matmul(out=pkk[:, sl], lhsT=Wk["kTb"][:, sl], rhs=Wk["kTb"][:, sl],
                                                             start=True, stop=True), reads=[tW["kTb"]], writes=[tpkk])
                P.op("pe", lambda e, sl=sl, pqk=pqk: e.matmul(out=pqk[:, sl], lhsT=Wk["kTb"][:, sl], rhs=Wk["qTb"][:, sl],
                                                             start=True, stop=True), reads=[tW["kTb"], tW["qTb"]], writes=[tpqk])
            P.op("dve", lambda e, pqk=pqk, Hh=Hh: e.tensor_tensor(
                out=Hh["AT"][:].rearrange("p t d -> p (t d)"), in0=pqk[:], in1=Wk["dI"][:], op=ALU.mult),
                reads=[tpqk, tW["dI"]], writes=[Hh["tD"]])
            A, B = Wk["A"], Wk["B"]
            for t in range(4):
                sl = slice(t * 128, (t + 1) * 128)
                P.op("dve", lambda e, sl=sl, t=t, pkk=pkk: e.scalar_tensor_tensor(
                    out=A[0][:, sl], in0=pkk[:, sl], scalar=sd[:, 4 + t:5 + t], in1=Wk["dS"][:, sl],
                    op0=ALU.mult, op1=ALU.mult), reads=[tpkk, tW["sd"], tW["dS"]], writes=[tW["A0"]])
            pbt, tpbt = bank()
            for t in range(4):
                sl = slice(t * 128, (t + 1) * 128)
                P.op("pe", lambda e, sl=sl, pbt=pbt: e.transpose(out=pbt[:, sl], in_=A[0][:, sl], identity=identf[:]),
                     reads=[tW["A0"], t_c], writes=[tpbt])
            P.op("act", lambda e, pbt=pbt: e.activation(out=B[0][:], in_=pbt[:], func=AF.Copy),
                 reads=[tpbt], writes=[tW["B0"]])

            def xupd(cur):
                px0, tpx0 = bank()
                px1, tpx1 = bank()
                for t in range(4):
                    px, tpx = (px0, tpx0) if t < 2 else (px1, tpx1)
                    o_ = (t % 2) * 256
                    P.op("pe", lambda e, px=px, o_=o_, t=t: e.matmul(
                        out=px[:, o_:o_ + 256], lhsT=A[cur][:, t * 128:(t + 1) * 128], rhs=X[:, t, :],
                        start=True, stop=True), reads=[tW[f"A{cur}"], tW["X"]], writes=[tpx])
                for hf, (px, tpx) in enumerate(((px0, tpx0), (px1, tpx1))):
                    P.op("dve", lambda e, px=px, hf=hf: e.tensor_tensor(
                        out=X[:, 2 * hf:2 * hf + 2, :].rearrange("p t d -> p (t d)"),
                        in0=X[:, 2 * hf:2 * hf + 2, :].rearrange("p t d -> p (t d)"), in1=px[:], op=ALU.add),
                        reads=[tpx, tW["X"]], writes=[tW["X"]])

            cur = 0
            xupd(cur)
            for lev in range(5):
                nxt = 1 - cur
                pa, tpa = bank()
                for t in range(4):
                    sl = slice(t * 128, (t + 1) * 128)
                    P.op("pe", lambda e, sl=sl, pa=pa, cur=cur: e.matmul(out=pa[:, sl], lhsT=B[cur][:, sl], rhs=A[cur][:, sl],
                                                                        start=True, stop=True),
                         reads=[tW[f"A{cur}"], tW[f"B{cur}"]], writes=[tpa])
                if lev < 4:
                    pb2, tpb2 = bank()
                    for t in range(4):
                        sl = slice(t * 128, (t + 1) * 128)
                        P.op("pe", lambda e, sl=sl, pb2=pb2, cur=cur: e.matmul(
                            out=pb2[:, sl], lhsT=A[cur][:, sl], rhs=B[cur][:, sl], start=True, stop=True),
                            reads=[tW[f"A{cur}"], tW[f"B{cur}"]], writes=[tpb2])
                P.op("act", lambda e, pa=pa, nxt=nxt: e.activation(out=A[nxt][:], in_=pa[:], func=AF.Copy),
                     reads=[tpa], writes=[tW[f"A{nxt}"]])
                if lev < 4:
                    P.op("dve", lambda e, pb2=pb2, nxt=nxt: e.tensor_copy(out=B[nxt][:], in_=pb2[:]),
                         reads=[tpb2], writes=[tW[f"B{nxt}"]])
                cur = nxt
                xupd(cur)
            for t in range(4):
                P.op("act", lambda e, t=t, Hh=Hh, sc=sc: e.activation(out=Hh["UW"][:, t, :], in_=X[:, t, :], func=AF.Copy,
                                                                    scale=sc[:, 4 + t:5 + t]),
                     reads=[tW["X"], t_sc], writes=[Hh["tD"]])
            pw, tpw = bank()
            for t in range(4):
                P.op("pe", lambda e, t=t, pw=pw, Hh=Hh: e.transpose(out=pw[:, t * 128:(t + 1) * 128],
                                                                  in_=Hh["UW"][:, t, 128:256], identity=identf[:]),
                     reads=[Hh["tD"], t_c], writes=[tpw])
            P.op("dve", lambda e, pw=pw, Hh=Hh: e.tensor_copy(out=Hh["wT"][:], in_=pw[:]), reads=[tpw], writes=[Hh["tD"]])

        def rec(b):
            t0 = b * 512
            gzl = []
            for h in range(NH):
                gz, t_gz, d_gz = gzs.next()
                P.dma("sp", d_gz, gz[:], T["gz"][t0:t0 + 512, h * 128:(h + 1) * 128].rearrange("(s p) d -> p s d", p=128),
                      writes=[t_gz])
                ys_, t_ys, d_ys = yTs.next()
                gzl.append((gz, t_gz, ys_, t_ys, d_ys))
            for c in range(8):
                t = c // 2
                r0 = (c % 2) * 64
                rs_ = slice(r0, r0 + 64)
                for h in range(NH):
                    Hh = dict(H[h][b % 2])
                    Hh.update(R[h])
                    hs = slice(h * 128, (h + 1) * 128)
                    cs = slice(t * 128 + r0, t * 128 + r0 + 64)
                    P.op("pe", lambda e, Hh=Hh, hs=hs, cs=cs, rs_=rs_: e.matmul(
                        out=p1[rs_, hs], lhsT=Hh["wT"][:, cs], rhs=Hh["S"][:], start=True, stop=True),
                        reads=[Hh["tD"], Hh["tS"]], writes=[t_p1[h]])
                    P.op("dve", lambda e, Hh=Hh, hs=hs, rs_=rs_, t=t: e.tensor_tensor(
                        out=Hh["vn"][rs_, t, :], in0=Hh["UW"][rs_, t, 0:128], in1=p1[rs_, hs], op=ALU.subtract),
                        reads=[Hh["tD"], t_p1[h]], writes=[Hh["tvn"]])
                    P.op("pe", lambda e, Hh=Hh, hs=hs, cs=cs, rs_=rs_: e.matmul(
                        out=po[rs_, hs], lhsT=Hh["qg"][:, cs], rhs=Hh["S"][:], start=True, stop=False),
                        reads=[Hh["tD"], Hh["tS"]], writes=[t_po[h]])
                    P.op("pe", lambda e, Hh=Hh, hs=hs, rs_=rs_, t=t, r0=r0: e.matmul(
                        out=po[rs_, hs], lhsT=Hh["AT"][rs_, t, r0:r0 + 64], rhs=Hh["vn"][rs_, t, :], start=False, stop=True),
                        reads=[Hh["tD"], Hh["tvn"]], writes=[t_po[h]])
                    P.op("pe", lambda e, Hh=Hh, hs=hs, rs_=rs_, t=t: e.matmul(
                        out=pS[:, hs], lhsT=Hh["kd"][rs_, t, :], rhs=Hh["vn"][rs_, t, :], start=True, stop=True),
                        reads=[Hh["tD"], Hh["tvn"]], writes=[t_pS[h]])
                    P.op("dve", lambda e, Hh=Hh, hs=hs, c=c: e.scalar_tensor_tensor(
                        out=Hh["S"][:], in0=Hh["S"][:], scalar=Hh["egl"][:, c:c + 1], in1=pS[:, hs],
                        op0=ALU.mult, op1=ALU.add), reads=[Hh["tD"], t_pS[h], Hh["tS"]], writes=[Hh["tS"]])
                    if c % 2 == 1:
                        gz, t_gz, ys_, t_ys, d_ys = gzl[h]
                        P.op("dve", lambda e, Hh=Hh, hs=hs: e.tensor_copy(out=Hh["y"][:], in_=po[:, hs]),
                             reads=[t_po[h]], writes=[Hh["ty"]])
                        P.op("act", lambda e, Hh=Hh, hs=hs: e.activation(out=Hh["junk"][:], in_=Hh["y"][:], func=AF.Square,
                                                                        accum_out=Hh["ss"][:]),
                             reads=[Hh["ty"]], writes=[Hh["ty"]])
                        P.op("dve", lambda e, Hh=Hh: e.tensor_scalar(out=Hh["ss"][:], in0=Hh["ss"][:], scalar1=1.0 / 128,
                                                                     scalar2=1e-6, op0=ALU.mult, op1=ALU.add),
                             reads=[Hh["ty"]], writes=[Hh["ty"]])
                        P.op("act", lambda e, Hh=Hh: e.activation(out=Hh["ss"][:], in_=Hh["ss"][:], func=AF.Sqrt),
                             reads=[Hh["ty"]], writes=[Hh["ty"]])
                        P.op("dve", lambda e, Hh=Hh: e.reciprocal(out=Hh["ss"][:], in_=Hh["ss"][:]),
                             reads=[Hh["ty"]], writes=[Hh["ty"]])
                        P.op("dve", lambda e, Hh=Hh, hs=hs: e.scalar_tensor_tensor(
                            out=Hh["y"][:], in0=Hh["y"][:], scalar=Hh["ss"][:], in1=gnw[:], op0=ALU.mult, op1=ALU.mult),
                            reads=[Hh["ty"], t_c], writes=[Hh["ty"]])
                        P.op("pool", lambda e, Hh=Hh, gz=gz, t=t: e.tensor_tensor(out=Hh["yb"][:], in0=Hh["y"][:], in1=gz[:, t, :],
                                                                                 op=ALU.mult),
                             reads=[Hh["ty"], t_gz], writes=[Hh["ty"]])
                        P.op("pe", lambda e, Hh=Hh, hs=hs: e.transpose(out=ptr[:, hs], in_=Hh["yb"][:], identity=ident[:]),
                             reads=[Hh["ty"], t_c], writes=[t_ptr[h]])
                        P.op("act", lambda e, hs=hs, ys_=ys_, t=t: e.activation(out=ys_[:, t * 128:(t + 1) * 128], in_=ptr[:, hs],
                                                                              func=AF.Copy), reads=[t_ptr[h]], writes=[t_ys])
                        if t == 3:
                            P.dma("sp", d_ys, T["yT"][NH + h, :, t0:t0 + 512], ys_[:], reads=[t_ys])

        for b in range(NB + 1):
            streams = []
            if b < NB:
                for h in range(NH):
                    P.rec = []
                    prep(b, h)
                    streams.append(P.rec)
            if b > 0:
                P.rec = []
                rec(b - 1)
                streams.append(P.rec)
            P.rec = None
            P.run_merged(streams)
        P.flush()


def declare_inputs(nc, S):
    I = {}

    def inp(name, shape, dt=F32):
        I[name] = nc.dram_tensor(name, shape, dt, kind="ExternalInput").ap()
    inp("x", [S, D])
    inp("norm_w", [2, D])
    inp("w_in", [2, D, NCOL])
    inp("attn_lambda", [2, 256])
    inp("attn_subln_w", [2, 128])
    inp("gdn_conv_w", [2, 4, 3 * HB])
    inp("gdn_a_log", [2, NH])
    inp("gdn_dt_bias", [2, NH])
    inp("gdn_norm_w", [2, 128])
    inp("lru_conv_w", [2, 4, HB])
    inp("lru_conv_b", [2, HB])
    inp("lru_gate_w", [2, 2, NH, 128, 128])
    inp("lru_gate_b", [2, 2, HB])
    inp("lru_log_param", [2, HB])
    inp("w_out", [2, 1536, 512])
    inp("final_norm_w", [1, 512])
    inp("xh", [S, 512])
    inp("ident", [128, 128], BF16)
    inp("identf", [128, 128], F32)
    inp("masks", [2, 128, 512], F32)
    inp("onesf", [128, 128], F32)
    inp("rmask", [8, 2048], F32)
    return I


def const_inputs():
    ident = np.eye(128, dtype=np.float32)
    i = np.arange(128)
    same = (i[:, None] // 64) == (i[None, :] // 64)
    m_incl = (same & (i[None, :] >= i[:, None])).astype(np.float32)
    m_strict = (same & (i[None, :] > i[:, None])).astype(np.float32)
    masks = np.stack([np.tile(m_incl, (1, 4)), np.tile(m_strict, (1, 4))])
    rmask = np.tile((np.arange(2048) % 64 != 0).astype(np.float32)[None, :], (8, 1))
    return {"ident": ident.astype(ml_dtypes.bfloat16), "identf": ident, "masks": masks,
            "onesf": np.ones((128, 128), np.float32), "rmask": rmask}


def pack_inputs(inp, b, hh, S=None):
    x = inp["x"][b] if S is None else inp["x"][b][:S]
    h2 = slice(hh * HB, (hh + 1) * HB)
    nh = slice(hh * NH, (hh + 1) * NH)
    w = inp["w_in"]
    cols = []
    for br in range(8):
        cols.append(np.arange(br * 512 + hh * HB, br * 512 + (hh + 1) * HB))
    cols.append(np.arange(4096 + hh * NH, 4096 + (hh + 1) * NH))
    cols.append(np.arange(4100 + hh * NH, 4100 + (hh + 1) * NH))
    cols.append(np.arange(4104 + hh * HB, 4104 + (hh + 1) * HB))
    cols.append(np.arange(4616 + hh * HB, 4616 + (hh + 1) * HB))
    cols = np.concatenate(cols)
    gcw = inp["gdn_conv_w"]
    gcw = np.concatenate([gcw[:, :, br * 512 + hh * HB:br * 512 + (hh + 1) * HB] for br in range(3)], axis=-1)
    rows = []
    for r in range(2):
        for br in range(3):
            rows.append(np.arange(br * 512 + r * HB, br * 512 + (r + 1) * HB))
    rows = np.concatenate(rows)
    dh = slice(hh * 512, (hh + 1) * 512)
    m = {"x": x, "xh": x[:, dh], "norm_w": inp["norm_w"], "w_in": w[:, :, cols],
         "attn_lambda": inp["attn_lambda"].reshape(2, 256), "attn_subln_w": inp["attn_subln_w"],
         "gdn_conv_w": gcw, "gdn_a_log": inp["gdn_a_log"][:, nh], "gdn_dt_bias": inp["gdn_dt_bias"][:, nh],
         "gdn_norm_w": inp["gdn_norm_w"], "lru_conv_w": inp["lru_conv_w"][:, :, h2], "lru_conv_b": inp["lru_conv_b"][:, h2],
         "lru_gate_w": inp["lru_gate_w"][:, :, nh], "lru_gate_b": inp["lru_gate_b"][:, :, h2],
         "lru_log_param": inp["lru_log_param"][:, h2], "w_out": inp["w_out"][:, rows][:, :, dh],
         "final_norm_w": inp["final_norm_w"][dh].reshape(1, 512)}
    m.update(const_inputs())
    return {k: np.ascontiguousarray(v) for k, v in m.items()}


def build(S, phases="ABCDE", dbg=False, nlayers=DEPTH):
    nc = bass.Bass("TRN2", target_bir_lowering=False)
    I = declare_inputs(nc, S)
    out = nc.dram_tensor("out", [S, 512], F32, kind="ExternalOutput").ap()
    T = make_dram(nc, S, dbg)
    XC = min(S, 1024)
    with ExitStack() as stack:
        P = Prog(nc, stack)
        xg_pre, xg_toks = None, None
        for l in range(nlayers):
            if "A" in phases:
                phase_A(nc, P, l, S, I["x"] if l == 0 else None, I, T, pre=xg_pre, xg_toks=xg_toks)
            if "B" in phases:
                phase_B(nc, P, l, S, I, T)
            full = "E" in phases
            if "C" in phases:
                phase_C(nc, P, l, S, I, T, pre=(lambda: start_allgather(
                    P, [(T["yT"][c], T["yTg"][c]) for c in range(0, NH)])) if full else None)
            if "D" in phases:
                phase_D(nc, P, l, S, I, T, pre=(lambda: start_allgather(
                    P, [(T["yT"][c], T["yTg"][c]) for c in range(2 * NH, 3 * NH)])) if full else None)
            if full:
                allgather(nc, P, [(T["yT"][c], T["yTg"][c]) for c in range(NH, 2 * NH)])
                xh_src = I["xh"] if l == 0 else T["xh1"]
                xh_dst = T["xh1"] if l == 0 else T["xh2"]
                phase_E(nc, P, l, S, xh_src, xh_dst, I, T)
                xg = T["x1g"] if l == 0 else T["x2g"]
                xg_toks = [Tok() for _ in range(S // XC)]
                xg_pre = (lambda xh_dst=xh_dst, xg=xg, toks=xg_toks: start_allgather(
                    P, [(xh_dst[tc * XC:(tc + 1) * XC, :], xg[tc]) for tc in range(S // XC)], toks))
                if l == DEPTH - 1:
                    phase_F(nc, P, S, I, T, out, pre=xg_pre, xg_toks=xg_toks)
    return nc


def kernel(**inputs):
    inp = {k: np.asarray(v) for k, v in inputs.items()}
    B, S, _ = inp["x"].shape
    nc = build(S)
    in_maps = [pack_inputs(inp, c // 2, c % 2) for c in range(2 * B)]
    res = run_bass_kernel_spmd(nc, in_maps, core_ids=list(range(2 * B)))
    out = np.empty((B, S, D), np.float32)
    for c in range(2 * B):
        out[c // 2, :, (c % 2) * 512:(c % 2 + 1) * 512] = np.asarray(res.results[c]["out"], dtype=np.float32)
    return out
```

```python
from contextlib import ExitStack
import numpy as np
import ml_dtypes
import concourse.bass as bass
import concourse.mybir as mybir
from concourse.bass_utils import run_bass_kernel_spmd

F32 = mybir.dt.float32
BF16 = mybir.dt.bfloat16
AF = mybir.ActivationFunctionType
ALU = mybir.AluOpType
AX = mybir.AxisListType

ENGS = ("pe", "act", "dve", "pool", "sp")
D = 1024
NH = 2
HB = NH * 128
NCOL = 10 * HB + 2 * NH
DEPTH = 2
LAMBDA_INIT = [0.8 - 0.6 * float(np.exp(-0.3 * l)) for l in range(DEPTH)]


_uid = [0]


def uname(name):
    _uid[0] += 1
    return f"{name}_u{_uid[0]}"


class Tok:
    __slots__ = ("name", "w", "r")

    def __init__(self, name=""):
        self.name = name
        self.w = None
        self.r = {}


class DSem:
    def __init__(self, prog, name, q="sp"):
        self.q = q
        if prog.free_d[q]:
            self.key, self.cnt = prog.free_d[q].pop()
        else:
            self.key = ("d", name)
            prog.sems[self.key] = prog._alloc_sem(name)
            self.cnt = 0
        prog.dsems.append(self)


class Prog:
    def __init__(self, nc, stack):
        self.nc = nc
        self.stack = stack
        self.sems = {}
        self.dsems = []
        self.free_d = {"sp": [], "pool": []}
        self.dcur = {}
        self.phase = 0
        self.esets = {}
        self.rec = None
        self.q = {e: [] for e in ENGS}
        self._new_engine_sems()

    def _alloc_sem(self, name):
        return self.stack.enter_context(self.nc.semaphore(name))

    NSETS = 4

    def _new_engine_sems(self):
        si = self.phase % self.NSETS
        if si not in self.esets:
            self.esets[si] = {}
            for e in ENGS:
                if e == "sp":
                    continue
                k = ("e", e, si)
                self.sems[k] = self._alloc_sem(f"s_{e}_{si}")
                self.esets[si][e] = [k, 0]
        self.eset = self.esets[si]
        self.ekey = {e: v[0] for e, v in self.eset.items()}
        self.cnt = {e: v[1] for e, v in self.eset.items()}
        self.cnt0 = dict(self.cnt)
        self.known = {e: {} for e in ENGS}

    def _wait(self, eng, ev):
        if ev is None:
            return
        k, v = ev
        if eng == "pe" and k == self.ekey.get("pe"):
            return
        if k[0] == "d":
            v = max(v, self.dcur.get(k, 0))
        if self.known[eng].get(k, 0) >= v:
            return
        self.known[eng][k] = v
        self.q[eng].append(("w", k, v))

    def _deps(self, eng, reads, writes):
        for t in reads:
            self._wait(eng, t.w)
        for t in writes:
            self._wait(eng, t.w)
            for k, v in t.r.items():
                self._wait(eng, (k, v))

    def _record(self, ev, reads, writes):
        k, v = ev
        for t in reads:
            if t.r.get(k, 0) < v:
                t.r[k] = v
        for t in writes:
            t.w = ev
            t.r = {}

    def run_merged(self, streams):
        streams = [list(x) for x in streams if x]
        idx = [0] * len(streams)
        live = True
        while live:
            live = False
            for si, st_ in enumerate(streams):
                if idx[si] < len(st_):
                    it = st_[idx[si]]
                    idx[si] += 1
                    live = True
                    if it[0] == "op":
                        self.op(*it[1:])
                    else:
                        self.dma(*it[1:5], reads=it[5], writes=it[6], **it[7])

    def op(self, eng, fn, reads=(), writes=()):
        if self.rec is not None:
            self.rec.append(("op", eng, fn, list(reads), list(writes)))
            return None
        self._deps(eng, reads, writes)
        self.cnt[eng] += 1
        ev = (self.ekey[eng], self.cnt[eng])
        self.q[eng].append(("i", fn, ev[0], 1))
        self._record(ev, reads, writes)
        return ev

    def dma(self, queue, dsem, out, in_, reads=(), writes=(), **kw):
        if self.rec is not None:
            self.rec.append(("dma", queue, dsem, out, in_, list(reads), list(writes), kw))
            return None
        assert queue == dsem.q, (queue, dsem.q)
        self._deps(queue, reads, writes)
        dsem.cnt += 16
        self.dcur[dsem.key] = dsem.cnt
        ev = (dsem.key, dsem.cnt)
        self.q[queue].append(("i", lambda e: e.dma_start(out=out, in_=in_, **kw), dsem.key, 16))
        self._record(ev, reads, writes)
        return ev

    def _replay(self, engname, eng):
        for item in self.q[engname]:
            if item[0] == "w":
                eng.wait_ge(self.sems[item[1]], item[2])
            else:
                _, fn, k, inc = item
                fn(eng).then_inc(self.sems[k], inc)
        self.q[engname] = []

    def flush(self):
        finals = [(self.ekey[e], self.cnt[e]) for e in self.cnt if self.cnt[e] > self.cnt0[e]]
        finals += [(d.key, d.cnt) for d in self.dsems if d.cnt > 0]
        for e in ENGS:
            for ev in finals:
                self._wait(e, ev)
        with self.nc.Block() as block:
            @block.tensor
            def _(e):
                self._replay("pe", e)

            @block.scalar
            def _(e):
                self._replay("act", e)

            @block.vector
            def _(e):
                self._replay("dve", e)

            @block.gpsimd
            def _(e):
                self._replay("pool", e)

            @block.sync
            def _(e):
                self._replay("sp", e)
        for e in self.cnt:
            self.eset[e][1] = self.cnt[e]
        for d in self.dsems:
            self.free_d[d.q].append((d.key, d.cnt))
        self.dsems = []
        old = self.known
        self.phase += 1
        self._new_engine_sems()
        for e in ENGS:
            for k, v in old[e].items():
                self.known[e][k] = v


class Slots:
    def __init__(self, P, st, nc, name, n, shape, dtype, q="sp"):
        self.t = [st.enter_context(nc.sbuf_tensor(uname(name), shape, dtype)) for i in range(n)]
        self.tok = [Tok(f"{name}{i}") for i in range(n)]
        self.ds = [DSem(P, uname("d_" + name), q) for i in range(n)]
        self.i = 0
        self.n = n

    def next(self):
        i = self.i
        self.i = (i + 1) % self.n
        return self.t[i], self.tok[i], self.ds[i]


C_AQ, C_AK, C_AV, C_AZ = 0, HB, 2 * HB, 3 * HB
C_GQ, C_GK, C_GV, C_GZ = 4 * HB, 5 * HB, 6 * HB, 7 * HB
C_GBA = 8 * HB
C_RX, C_RZ = 8 * HB + 2 * NH, 9 * HB + 2 * NH
RG = [[0, 1], [2, 3], [4, 5], [6, 7]]


def make_dram(nc, S, dbg):
    kind = "ExternalOutput" if dbg else "Internal"
    T = {}

    def mk(name, shape, dt, k=None):
        T[name] = nc.dram_tensor(name, shape, dt, kind=k or kind).ap()
    XC = min(S, 1024)
    mk("aqk", [2 * NH, 128, S], BF16)
    mk("av", [S, HB], BF16)
    mk("az", [S, HB], BF16)
    mk("gqkv", [3 * NH, 128, S], F32)
    mk("gz", [S, HB], BF16)
    mk("gba", [2 * NH, S], F32)
    mk("rx", [NH, 128, S], F32)
    mk("rz", [NH, 128, S], BF16)
    mk("gcs", [2 * NH, S], F32)
    mk("beta", [2 * NH, S], F32)
    mk("yT", [3 * NH, 128, S], BF16)
    mk("yTg", [3 * NH, 2 * 128, S], BF16, "Internal")
    mk("xh1", [S, 512], F32)
    mk("xh2", [S, 512], F32)
    mk("x1g", [S // XC, 2 * XC, 512], F32, "Internal")
    mk("x2g", [S // XC, 2 * XC, 512], F32, "Internal")
    return T


def start_allgather(P, pairs, toks=None):
    for i, (src, dst) in enumerate(pairs):
        ds = DSem(P, uname("d_cc"), "pool")
        writes = [toks[i]] if toks else []
        P._deps("pool", [], writes)
        ds.cnt += 1
        P.dcur[ds.key] = ds.cnt
        P.q["pool"].append(("i", lambda e, src=src, dst=dst: e.collective_compute(
            "AllGather", ALU.bypass, replica_groups=RG, ins=[src], outs=[dst]), ds.key, 1))
        P._record((ds.key, ds.cnt), [], writes)


def allgather(nc, P, pairs):
    with ExitStack() as st:
        ds = DSem(P, uname("d_cc"), "pool")
        for src, dst in pairs:
            P._deps("pool", [], [])
            ds.cnt += 1
            P.dcur[ds.key] = ds.cnt
            P.q["pool"].append(("i", lambda e, src=src, dst=dst: e.collective_compute(
                "AllGather", ALU.bypass, replica_groups=RG, ins=[src], outs=[dst]), ds.key, 1))
        P.flush()


def phase_A(nc, P, l, S, x_src, I, T, pre=None, xg_toks=None):
    XC = min(S, 1024)
    NB = S // 512
    with ExitStack() as st:
        sb = lambda name, shape, dt: st.enter_context(nc.sbuf_tensor(uname(name), shape, dt))
        Wb = sb("Wb", [128, 8, NCOL], BF16)
        nw = sb("nw", [128, 8], F32)
        ident = sb("ident", [128, 128], BF16)
        xs = sb("xs", [128, 4, D], BF16)
        junk = sb("junk", [128, D], BF16)
        ss = sb("ss", [128, 4], F32)
        rstd = sb("rstd", [128, 4], F32)
        t_W, t_nw, t_id, t_xs, t_junk, t_ss, t_rstd = (Tok() for _ in range(7))
        d_c = DSem(P, uname("dA_c"))
        P.dma("sp", d_c, nw[:], I["norm_w"][l].rearrange("(k p) -> p k", p=128),
              writes=[t_nw], allow_slow_non_contiguous=True)
        P.dma("sp", d_c, ident[:], I["ident"], writes=[t_id])
        wst = Slots(P, st, nc, "wst", 2, [128, NCOL // 2], F32)
        for kc in range(8):
            for hf in range(2):
                t, tok, ds = wst.next()
                c0 = hf * (NCOL // 2)
                P.dma("sp", ds, t[:], I["w_in"][l, kc * 128:(kc + 1) * 128, c0:c0 + NCOL // 2], writes=[tok])
                eng = "dve" if hf == 0 else "pool"
                P.op(eng, lambda e, t=t, kc=kc, c0=c0: e.tensor_scalar(
                    out=Wb[:, kc, c0:c0 + NCOL // 2], in0=t[:], scalar1=nw[:, kc:kc + 1], scalar2=None,
                    op0=ALU.mult), reads=[tok, t_nw], writes=[t_W])
        if pre is not None:
            pre()
        xts = Slots(P, st, nc, "xt", 2, [128, 4, D], F32)
        hTs = [sb("hT", [128, 8, 512], BF16) for i in range(2)]
        t_hT = [Tok(), Tok()]
        ptr = [st.enter_context(nc.psum_tensor(uname("ptr"), [128, 1024], BF16)) for i in range(2)]
        t_ptr = [Tok(), Tok()]
        pmm = [st.enter_context(nc.psum_tensor(uname("pmm"), [128, 512], F32)) for i in range(6)]
        t_pmm = [Tok() for _ in range(6)]
        so_f = Slots(P, st, nc, "sof", 4, [128, 512], F32)
        so_b = Slots(P, st, nc, "sob", 4, [128, 512], BF16)
        pi = 0
        ev_i = 0
        xq = []

        def load_x(b):
            xt, t_xt, d_xt = xts.next()
            if x_src is not None:
                P.dma("sp", d_xt, xt[:], x_src[b * 512:(b + 1) * 512, :].rearrange("(j p) d -> p j d", p=128),
                      writes=[t_xt])
            else:
                tc_, off = (b * 512) // XC, (b * 512) % XC
                for r in range(2):
                    P.dma("sp", d_xt, xt[:, :, r * 512:(r + 1) * 512],
                          T["x1g"][tc_, r * XC + off:r * XC + off + 512, :].rearrange("(j p) d -> p j d", p=128),
                          reads=([xg_toks[tc_]] if xg_toks else []), writes=[t_xt])
            xq.append((xt, t_xt))
        load_x(0)
        for b in range(NB):
            xt, t_xt = xq.pop(0)
            if b + 1 < NB:
                load_x(b + 1)
            for j in range(4):
                P.op("act", lambda e, j=j, xt=xt: e.activation(out=junk[:], in_=xt[:, j, :], func=AF.Square,
                                                              accum_out=ss[:, j:j + 1]),
                     reads=[t_xt], writes=[t_junk, t_ss])
            P.op("dve", lambda e: e.tensor_scalar(out=rstd[:], in0=ss[:], scalar1=1.0 / D, scalar2=1e-6,
                                                  op0=ALU.mult, op1=ALU.add), reads=[t_ss], writes=[t_rstd])
            P.op("act", lambda e: e.activation(out=rstd[:], in_=rstd[:], func=AF.Sqrt), reads=[t_rstd], writes=[t_rstd])
            P.op("dve", lambda e: e.reciprocal(out=rstd[:], in_=rstd[:]), reads=[t_rstd], writes=[t_rstd])
            for j in range(4):
                if j % 2 == 0:
                    P.op("act", lambda e, j=j, xt=xt: e.activation(out=xs[:, j, :], in_=xt[:, j, :], func=AF.Copy,
                                                                  scale=rstd[:, j:j + 1]),
                         reads=[t_xt, t_rstd], writes=[t_xs])
                else:
                    P.op("dve", lambda e, j=j, xt=xt: e.tensor_scalar(out=xs[:, j, :], in0=xt[:, j, :],
                                                                     scalar1=rstd[:, j:j + 1], scalar2=None,
                                                                     op0=ALU.mult),
                         reads=[t_xt, t_rstd], writes=[t_xs])
            hT = hTs[b % 2]
            th = t_hT[b % 2]
            for kc in range(8):
                pt = ptr[(kc // 2) % 2]
                tp = t_ptr[(kc // 2) % 2]
                for j in range(4):
                    o = (kc % 2) * 512 + j * 128
                    P.op("pe", lambda e, pt=pt, o=o, j=j, kc=kc: e.transpose(
                        out=pt[:, o:o + 128], in_=xs[:, j, kc * 128:(kc + 1) * 128], identity=ident[:]),
                        reads=[t_xs, t_id], writes=[tp])
                eng = "dve" if (kc // 2) % 2 == 0 else "act"
                o = (kc % 2) * 512
                if eng == "dve":
                    P.op("dve", lambda e, pt=pt, o=o, kc=kc, hT=hT: e.tensor_copy(out=hT[:, kc, :], in_=pt[:, o:o + 512]),
                         reads=[tp], writes=[th])
                else:
                    P.op("act", lambda e, pt=pt, o=o, kc=kc, hT=hT: e.activation(out=hT[:, kc, :], in_=pt[:, o:o + 512],
                                                                               func=AF.Copy),
                         reads=[tp], writes=[th])
            tok0 = b * 512

            def emit(kind, c0, ncols, dst, func, outdt, j=None, hT=hT, th=th):
                nonlocal pi, ev_i
                ps = pmm[pi % 6]
                tps = t_pmm[pi % 6]
                pi += 1
                for kc in range(8):
                    if kind == "fm":
                        P.op("pe", lambda e, ps=ps, kc=kc: e.matmul(
                            out=ps[0:ncols, :], lhsT=Wb[:, kc, c0:c0 + ncols], rhs=hT[:, kc, :],
                            start=(kc == 0), stop=(kc == 7)), reads=[t_W, th], writes=[tps])
                    else:
                        P.op("pe", lambda e, ps=ps, kc=kc: e.matmul(
                            out=ps[:, 0:ncols], lhsT=hT[:, kc, j * 128:(j + 1) * 128], rhs=Wb[:, kc, c0:c0 + ncols],
                            start=(kc == 0), stop=(kc == 7)), reads=[t_W, th], writes=[tps])
                pool = so_f if outdt == F32 else so_b
                so, t_so, d_so = pool.next()
                src = ps[0:ncols, :] if kind == "fm" else ps[:, 0:ncols]
                dsto = so[0:ncols, :] if kind == "fm" else so[:, 0:ncols]
                if func is None and ev_i % 2 == 0:
                    P.op("dve", lambda e: e.tensor_copy(out=dsto, in_=src), reads=[tps], writes=[t_so])
                else:
                    P.op("act", lambda e: e.activation(out=dsto, in_=src, func=(func or AF.Copy)),
                         reads=[tps], writes=[t_so])
                ev_i += 1
                P.dma("sp", d_so, dst, dsto, reads=[t_so])

            for c in range(2 * NH):
                emit("fm", C_AQ + c * 128, 128, T["aqk"][c, :, tok0:tok0 + 512], None, BF16)
            for c in range(3 * NH):
                emit("fm", C_GQ + c * 128, 128, T["gqkv"][c, :, tok0:tok0 + 512], None, F32)
            emit("fm", C_GBA, 2 * NH, T["gba"][:, tok0:tok0 + 512], None, F32)
            for c in range(NH):
                emit("fm", C_RX + c * 128, 128, T["rx"][c, :, tok0:tok0 + 512], None, F32)
            for c in range(NH):
                emit("fm", C_RZ + c * 128, 128, T["rz"][c, :, tok0:tok0 + 512], AF.Silu, BF16)
            for j in range(4):
                r0 = tok0 + j * 128
                emit("tm", C_AV, HB, T["av"][r0:r0 + 128, :], None, BF16, j)
                emit("tm", C_AZ, HB, T["az"][r0:r0 + 128, :], AF.Silu, BF16, j)
                emit("tm", C_GZ, HB, T["gz"][r0:r0 + 128, :], AF.Silu, BF16, j)
        P.flush()


class Ctx:
    def __init__(self, nc, P, st):
        self.nc, self.P, self.st = nc, P, st

    def sb(self, name, shape, dt):
        return self.st.enter_context(self.nc.sbuf_tensor(uname(name), shape, dt))

    def ps(self, name, shape, dt=F32):
        return self.st.enter_context(self.nc.psum_tensor(uname(name), shape, dt))

    def slots(self, name, n, shape, dt, q="sp"):
        return Slots(self.P, self.st, self.nc, name, n, shape, dt, q)

    def dsem(self, name):
        return DSem(self.P, uname(name))


def phase_E(nc, P, l, S, x_src, x_dst, I, T):
    NB = S // 512
    NCH = 6 * NH
    with ExitStack() as st:
        C = Ctx(nc, P, st)
        Wo = C.sb("Wo", [128, NCH, 512], BF16)
        t_Wo = Tok()
        wst = C.slots("wost", 2, [128, 512], F32)
        for c in range(NCH):
            t, tok, ds = wst.next()
            P.dma("sp", ds, t[:], I["w_out"][l, c * 128:(c + 1) * 128, :], writes=[tok])
            eng = "dve" if c % 2 == 0 else "pool"
            P.op(eng, lambda e, t=t, c=c: e.tensor_copy(out=Wo[:, c, :], in_=t[:]), reads=[tok], writes=[t_Wo])
        ys = C.slots("ys", 2, [128, NCH, 512], BF16)
        xts = C.slots("xe", 2, [128, 4, 512], F32)
        xo = C.slots("xo", 3, [128, 512], F32)
        pm = [C.ps("pe", [128, 512]) for _ in range(4)]
        t_pm = [Tok() for _ in range(4)]
        pi = 0
        lq_ = []

        def load_E(b):
            y, t_y, d_y = ys.next()
            xt, t_xt, d_xt = xts.next()
            for r in range(2):
                P.dma("sp", d_y, y[:, r * 3 * NH:(r + 1) * 3 * NH, :],
                      T["yTg"][:, r * 128:(r + 1) * 128, b * 512:(b + 1) * 512].rearrange("c p t -> p c t"), writes=[t_y])
            P.dma("sp", d_xt, xt[:], x_src[b * 512:(b + 1) * 512, :].rearrange("(j p) d -> p j d", p=128), writes=[t_xt])
            lq_.append((y, t_y, xt, t_xt))
        load_E(0)
        for b in range(NB):
            y, t_y, xt, t_xt = lq_.pop(0)
            if b + 1 < NB:
                load_E(b + 1)
            for j in range(4):
                o, t_o, d_o = xo.next()
                ps, tps = pm[pi % 4], t_pm[pi % 4]
                pi += 1
                for c in range(NCH):
                    P.op("pe", lambda e, ps=ps, c=c, y=y, j=j: e.matmul(
                        out=ps[:], lhsT=y[:, c, j * 128:(j + 1) * 128], rhs=Wo[:, c, :],
                        start=(c == 0), stop=(c == NCH - 1)), reads=[t_y, t_Wo], writes=[tps])
                P.op("dve", lambda e, ps=ps, o=o, xt=xt, j=j: e.tensor_tensor(
                    out=o[:], in0=ps[:], in1=xt[:, j, :], op=ALU.add), reads=[tps, t_xt], writes=[t_o])
                r0 = b * 512 + j * 128
                P.dma("sp", d_o, x_dst[r0:r0 + 128, :], o[:], reads=[t_o])
        P.flush()


def phase_F(nc, P, S, I, T, out, pre=None, xg_toks=None):
    NB = S // 512
    XC = min(S, 1024)
    with ExitStack() as st:
        C = Ctx(nc, P, st)
        if pre is not None:
            pre()
        fnw = C.sb("fnw", [128, 512], F32)
        t_fnw = Tok()
        d_c = C.dsem("dF_c")
        P.dma("sp", d_c, fnw[:], I["final_norm_w"].partition_broadcast(128), writes=[t_fnw])
        xg = C.slots("xg", 2, [128, 4, 1024], F32)
        xm = C.slots("xm", 2, [128, 4, 512], F32)
        xo = C.slots("xoF", 2, [128, 4, 512], F32)
        junk = C.sb("junkF", [128, 1024], BF16)
        ss = C.sb("ssF", [128, 4], F32)
        t_junk, t_ss = Tok(), Tok()
        lq_ = []

        def load_F(b):
            g, t_g, d_g = xg.next()
            m, t_m, d_m = xm.next()
            tc_, off = (b * 512) // XC, (b * 512) % XC
            for r in range(2):
                P.dma("sp", d_g, g[:, :, r * 512:(r + 1) * 512],
                      T["x2g"][tc_, r * XC + off:r * XC + off + 512, :].rearrange("(j p) d -> p j d", p=128),
                      reads=([xg_toks[tc_]] if xg_toks else []), writes=[t_g])
            P.dma("sp", d_m, m[:], T["xh2"][b * 512:(b + 1) * 512, :].rearrange("(j p) d -> p j d", p=128), writes=[t_m])
            lq_.append((g, t_g, m, t_m))
        load_F(0)
        for b in range(NB):
            g, t_g, m, t_m = lq_.pop(0)
            if b + 1 < NB:
                load_F(b + 1)
            for j in range(4):
                P.op("act", lambda e, g=g, j=j: e.activation(out=junk[:], in_=g[:, j, :], func=AF.Square,
                                                            accum_out=ss[:, j:j + 1]), reads=[t_g], writes=[t_junk, t_ss])
            P.op("dve", lambda e: e.tensor_scalar(out=ss[:], in0=ss[:], scalar1=1.0 / D, scalar2=1e-6,
                                                  op0=ALU.mult, op1=ALU.add), reads=[t_ss], writes=[t_ss])
            P.op("act", lambda e: e.activation(out=ss[:], in_=ss[:], func=AF.Sqrt), reads=[t_ss], writes=[t_ss])
            P.op("dve", lambda e: e.reciprocal(out=ss[:], in_=ss[:]), reads=[t_ss], writes=[t_ss])
            o, t_o, d_o = xo.next()
            for j in range(4):
                P.op("dve", lambda e, o=o, m=m, j=j: e.scalar_tensor_tensor(
                    out=o[:, j, :], in0=m[:, j, :], scalar=ss[:, j:j + 1], in1=fnw[:], op0=ALU.mult, op1=ALU.mult),
                    reads=[t_m, t_ss, t_fnw], writes=[t_o])
            P.dma("sp", d_o, out[b * 512:(b + 1) * 512, :].rearrange("(j p) d -> p j d", p=128), o[:], reads=[t_o])
        P.flush()


def phase_C(nc, P, l, S, I, T, pre=None):
    NB = S // 512
    with ExitStack() as st:
        C = Ctx(nc, P, st)
        if pre is not None:
            pre()
        d_c = C.dsem("dC_c")
        cw = C.sb("cw", [128, NH, 4], F32)
        cb = C.sb("cb", [128, NH], F32)
        gb = C.sb("gb", [128, 2, NH], F32)
        lp = C.sb("lp", [128, NH], F32)
        c8 = C.sb("c8", [128, NH], F32)
        c16 = C.sb("c16", [128, NH], F32)
        gwf = C.sb("gwf", [128, 2 * NH, 128], F32)
        gw = C.sb("gw", [128, 2 * NH, 128], BF16)
        t_c = Tok()
        for n in range(NH):
            P.dma("sp", d_c, cw[:, n, :], I["lru_conv_w"][l][:, n * 128:(n + 1) * 128].rearrange("j p -> p j"),
                  writes=[t_c], allow_slow_non_contiguous=True)
        P.dma("sp", d_c, cb[:], I["lru_conv_b"][l].rearrange("(n p) -> p n", p=128), writes=[t_c],
              allow_slow_non_contiguous=True)
        for g in range(2):
            P.dma("sp", d_c, gb[:, g, :], I["lru_gate_b"][l, g].rearrange("(n p) -> p n", p=128), writes=[t_c],
                  allow_slow_non_contiguous=True)
        P.dma("sp", d_c, lp[:], I["lru_log_param"][l].rearrange("(n p) -> p n", p=128), writes=[t_c],
              allow_slow_non_contiguous=True)
        P.dma("sp", d_c, gwf[:], I["lru_gate_w"][l].rearrange("g n d e -> d (g n) e"), writes=[t_c])
        P.op("dve", lambda e: e.tensor_copy(out=gw[:], in_=gwf[:]), reads=[t_c], writes=[t_c])
        P.op("act", lambda e: e.activation(out=c8[:], in_=lp[:], func=AF.Exp, scale=-1.0), reads=[t_c], writes=[t_c])
        P.op("act", lambda e: e.activation(out=c8[:], in_=c8[:], func=AF.Ln, bias=1.0), reads=[t_c], writes=[t_c])
        P.op("dve", lambda e: e.tensor_scalar(out=c16[:], in0=c8[:], scalar1=-16.0, scalar2=None, op0=ALU.mult),
             reads=[t_c], writes=[t_c])
        P.op("dve", lambda e: e.tensor_scalar(out=c8[:], in0=c8[:], scalar1=-8.0, scalar2=None, op0=ALU.mult),
             reads=[t_c], writes=[t_c])
        xins = [C.slots("xin", 2, [128, 515], F32) for _ in range(NH)]
        zins = [C.slots("zin", 2, [128, 512], BF16) for _ in range(NH)]
        yos = [C.slots("yo", 2, [128, 512], BF16) for _ in range(NH)]
        NW = NH
        W = []
        for i in range(NW):
            W.append(dict(
                xc=C.sb("xc", [128, 512], F32), xcb=C.sb("xcb", [128, 512], BF16),
                it=C.sb("it", [128, 512], F32), rt=C.sb("rt", [128, 512], F32),
                at=C.sb("at", [128, 512], F32), a2=C.sb("a2", [128, 512], F32),
                tok=Tok()))
        hs = [[C.sb("h", [128, 512], F32) for _ in range(2)] for n in range(NH)]
        t_h = [[Tok(), Tok()] for n in range(NH)]
        pg = [C.ps("pg", [128, 512]) for _ in range(4)]
        t_pg = [Tok() for _ in range(4)]
        lcs = [[] for _ in range(NH)]

        def load_C(b, n):
            t0 = b * 512
            lc_ = lcs[n]
            x, t_x, d_x = xins[n].next()
            z, t_z, d_z = zins[n].next()
            if b == 0:
                P.op("pool", lambda e, x=x: e.memset(x[:, 0:3], 0.0), writes=[t_x])
                P.dma("sp", d_x, x[:, 3:515], T["rx"][n, :, 0:512], writes=[t_x])
            else:
                P.dma("sp", d_x, x[:], T["rx"][n, :, t0 - 3:t0 + 512], writes=[t_x])
            P.dma("sp", d_z, z[:], T["rz"][n, :, t0:t0 + 512], writes=[t_z])
            lc_.append((x, t_x, z, t_z))
        def stream_C(n):
            load_C(0, n)
            for b in range(NB):
                t0 = b * 512
                x, t_x, z, t_z = lcs[n].pop(0)
                if b + 1 < NB:
                    load_C(b + 1, n)
                it_ = n
                w = W[n]
                tw = w["tok"]
                xc = w["xc"]
                P.op("act", lambda e, x=x, xc=xc, n=n: e.activation(
                    out=xc[:], in_=x[:, 3:515], func=AF.Identity, scale=cw[:, n, 3:4], bias=cb[:, n:n + 1]),
                    reads=[t_x, t_c], writes=[tw])
                for k in range(1, 4):
                    P.op("dve", lambda e, x=x, xc=xc, n=n, k=k: e.scalar_tensor_tensor(
                        out=xc[:], in0=x[:, 3 - k:515 - k], scalar=cw[:, n, 3 - k:4 - k], in1=xc[:],
                        op0=ALU.mult, op1=ALU.add), reads=[t_x, t_c], writes=[tw])
                P.op("pool", lambda e, w=w: e.tensor_copy(out=w["xcb"][:], in_=w["xc"][:]), reads=[tw], writes=[tw])
                p_i, tp_i = pg[(2 * it_) % 4], t_pg[(2 * it_) % 4]
                p_r, tp_r = pg[(2 * it_ + 1) % 4], t_pg[(2 * it_ + 1) % 4]
                P.op("pe", lambda e, w=w, p_i=p_i, n=n: e.matmul(out=p_i[:], lhsT=gw[:, n, :], rhs=w["xcb"][:],
                                                                start=True, stop=True), reads=[tw, t_c], writes=[tp_i])
                P.op("pe", lambda e, w=w, p_r=p_r, n=n: e.matmul(out=p_r[:], lhsT=gw[:, NH + n, :], rhs=w["xcb"][:],
                                                                start=True, stop=True), reads=[tw, t_c], writes=[tp_r])
                P.op("act", lambda e, w=w, p_i=p_i, n=n: e.activation(out=w["it"][:], in_=p_i[:], func=AF.Sigmoid,
                                                                     bias=gb[:, 0, n:n + 1]), reads=[tp_i, t_c], writes=[tw])
                P.op("act", lambda e, w=w, p_r=p_r, n=n: e.activation(out=w["rt"][:], in_=p_r[:], func=AF.Sigmoid,
                                                                     bias=gb[:, 1, n:n + 1]), reads=[tp_r, t_c], writes=[tw])
                P.op("act", lambda e, w=w, n=n: e.activation(out=w["at"][:], in_=w["rt"][:], func=AF.Exp,
                                                            scale=c8[:, n:n + 1]), reads=[tw, t_c], writes=[tw])
                P.op("act", lambda e, w=w, n=n: e.activation(out=w["a2"][:], in_=w["rt"][:], func=AF.Exp,
                                                            scale=c16[:, n:n + 1]), reads=[tw, t_c], writes=[tw])
                P.op("dve", lambda e, w=w: e.tensor_scalar(out=w["a2"][:], in0=w["a2"][:], scalar1=-1.0, scalar2=1.0,
                                                           op0=ALU.mult, op1=ALU.add), reads=[tw], writes=[tw])
                P.op("dve", lambda e, w=w: e.tensor_scalar(out=w["a2"][:], in0=w["a2"][:], scalar1=1e-30, scalar2=None,
                                                           op0=ALU.max), reads=[tw], writes=[tw])
                P.op("act", lambda e, w=w: e.activation(out=w["a2"][:], in_=w["a2"][:], func=AF.Sqrt), reads=[tw], writes=[tw])
                P.op("pool", lambda e, w=w: e.tensor_tensor(out=w["it"][:], in0=w["it"][:], in1=w["xc"][:], op=ALU.mult),
                     reads=[tw], writes=[tw])
                P.op("pool", lambda e, w=w: e.tensor_tensor(out=w["it"][:], in0=w["it"][:], in1=w["a2"][:], op=ALU.mult),
                     reads=[tw], writes=[tw])
                h, th = hs[n][b % 2], t_h[n][b % 2]
                hp, thp = hs[n][(b + 1) % 2], t_h[n][(b + 1) % 2]
                init = 0.0 if b == 0 else hp[:, 511:512]
                P.op("dve", lambda e, w=w, h=h, init=init: e.tensor_tensor_scan(
                    out=h[:], data0=w["at"][:], data1=w["it"][:], initial=init, op0=ALU.mult, op1=ALU.add),
                    reads=[tw] + ([thp] if b > 0 else []), writes=[th])
                y, t_y, d_y = yos[n].next()
                P.op("pool", lambda e, h=h, z=z, y=y: e.tensor_tensor(out=y[:], in0=h[:], in1=z[:], op=ALU.mult),
                     reads=[th, t_z], writes=[t_y])
                P.dma("sp", d_y, T["yT"][2 * NH + n, :, t0:t0 + 512], y[:], reads=[t_y])

        streams = []
        for n in range(NH):
            P.rec = []
            stream_C(n)
            streams.append(P.rec)
        P.rec = None
        P.run_merged(streams)
        P.flush()


def phase_B(nc, P, l, S, I, T):
    NB = S // 512
    NT = S // 128
    LOOK = 3
    with ExitStack() as st:
        C = Ctx(nc, P, st)
        d_c = C.dsem("dB_c")
        t_c = Tok()
        lq = C.sb("lq", [128, 256], F32)
        lj = C.sb("lj", [128, 64], F32)
        lam = C.sb("lam", [128, 4], F32)
        sw = C.sb("sw", [128, 128], F32)
        ident = C.sb("identB", [128, 128], BF16)
        zl = C.sb("zl", [1, 128], BF16)
        zr = C.sb("zr", [1, 512], BF16)
        P.dma("sp", d_c, lq[:], I["attn_lambda"][l:l + 1, :].partition_broadcast(128), writes=[t_c])
        P.dma("sp", d_c, sw[:], I["attn_subln_w"][l:l + 1, :].partition_broadcast(128), writes=[t_c])
        P.dma("sp", d_c, ident[:], I["ident"], writes=[t_c])
        P.op("pool", lambda e: e.memset(zl[:], 0.0), writes=[t_c])
        P.op("pool", lambda e: e.memset(zr[:], 0.0), writes=[t_c])
        for k in range(2):
            P.op("dve", lambda e, k=k: e.scalar_tensor_tensor(
                out=lj[:], in0=lq[:, 128 * k:128 * k + 64], scalar=1.0, in1=lq[:, 128 * k + 64:128 * k + 128],
                op0=ALU.mult, op1=ALU.mult, accum_out=lam[:, k:k + 1]), reads=[t_c], writes=[t_c])
        P.op("act", lambda e: e.activation(out=lam[:, 0:2], in_=lam[:, 0:2], func=AF.Exp), reads=[t_c], writes=[t_c])
        P.op("dve", lambda e: e.tensor_tensor(out=lam[:, 2:3], in0=lam[:, 1:2], in1=lam[:, 0:1], op=ALU.subtract),
             reads=[t_c], writes=[t_c])
        P.op("dve", lambda e: e.tensor_scalar(out=lam[:, 3:4], in0=lam[:, 2:3], scalar1=-LAMBDA_INIT[l], scalar2=None,
                                              op0=ALU.add), reads=[t_c], writes=[t_c])
        P.op("dve", lambda e: e.tensor_scalar(out=sw[:], in0=sw[:], scalar1=1.0 - LAMBDA_INIT[l], scalar2=None,
                                              op0=ALU.mult), reads=[t_c], writes=[t_c])
        neglam = lam[:, 3:4]
        kTs = C.slots("kT", 2, [128, S], BF16)
        Vxs = C.slots("Vx", 2, [128, NT, 130], BF16)
        for i in range(2):
            P.op("pool", lambda e, i=i: e.memset(Vxs.t[i][:, :, 128:130], 1.0), writes=[Vxs.tok[i]])
        qzs = [C.slots("qz0", 2, [128, 512], BF16), C.slots("qz1", 2, [128, 512], BF16)]
        for i in range(2):
            P.op("pool", lambda e, i=i: e.memset(qzs[0].t[i][64:128, :], 0.0), writes=[qzs[0].tok[i]])
            P.op("pool", lambda e, i=i: e.memset(qzs[1].t[i][0:64, :], 0.0), writes=[qzs[1].tok[i]])
        azs = C.slots("azb", 2, [128, 4, 128], BF16)
        pTs = C.slots("pT", 6, [128, 512], BF16)
        yTs = C.slots("yTs", 2, [128, 512], BF16)
        acc = [C.ps("acc", [128, 512]) for _ in range(4)]
        t_acc = [Tok() for _ in range(4)]
        pq = [C.ps("pq", [128, 512]) for _ in range(3)]
        t_pq = [Tok() for _ in range(3)]
        ptr = C.ps("ptrB", [128, 1024], BF16)
        t_ptr = Tok()
        accs = C.sb("accs", [128, 4, 512], F32)
        o = C.sb("oB", [128, 4, 128], F32)
        ybs = [C.sb("ybB", [128, 4, 128], BF16) for _ in range(2)]
        t_ybs = [Tok(), Tok()]
        rs = C.sb("rsB", [128, 8], F32)
        ssq = C.sb("ssB", [128, 4], F32)
        junk = C.sb("junkB", [128, 128], BF16)
        t_accs, t_o, t_rs, t_ss, t_junk = (Tok() for _ in range(5))

        heads = {}

        def load_head(h):
            kT, t_kT, d_kT = kTs.next()
            Vx, t_V, d_V = Vxs.next()
            P.dma("sp", d_kT, kT[:], T["aqk"][NH + h], writes=[t_kT])
            P.dma("sp", d_V, Vx[:, :, 0:128], T["av"][:, h * 128:(h + 1) * 128].rearrange("(t p) d -> p t d", p=128),
                  writes=[t_V])
            heads[h] = (kT, t_kT, Vx, t_V)

        blocks = {}

        def load_block(h, i):
            q0 = i * 512
            qz = []
            for m in range(2):
                t, tok, ds = qzs[m].next()
                P.dma("sp", ds, t[64 * m:64 * m + 64, :], T["aqk"][h, 64 * m:64 * m + 64, q0:q0 + 512], writes=[tok])
                qz.append((t, tok))
            az, t_az, d_az = azs.next()
            P.dma("sp", d_az, az[:], T["az"][q0:q0 + 512, h * 128:(h + 1) * 128].rearrange("(s p) d -> p s d", p=128),
                  writes=[t_az])
            blocks[(h, i)] = (qz, az, t_az)

        steps = []
        for h in range(NH):
            for i in range(NB):
                nj = 4 * i + 4
                for j in range(nj):
                    for m in range(2):
                        steps.append((h, i, j, m, j == 0 and m == 0, j == nj - 1 and m == 1))
        order = [(h, i) for h in range(NH) for i in range(NB)]
        load_head(0)
        load_block(0, 0)
        if len(order) > 1:
            load_block(*order[1])
        pend = {}
        deferred = []
        ep_i = [0]

        def front(k):
            h, i, j, m, first, last = steps[k]
            kT, t_kT, Vx, t_V = heads[h]
            qz, az, t_az = blocks[(h, i)]
            r = j - 4 * i
            c0 = 128 * r if r > 0 else 0
            ps, tps = pq[k % 3], t_pq[k % 3]
            qt, t_q = qz[m]
            P.op("pe", lambda e: e.matmul(out=ps[:, c0:512], lhsT=kT[:, j * 128:(j + 1) * 128], rhs=qt[:, c0:512],
                                          start=True, stop=True), reads=[t_kT, t_q], writes=[tps])
            pt, t_pt, _ = pTs.next()
            P.op("act", lambda e: e.activation(out=pt[:, c0:512], in_=ps[:, c0:512], func=AF.Exp, scale=0.125),
                 reads=[tps], writes=[t_pt])
            if r >= 0:
                P.op("pool", lambda e: e.memset(pt[64:128, 128 * r:128 * r + 64], 0.0), writes=[t_pt])
            pend[k] = (pt, t_pt)

        def epilogue1(h, i):
            qz, az, t_az = blocks[(h, i)]
            yb, t_yb = ybs[ep_i[0] % 2], t_ybs[ep_i[0] % 2]
            ep_i[0] += 1
            for a in range(4):
                if a % 2 == 0:
                    P.op("dve", lambda e, a=a: e.tensor_copy(out=accs[:, a, 0:385], in_=acc[a][:, 0:385]),
                         reads=[t_acc[a]], writes=[t_accs])
                else:
                    P.op("act", lambda e, a=a: e.activation(out=accs[:, a, 0:385], in_=acc[a][:, 0:385], func=AF.Copy),
                         reads=[t_acc[a]], writes=[t_accs])
            for s_ in range(4):
                co = (s_ % 2) * 256
                for m in range(2):
                    P.op("dve", lambda e, s_=s_, m=m, co=co: e.reciprocal(
                        out=rs[:, 4 * m + s_:4 * m + s_ + 1], in_=accs[:, 2 * m + s_ // 2, co + 128:co + 129]),
                        reads=[t_accs], writes=[t_rs])
            P.op("dve", lambda e: e.tensor_scalar(out=rs[:, 4:8], in0=rs[:, 4:8], scalar1=neglam, scalar2=None,
                                                  op0=ALU.mult), reads=[t_rs, t_c], writes=[t_rs])
            for s_ in range(4):
                co = (s_ % 2) * 256
                P.op("dve", lambda e, s_=s_, co=co: e.tensor_scalar(
                    out=o[:, s_, :], in0=accs[:, s_ // 2, co:co + 128], scalar1=rs[:, s_:s_ + 1], scalar2=None,
                    op0=ALU.mult), reads=[t_accs, t_rs], writes=[t_o])
                P.op("dve", lambda e, s_=s_, co=co: e.scalar_tensor_tensor(
                    out=o[:, s_, :], in0=accs[:, 2 + s_ // 2, co:co + 128], scalar=rs[:, 4 + s_:5 + s_], in1=o[:, s_, :],
                    op0=ALU.mult, op1=ALU.add), reads=[t_accs, t_rs], writes=[t_o])
                P.op("dve", lambda e, s_=s_: e.scalar_tensor_tensor(
                    out=junk[:], in0=o[:, s_, :], scalar=1.0, in1=o[:, s_, :], op0=ALU.mult, op1=ALU.mult,
                    accum_out=ssq[:, s_:s_ + 1]), reads=[t_o], writes=[t_junk, t_ss])
            P.op("dve", lambda e: e.tensor_scalar(out=ssq[:], in0=ssq[:], scalar1=1.0 / 128, scalar2=1e-5,
                                                  op0=ALU.mult, op1=ALU.add), reads=[t_ss], writes=[t_ss])
            P.op("act", lambda e: e.activation(out=ssq[:], in_=ssq[:], func=AF.Sqrt), reads=[t_ss], writes=[t_ss])
            P.op("dve", lambda e: e.reciprocal(out=ssq[:], in_=ssq[:]), reads=[t_ss], writes=[t_ss])
            for s_ in range(4):
                P.op("dve", lambda e, s_=s_: e.scalar_tensor_tensor(
                    out=o[:, s_, :], in0=o[:, s_, :], scalar=ssq[:, s_:s_ + 1], in1=sw[:],
                    op0=ALU.mult, op1=ALU.mult), reads=[t_o, t_ss, t_c], writes=[t_o])
                P.op("pool", lambda e, s_=s_: e.tensor_tensor(out=yb[:, s_, :], in0=o[:, s_, :], in1=az[:, s_, :],
                                                             op=ALU.mult), reads=[t_o, t_az], writes=[t_yb])

            def epilogue2():
                for s_ in range(4):
                    P.op("pe", lambda e, s_=s_: e.transpose(out=ptr[:, s_ * 128:(s_ + 1) * 128], in_=yb[:, s_, :],
                                                          identity=ident[:]), reads=[t_yb, t_c], writes=[t_ptr])
                ys, t_ys, d_ys = yTs.next()
                P.op("act", lambda e: e.activation(out=ys[:], in_=ptr[:, 0:512], func=AF.Copy),
                     reads=[t_ptr], writes=[t_ys])
                P.dma("sp", d_ys, T["yT"][h, :, i * 512:(i + 1) * 512], ys[:], reads=[t_ys])
            return epilogue2

        def back(k):
            h, i, j, m, first, last = steps[k]
            kT, t_kT, Vx, t_V = heads[h]
            r = j - 4 * i
            pt, t_pt = pend.pop(k)
            if first:
                for a in range(4):
                    P.op("pe", lambda e, a=a: e.matmul(out=acc[a][:], lhsT=zl[0:1, :], rhs=zr[0:1, :], start=True, stop=True),
                         reads=[t_c], writes=[t_acc[a]])
            for s_ in range(max(r, 0), 4):
                ai = m * 2 + s_ // 2
                co = (s_ % 2) * 256
                P.op("pe", lambda e, ai=ai, co=co, s_=s_: e.matmul(
                    out=acc[ai][:, co:co + 129], lhsT=pt[:, s_ * 128:(s_ + 1) * 128], rhs=Vx[:, j, 0:129],
                    start=False, stop=False, skip_group_check=True), reads=[t_pt, t_V], writes=[t_acc[ai]])
            if last:
                e2 = epilogue1(h, i)
                deferred.append((k + 12, e2))
                oi = order.index((h, i))
                if oi + 2 < len(order):
                    nh, ni = order[oi + 2]
                    load_block(nh, ni)
                if i == 0 and h + 1 < NH:
                    load_head(h + 1)

        n = len(steps)
        for k in range(n + LOOK):
            if k < n:
                front(k)
            if k >= LOOK:
                back(k - LOOK)
            while deferred and deferred[0][0] <= k:
                deferred.pop(0)[1]()
        for _, fn in deferred:
            fn()
        P.flush()


def phase_D(nc, P, l, S, I, T, pre=None):
    NB = S // 512
    GB = min(S, 2048)
    with ExitStack() as st:
        C = Ctx(nc, P, st)
        d_c = C.dsem("dD0_c")
        t_c = Tok()
        par = C.sb("par", [2 * NH, 4], F32)
        rmask = C.sb("rmask", [2 * NH, GB], F32)
        P.op("pool", lambda e: e.memset(par[:], 0.0), writes=[t_c])
        P.dma("sp", d_c, par[NH:2 * NH, 0:1], I["gdn_a_log"][l].rearrange("(p o) -> p o", o=1), writes=[t_c])
        P.dma("sp", d_c, par[NH:2 * NH, 1:2], I["gdn_dt_bias"][l].rearrange("(p o) -> p o", o=1), writes=[t_c])
        P.dma("sp", d_c, rmask[:], I["rmask"][0:2 * NH, 0:GB], writes=[t_c])
        P.op("act", lambda e: e.activation(out=par[:, 2:3], in_=par[:, 0:1], func=AF.Exp), reads=[t_c], writes=[t_c])
        P.op("dve", lambda e: e.tensor_scalar(out=par[:, 2:3], in0=par[:, 2:3], scalar1=-1.0, scalar2=None, op0=ALU.mult),
             reads=[t_c], writes=[t_c])
        bas = C.slots("ba", 2, [2 * NH, GB], F32)
        sg = C.slots("sg", 2, [2 * NH, GB], F32)
        gg = C.slots("gg", 2, [2 * NH, GB], F32)
        gc = C.slots("gc", 2, [2 * NH, GB], F32)
        for b in range(S // GB):
            c0 = b * GB
            ba, t_ba, d_ba = bas.next()
            sgt, t_sg, d_sg = sg.next()
            g, t_g, _ = gg.next()
            gct, t_gc, d_gc = gc.next()
            P.dma("sp", d_ba, ba[:], T["gba"][:, c0:c0 + GB], writes=[t_ba])
            P.op("act", lambda e, ba=ba, sgt=sgt: e.activation(out=sgt[:], in_=ba[:], func=AF.Sigmoid), reads=[t_ba], writes=[t_sg])
            P.dma("sp", d_sg, T["beta"][:, c0:c0 + GB], sgt[:], reads=[t_sg])
            P.op("act", lambda e, ba=ba, g=g: e.activation(out=g[:], in_=ba[:], func=AF.Exp, bias=par[:, 1:2]),
                 reads=[t_ba, t_c], writes=[t_g])
            P.op("act", lambda e, g=g: e.activation(out=g[:], in_=g[:], func=AF.Ln, bias=1.0), reads=[t_g], writes=[t_g])
            P.op("dve", lambda e, g=g: e.tensor_scalar(out=g[:], in0=g[:], scalar1=par[:, 2:3], scalar2=None, op0=ALU.mult),
                 reads=[t_g, t_c], writes=[t_g])
            P.op("dve", lambda e, g=g, gct=gct: e.tensor_tensor_scan(out=gct[:], data0=rmask[:], data1=g[:], initial=0.0,
                                                                    op0=ALU.mult, op1=ALU.add), reads=[t_g, t_c], writes=[t_gc])
            P.dma("sp", d_gc, T["gcs"][:, c0:c0 + GB], gct[:], reads=[t_gc])
        P.flush()
    with ExitStack() as st:
        C = Ctx(nc, P, st)
        if pre is not None:
            pre()
        d_c = C.dsem("dD_c")
        t_c = Tok()
        gcw = C.sb("gcw", [128, 3 * NH, 4], F32)
        gnw = C.sb("gnw", [128, 128], F32)
        onesf = C.sb("onesf", [128, 128], F32)
        identf = C.sb("identf", [128, 128], F32)
        ident = C.sb("identD", [128, 128], BF16)
        mI = C.sb("mI", [128, 512], F32)
        mS = C.sb("mS", [128, 512], F32)
        for c in range(3 * NH):
            P.dma("sp", d_c, gcw[:, c, :], I["gdn_conv_w"][l][:, c * 128:(c + 1) * 128].rearrange("j p -> p j"),
                  writes=[t_c], allow_slow_non_contiguous=True)
        P.dma("sp", d_c, gnw[:], I["gdn_norm_w"][l:l + 1, :].partition_broadcast(128), writes=[t_c])
        P.dma("sp", d_c, onesf[:], I["onesf"], writes=[t_c])
        P.dma("sp", d_c, identf[:], I["identf"], writes=[t_c])
        P.dma("sp", d_c, ident[:], I["ident"], writes=[t_c])
        P.dma("sp", d_c, mI[:], I["masks"][0], writes=[t_c])
        P.dma("sp", d_c, mS[:], I["masks"][1], writes=[t_c])
        p1 = C.ps("p1", [128, 512]); pS = C.ps("pS", [128, 512]); po = C.ps("po", [128, 512])
        ptr = C.ps("ptrD", [128, 1024], BF16)
        t_p1 = [Tok() for _ in range(4)]; t_pS = [Tok() for _ in range(4)]; t_po = [Tok() for _ in range(4)]
        t_ptr = [Tok() for _ in range(4)]
        pp = [C.ps("pp", [128, 512]) for _ in range(4)]
        t_pp = [Tok() for _ in range(4)]
        R, H, WkH, tWH = [], [], [], []
        for h in range(NH):
            R.append(dict(
                S=C.sb("Sst", [128, 128], F32), tS=Tok(), vn=C.sb("vn", [128, 4, 128], BF16), tvn=Tok(),
                y=C.sb("yD", [128, 128], F32), yb=C.sb("ybD", [128, 128], BF16), ss=C.sb("ssD", [128, 1], F32),
                ty=Tok(), junk=C.sb("junkD", [128, 128], BF16)))
            P.op("pool", lambda e, h=h: e.memset(R[h]["S"][:], 0.0), writes=[R[h]["tS"]])
            H.append([dict(
                UW=C.sb("UW", [128, 4, 256], F32), wT=C.sb("wT", [128, 512], F32), qg=C.sb("qg", [128, 512], F32),
                AT=C.sb("AT", [128, 4, 128], BF16), kd=C.sb("kd", [128, 4, 128], BF16),
                egl=C.sb("egl", [128, 8], F32), tD=Tok()) for _ in range(2)])
            WkH.append(dict(
                xs=[C.sb("xsD", [128, 512], F32) for _ in range(3)], sq=C.sb("sqD", [128, 512], F32),
                rn=C.sb("rnD", [128, 512], F32), kTb=C.sb("kTb", [128, 512], BF16), qTb=C.sb("qTb", [128, 512], BF16),
                egb=C.sb("egb", [128, 512], F32), dec=C.sb("dec", [128, 512], F32), dI=C.sb("dI", [128, 512], F32),
                dS=C.sb("dSm", [128, 512], F32), A=[C.sb("Am", [128, 512], F32) for _ in range(2)],
                B=[C.sb("Bm", [128, 512], F32) for _ in range(2)], X=C.sb("Xm", [128, 4, 256], F32),
                sd=C.sb("sd", [128, 16], F32)))
            tWH.append({k: Tok() for k in ("xs0", "xs1", "xs2", "sq", "rn", "kTb", "qTb", "egb", "dec", "dI", "dS",
                                           "A0", "A1", "B0", "B1", "X", "sd")})
        gzs = C.slots("gzb", 4, [128, 4, 128], BF16)
        yTs = C.slots("yTsD", 4, [128, 512], BF16)
        xin = C.slots("xinD", 6, [128, 515], F32)
        scs = C.slots("sc", 4, [128, 24], F32)
        gbrs = C.slots("gbr", 4, [128, 512], F32)
        ppi = [0, 0]

        def prep(b, h):
            t0 = b * 512
            Hh = H[h][b % 2]
            Wk = WkH[h]
            tW = tWH[h]

            def bank():
                i = 2 * h + ppi[h] % 2
                ppi[h] += 1
                return pp[i], t_pp[i]
            sc, t_sc, d_sc = scs.next()
            gbr, t_gbr, d_gbr = gbrs.next()
            grow = T["gcs"][NH + h:NH + h + 1, :]
            brow = T["beta"][h:h + 1, :]
            P.dma("sp", d_gbr, gbr[:], grow[:, t0:t0 + 512].partition_broadcast(128), writes=[t_gbr])
            P.dma("sp", d_sc, sc[:, 0:4], T["gcs"][NH + h, t0:t0 + 512].rearrange("(t p) -> p t", p=128), writes=[t_sc],
                  allow_slow_non_contiguous=True)
            P.dma("sp", d_sc, sc[:, 4:8], T["beta"][h, t0:t0 + 512].rearrange("(t p) -> p t", p=128), writes=[t_sc],
                  allow_slow_non_contiguous=True)
            sd = Wk["sd"]
            P.op("act", lambda e, sc=sc: e.activation(out=sd[:, 0:4], in_=sc[:, 0:4], func=AF.Exp),
                 reads=[t_sc], writes=[tW["sd"]])
            P.op("dve", lambda e, sc=sc: e.tensor_scalar(out=sd[:, 4:8], in0=sc[:, 4:8], scalar1=-1.0, scalar2=None,
                                                        op0=ALU.mult), reads=[t_sc], writes=[tW["sd"]])
            for hf in range(2):
                P.op("dve", lambda e, sc=sc, gbr=gbr, hf=hf: e.tensor_tensor(
                    out=sd[hf * 64:(hf + 1) * 64, 8:12], in0=gbr[hf * 64:(hf + 1) * 64, hf * 64 + 63:512:128],
                    in1=sc[hf * 64:(hf + 1) * 64, 0:4], op=ALU.subtract), reads=[t_sc, t_gbr], writes=[tW["sd"]])
            P.op("act", lambda e: e.activation(out=sd[:, 8:12], in_=sd[:, 8:12], func=AF.Exp),
                 reads=[tW["sd"]], writes=[tW["sd"]])
            P.op("act", lambda e, gbr=gbr, Hh=Hh: e.activation(out=Hh["egl"][:], in_=gbr[:, 63:512:64], func=AF.Exp),
                 reads=[t_gbr], writes=[Hh["tD"]])
            for w in range(3):
                x, t_x, d_x = xin.next()
                ch = w * NH + h
                if b == 0:
                    P.op("pool", lambda e, x=x: e.memset(x[:, 0:3], 0.0), writes=[t_x])
                    P.dma("sp", d_x, x[:, 3:515], T["gqkv"][ch, :, 0:512], writes=[t_x])
                else:
                    P.dma("sp", d_x, x[:], T["gqkv"][ch, :, t0 - 3:t0 + 512], writes=[t_x])
                xs = Wk["xs"][w]
                tx = tW[f"xs{w}"]
                P.op("act", lambda e, x=x, xs=xs, ch=ch: e.activation(out=xs[:], in_=x[:, 3:515], func=AF.Copy,
                                                                     scale=gcw[:, ch, 3:4]), reads=[t_x, t_c], writes=[tx])
                for k in range(1, 4):
                    P.op("dve", lambda e, x=x, xs=xs, ch=ch, k=k: e.scalar_tensor_tensor(
                        out=xs[:], in0=x[:, 3 - k:515 - k], scalar=gcw[:, ch, 3 - k:4 - k], in1=xs[:],
                        op0=ALU.mult, op1=ALU.add), reads=[t_x, t_c], writes=[tx])
                P.op("act", lambda e, xs=xs: e.activation(out=xs[:], in_=xs[:], func=AF.Silu), reads=[tx], writes=[tx])
            xq, xk, xv = Wk["xs"]
            for w, xx in ((0, xq), (1, xk)):
                tx = tW[f"xs{w}"]
                P.op("act", lambda e, xx=xx: e.activation(out=Wk["sq"][:], in_=xx[:], func=AF.Square),
                     reads=[tx], writes=[tW["sq"]])
                pb, tpb = bank()
                P.op("pe", lambda e, pb=pb: e.matmul(out=pb[:], lhsT=onesf[:], rhs=Wk["sq"][:], start=True, stop=True),
                     reads=[tW["sq"], t_c], writes=[tpb])
                P.op("dve", lambda e, pb=pb: e.tensor_scalar(out=Wk["rn"][:], in0=pb[:], scalar1=1e-6, scalar2=None,
                                                            op0=ALU.add), reads=[tpb], writes=[tW["rn"]])
                P.op("act", lambda e: e.activation(out=Wk["rn"][:], in_=Wk["rn"][:], func=AF.Sqrt),
                     reads=[tW["rn"]], writes=[tW["rn"]])
                P.op("dve", lambda e: e.reciprocal(out=Wk["rn"][:], in_=Wk["rn"][:]), reads=[tW["rn"]], writes=[tW["rn"]])
                if w == 0:
                    P.op("dve", lambda e, xx=xx: e.scalar_tensor_tensor(
                        out=xx[:], in0=xx[:], scalar=128.0 ** -0.5, in1=Wk["rn"][:], op0=ALU.mult, op1=ALU.mult),
                        reads=[tx, tW["rn"]], writes=[tx])
                else:
                    P.op("dve", lambda e, xx=xx: e.tensor_tensor(out=xx[:], in0=xx[:], in1=Wk["rn"][:], op=ALU.mult),
                         reads=[tx, tW["rn"]], writes=[tx])
            P.op("pool", lambda e: e.tensor_copy(out=Wk["qTb"][:], in_=xq[:]), reads=[tW["xs0"]], writes=[tW["qTb"]])
            P.op("pool", lambda e: e.tensor_copy(out=Wk["kTb"][:], in_=xk[:]), reads=[tW["xs1"]], writes=[tW["kTb"]])
            P.op("act", lambda e, gbr=gbr: e.activation(out=Wk["egb"][:], in_=gbr[:], func=AF.Exp),
                 reads=[t_gbr], writes=[tW["egb"]])
            P.op("pool", lambda e, Hh=Hh: e.tensor_tensor(out=Hh["qg"][:], in0=xq[:], in1=Wk["egb"][:], op=ALU.mult),
                 reads=[tW["xs0"], tW["egb"]], writes=[Hh["tD"]])
            X = Wk["X"]
            pv, tpv = bank()
            for t in range(4):
                P.op("pe", lambda e, pv=pv, t=t: e.transpose(out=pv[:, t * 128:(t + 1) * 128],
                                                            in_=xv[:, t * 128:(t + 1) * 128], identity=identf[:]),
                     reads=[tW["xs2"], t_c], writes=[tpv])
            P.op("act", lambda e, pv=pv: e.activation(out=X[:, :, 0:128], in_=pv[:].rearrange("p (t d) -> p t d", t=4),
                                                     func=AF.Copy), reads=[tpv], writes=[tW["X"]])
            pk, tpk = bank()
            for t in range(4):
                P.op("pe", lambda e, pk=pk, t=t: e.transpose(out=pk[:, t * 128:(t + 1) * 128],
                                                            in_=xk[:, t * 128:(t + 1) * 128], identity=identf[:]),
                     reads=[tW["xs1"], t_c], writes=[tpk])
            for t in range(4):
                P.op("act", lambda e, pk=pk, t=t: e.activation(out=X[:, t, 128:256], in_=pk[:, t * 128:(t + 1) * 128],
                                                              func=AF.Copy, scale=sd[:, t:t + 1]),
                     reads=[tpk, tW["sd"]], writes=[tW["X"]])
                P.op("act", lambda e, pk=pk, t=t, Hh=Hh: e.activation(
                    out=Hh["kd"][:, t, :], in_=pk[:, t * 128:(t + 1) * 128], func=AF.Copy, scale=sd[:, 8 + t:9 + t]),
                    reads=[tpk, tW["sd"]], writes=[Hh["tD"]])
            for t in range(4):
                P.op("dve", lambda e, t=t, gbr=gbr, sc=sc: e.tensor_scalar(
                    out=Wk["dec"][:, t * 128:(t + 1) * 128], in0=gbr[:, t * 128:(t + 1) * 128], scalar1=sc[:, t:t + 1],
                    scalar2=0.0, op0=ALU.subtract, op1=ALU.min), reads=[t_gbr, t_sc], writes=[tW["dec"]])
            P.op("act", lambda e: e.activation(out=Wk["dec"][:], in_=Wk["dec"][:], func=AF.Exp),
                 reads=[tW["dec"]], writes=[tW["dec"]])
            P.op("pool", lambda e: e.tensor_tensor(out=Wk["dI"][:], in0=Wk["dec"][:], in1=mI[:], op=ALU.mult),
                 reads=[tW["dec"], t_c], writes=[tW["dI"]])
            P.op("pool", lambda e: e.tensor_tensor(out=Wk["dS"][:], in0=Wk["dec"][:], in1=mS[:], op=ALU.mult),
                 reads=[tW["dec"], t_c], writes=[tW["dS"]])
            pkk, tpkk = bank()
            pqk, tpqk = bank()
            for t in range(4):
                sl = slice(t * 128, (t + 1) * 128)
                P.op("pe", lambda e, sl=sl, pkk=pkk: e.matmul(out=pkk[:, sl], lhsT=Wk["kTb"][:, sl], rhs=Wk["kTb"][:, sl],
                                                             start=True, stop=True), reads=[tW["kTb"]], writes=[tpkk])
                P.op("pe", lambda e, sl=sl, pqk=pqk: e.matmul(out=pqk[:, sl], lhsT=Wk["kTb"][:, sl], rhs=Wk["qTb"][:, sl],
                                                             start=True, stop=True), reads=[tW["kTb"], tW["qTb"]], writes=[tpqk])
            P.op("dve", lambda e, pqk=pqk, Hh=Hh: e.tensor_tensor(
                out=Hh["AT"][:].rearrange("p t d -> p (t d)"), in0=pqk[:], in1=Wk["dI"][:], op=ALU.mult),
                reads=[tpqk, tW["dI"]], writes=[Hh["tD"]])
            A, B = Wk["A"], Wk["B"]
            for t in range(4):
                sl = slice(t * 128, (t + 1) * 128)
                P.op("dve", lambda e, sl=sl, t=t, pkk=pkk: e.scalar_tensor_tensor(
                    out=A[0][:, sl], in0=pkk[:, sl], scalar=sd[:, 4 + t:5 + t], in1=Wk["dS"][:, sl],
                    op0=ALU.mult, op1=ALU.mult), reads=[tpkk, tW["sd"], tW["dS"]], writes=[tW["A0"]])
            pbt, tpbt = bank()
            for t in range(4):
                sl = slice(t * 128, (t + 1) * 128)
                P.op("pe", lambda e, sl=sl, pbt=pbt: e.transpose(out=pbt[:, sl], in_=A[0][:, sl], identity=identf[:]),
                     reads=[tW["A0"], t_c], writes=[tpbt])
            P.op("act", lambda e, pbt=pbt: e.activation(out=B[0][:], in_=pbt[:], func=AF.Copy),
                 reads=[tpbt], writes=[tW["B0"]])

            def xupd(cur):
                px0, tpx0 = bank()
                px1, tpx1 = bank()
                for t in range(4):
                    px, tpx = (px0, tpx0) if t < 2 else (px1, tpx1)
                    o_ = (t % 2) * 256
                    P.op("pe", lambda e, px=px, o_=o_, t=t: e.matmul(
                        out=px[:, o_:o_ + 256], lhsT=A[cur][:, t * 128:(t + 1) * 128], rhs=X[:, t, :],
                        start=True, stop=True), reads=[tW[f"A{cur}"], tW["X"]], writes=[tpx])
                for hf, (px, tpx) in enumerate(((px0, tpx0), (px1, tpx1))):
                    P.op("dve", lambda e, px=px, hf=hf: e.tensor_tensor(
                        out=X[:, 2 * hf:2 * hf + 2, :].rearrange("p t d -> p (t d)"),
                        in0=X[:, 2 * hf:2 * hf + 2, :].rearrange("p t d -> p (t d)"), in1=px[:], op=ALU.add),
                        reads=[tpx, tW["X"]], writes=[tW["X"]])

            cur = 0
            xupd(cur)
            for lev in range(5):
                nxt = 1 - cur
                pa, tpa = bank()
                for t in range(4):
                    sl = slice(t * 128, (t + 1) * 128)
                    P.op("pe", lambda e, sl=sl, pa=pa, cur=cur: e.matmul(out=pa[:, sl], lhsT=B[cur][:, sl], rhs=A[cur][:, sl],
                                                                        start=True, stop=True),
                         reads=[tW[f"A{cur}"], tW[f"B{cur}"]], writes=[tpa])
                if lev < 4:
                    pb2, tpb2 = bank()
                    for t in range(4):
                        sl = slice(t * 128, (t + 1) * 128)
                        P.op("pe", lambda e, sl=sl, pb2=pb2, cur=cur: e.matmul(
                            out=pb2[:, sl], lhsT=A[cur][:, sl], rhs=B[cur][:, sl], start=True, stop=True),
                            reads=[tW[f"A{cur}"], tW[f"B{cur}"]], writes=[tpb2])
                P.op("act", lambda e, pa=pa, nxt=nxt: e.activation(out=A[nxt][:], in_=pa[:], func=AF.Copy),
                     reads=[tpa], writes=[tW[f"A{nxt}"]])
                if lev < 4:
                    P.op("dve", lambda e, pb2=pb2, nxt=nxt: e.tensor_copy(out=B[nxt][:], in_=pb2[:]),
                         reads=[tpb2], writes=[tW[f"B{nxt}"]])
                cur = nxt
                xupd(cur)
            for t in range(4):
                P.op("act", lambda e, t=t, Hh=Hh, sc=sc: e.activation(out=Hh["UW"][:, t, :], in_=X[:, t, :], func=AF.Copy,
                                                                    scale=sc[:, 4 + t:5 + t]),
                     reads=[tW["X"], t_sc], writes=[Hh["tD"]])
            pw, tpw = bank()
            for t in range(4):
                P.op("pe", lambda e, t=t, pw=pw, Hh=Hh: e.transpose(out=pw[:, t * 128:(t + 1) * 128],
                                                                  in_=Hh["UW"][:, t, 128:256], identity=identf[:]),
                     reads=[Hh["tD"], t_c], writes=[tpw])
            P.op("dve", lambda e, pw=pw, Hh=Hh: e.tensor_copy(out=Hh["wT"][:], in_=pw[:]), reads=[tpw], writes=[Hh["tD"]])

        def rec(b):
            t0 = b * 512
            gzl = []
            for h in range(NH):
                gz, t_gz, d_gz = gzs.next()
                P.dma("sp", d_gz, gz[:], T["gz"][t0:t0 + 512, h * 128:(h + 1) * 128].rearrange("(s p) d -> p s d", p=128),
                      writes=[t_gz])
                ys_, t_ys, d_ys = yTs.next()
                gzl.append((gz, t_gz, ys_, t_ys, d_ys))
            for c in range(8):
                t = c // 2
                r0 = (c % 2) * 64
                rs_ = slice(r0, r0 + 64)
                for h in range(NH):
                    Hh = dict(H[h][b % 2])
                    Hh.update(R[h])
                    hs = slice(h * 128, (h + 1) * 128)
                    cs = slice(t * 128 + r0, t * 128 + r0 + 64)
                    P.op("pe", lambda e, Hh=Hh, hs=hs, cs=cs, rs_=rs_: e.matmul(
                        out=p1[rs_, hs], lhsT=Hh["wT"][:, cs], rhs=Hh["S"][:], start=True, stop=True),
                        reads=[Hh["tD"], Hh["tS"]], writes=[t_p1[h]])
                    P.op("dve", lambda e, Hh=Hh, hs=hs, rs_=rs_, t=t: e.tensor_tensor(
                        out=Hh["vn"][rs_, t, :], in0=Hh["UW"][rs_, t, 0:128], in1=p1[rs_, hs], op=ALU.subtract),
                        reads=[Hh["tD"], t_p1[h]], writes=[Hh["tvn"]])
                    P.op("pe", lambda e, Hh=Hh, hs=hs, cs=cs, rs_=rs_: e.matmul(
                        out=po[rs_, hs], lhsT=Hh["qg"][:, cs], rhs=Hh["S"][:], start=True, stop=False),
                        reads=[Hh["tD"], Hh["tS"]], writes=[t_po[h]])
                    P.op("pe", lambda e, Hh=Hh, hs=hs, rs_=rs_, t=t, r0=r0: e.matmul(
                        out=po[rs_, hs], lhsT=Hh["AT"][rs_, t, r0:r0 + 64], rhs=Hh["vn"][rs_, t, :], start=False, stop=True),
                        reads=[Hh["tD"], Hh["tvn"]], writes=[t_po[h]])
                    P.op("pe", lambda e, Hh=Hh, hs=hs, rs_=rs_, t=t: e.matmul(
                        out=pS[:, hs], lhsT=Hh["kd"][rs_, t, :], rhs=Hh["vn"][rs_, t, :], start=True, stop=True),
                        reads=[Hh["tD"], Hh["tvn"]], writes=[t_pS[h]])
                    P.op("dve", lambda e, Hh=Hh, hs=hs, c=c: e.scalar_tensor_tensor(
                        out=Hh["S"][:], in0=Hh["S"][:], scalar=Hh["egl"][:, c:c + 1], in1=pS[:, hs],
                        op0=ALU.mult, op1=ALU.add), reads=[Hh["tD"], t_pS[h], Hh["tS"]], writes=[Hh["tS"]])
                    if c % 2 == 1:
                        gz, t_gz, ys_, t_ys, d_ys = gzl[h]
                        P.op("dve", lambda e, Hh=Hh, hs=hs: e.tensor_copy(out=Hh["y"][:], in_=po[:, hs]),
                             reads=[t_po[h]], writes=[Hh["ty"]])
                        P.op("act", lambda e, Hh=Hh, hs=hs: e.activation(out=Hh["junk"][:], in_=Hh["y"][:], func=AF.Square,
                                                                        accum_out=Hh["ss"][:]),
                             reads=[Hh["ty"]], writes=[Hh["ty"]])
                        P.op("dve", lambda e, Hh=Hh: e.tensor_scalar(out=Hh["ss"][:], in0=Hh["ss"][:], scalar1=1.0 / 128,
                                                                     scalar2=1e-6, op0=ALU.mult, op1=ALU.add),
                             reads=[Hh["ty"]], writes=[Hh["ty"]])
                        P.op("act", lambda e, Hh=Hh: e.activation(out=Hh["ss"][:], in_=Hh["ss"][:], func=AF.Sqrt),
                             reads=[Hh["ty"]], writes=[Hh["ty"]])
                        P.op("dve", lambda e, Hh=Hh: e.reciprocal(out=Hh["ss"][:], in_=Hh["ss"][:]),
                             reads=[Hh["ty"]], writes=[Hh["ty"]])
                        P.op("dve", lambda e, Hh=Hh, hs=hs: e.scalar_tensor_tensor(
                            out=Hh["y"][:], in0=Hh["y"][:], scalar=Hh["ss"][:], in1=gnw[:], op0=ALU.mult, op1=ALU.mult),
                            reads=[Hh["ty"], t_c], writes=[Hh["ty"]])
                        P.op("pool", lambda e, Hh=Hh, gz=gz, t=t: e.tensor_tensor(out=Hh["yb"][:], in0=Hh["y"][:], in1=gz[:, t, :],
                                                                                 op=ALU.mult),
                             reads=[Hh["ty"], t_gz], writes=[Hh["ty"]])
                        P.op("pe", lambda e, Hh=Hh, hs=hs: e.transpose(out=ptr[:, hs], in_=Hh["yb"][:], identity=ident[:]),
                             reads=[Hh["ty"], t_c], writes=[t_ptr[h]])
                        P.op("act", lambda e, hs=hs, ys_=ys_, t=t: e.activation(out=ys_[:, t * 128:(t + 1) * 128], in_=ptr[:, hs],
                                                                              func=AF.Copy), reads=[t_ptr[h]], writes=[t_ys])
                        if t == 3:
                            P.dma("sp", d_ys, T["yT"][NH + h, :, t0:t0 + 512], ys_[:], reads=[t_ys])

        for b in range(NB + 1):
            streams = []
            if b < NB:
                for h in range(NH):
                    P.rec = []
                    prep(b, h)
                    streams.append(P.rec)
            if b > 0:
                P.rec = []
                rec(b - 1)
                streams.append(P.rec)
            P.rec = None
            P.run_merged(streams)
        P.flush()


def declare_inputs(nc, S):
    I = {}

    def inp(name, shape, dt=F32):
        I[name] = nc.dram_tensor(name, shape, dt, kind="ExternalInput").ap()
    inp("x", [S, D])
    inp("norm_w", [2, D])
    inp("w_in", [2, D, NCOL])
    inp("attn_lambda", [2, 256])
    inp("attn_subln_w", [2, 128])
    inp("gdn_conv_w", [2, 4, 3 * HB])
    inp("gdn_a_log", [2, NH])
    inp("gdn_dt_bias", [2, NH])
    inp("gdn_norm_w", [2, 128])
    inp("lru_conv_w", [2, 4, HB])
    inp("lru_conv_b", [2, HB])
    inp("lru_gate_w", [2, 2, NH, 128, 128])
    inp("lru_gate_b", [2, 2, HB])
    inp("lru_log_param", [2, HB])
    inp("w_out", [2, 1536, 512])
    inp("final_norm_w", [1, 512])
    inp("xh", [S, 512])
    inp("ident", [128, 128], BF16)
    inp("identf", [128, 128], F32)
    inp("masks", [2, 128, 512], F32)
    inp("onesf", [128, 128], F32)
    inp("rmask", [8, 2048], F32)
    return I


def const_inputs():
    ident = np.eye(128, dtype=np.float32)
    i = np.arange(128)
    same = (i[:, None] // 64) == (i[None, :] // 64)
    m_incl = (same & (i[None, :] >= i[:, None])).astype(np.float32)
    m_strict = (same & (i[None, :] > i[:, None])).astype(np.float32)
    masks = np.stack([np.tile(m_incl, (1, 4)), np.tile(m_strict, (1, 4))])
    rmask = np.tile((np.arange(2048) % 64 != 0).astype(np.float32)[None, :], (8, 1))
    return {"ident": ident.astype(ml_dtypes.bfloat16), "identf": ident, "masks": masks,
            "onesf": np.ones((128, 128), np.float32), "rmask": rmask}


def pack_inputs(inp, b, hh, S=None):
    x = inp["x"][b] if S is None else inp["x"][b][:S]
    h2 = slice(hh * HB, (hh + 1) * HB)
    nh = slice(hh * NH, (hh + 1) * NH)
    w = inp["w_in"]
    cols = []
    for br in range(8):
        cols.append(np.arange(br * 512 + hh * HB, br * 512 + (hh + 1) * HB))
    cols.append(np.arange(4096 + hh * NH, 4096 + (hh + 1) * NH))
    cols.append(np.arange(4100 + hh * NH, 4100 + (hh + 1) * NH))
    cols.append(np.arange(4104 + hh * HB, 4104 + (hh + 1) * HB))
    cols.append(np.arange(4616 + hh * HB, 4616 + (hh + 1) * HB))
    cols = np.concatenate(cols)
    gcw = inp["gdn_conv_w"]
    gcw = np.concatenate([gcw[:, :, br * 512 + hh * HB:br * 512 + (hh + 1) * HB] for br in range(3)], axis=-1)
    rows = []
    for r in range(2):
        for br in range(3):
            rows.append(np.arange(br * 512 + r * HB, br * 512 + (r + 1) * HB))
    rows = np.concatenate(rows)
    dh = slice(hh * 512, (hh + 1) * 512)
    m = {"x": x, "xh": x[:, dh], "norm_w": inp["norm_w"], "w_in": w[:, :, cols],
         "attn_lambda": inp["attn_lambda"].reshape(2, 256), "attn_subln_w": inp["attn_subln_w"],
         "gdn_conv_w": gcw, "gdn_a_log": inp["gdn_a_log"][:, nh], "gdn_dt_bias": inp["gdn_dt_bias"][:, nh],
         "gdn_norm_w": inp["gdn_norm_w"], "lru_conv_w": inp["lru_conv_w"][:, :, h2], "lru_conv_b": inp["lru_conv_b"][:, h2],
         "lru_gate_w": inp["lru_gate_w"][:, :, nh], "lru_gate_b": inp["lru_gate_b"][:, :, h2],
         "lru_log_param": inp["lru_log_param"][:, h2], "w_out": inp["w_out"][:, rows][:, :, dh],
         "final_norm_w": inp["final_norm_w"][dh].reshape(1, 512)}
    m.update(const_inputs())
    return {k: np.ascontiguousarray(v) for k, v in m.items()}


def build(S, phases="ABCDE", dbg=False, nlayers=DEPTH):
    nc = bass.Bass("TRN2", target_bir_lowering=False)
    I = declare_inputs(nc, S)
    out = nc.dram_tensor("out", [S, 512], F32, kind="ExternalOutput").ap()
    T = make_dram(nc, S, dbg)
    XC = min(S, 1024)
    with ExitStack() as stack:
        P = Prog(nc, stack)
        xg_pre, xg_toks = None, None
        for l in range(nlayers):
            if "A" in phases:
                phase_A(nc, P, l, S, I["x"] if l == 0 else None, I, T, pre=xg_pre, xg_toks=xg_toks)
            if "B" in phases:
                phase_B(nc, P, l, S, I, T)
            full = "E" in phases
            if "C" in phases:
                phase_C(nc, P, l, S, I, T, pre=(lambda: start_allgather(
                    P, [(T["yT"][c], T["yTg"][c]) for c in range(0, NH)])) if full else None)
            if "D" in phases:
                phase_D(nc, P, l, S, I, T, pre=(lambda: start_allgather(
                    P, [(T["yT"][c], T["yTg"][c]) for c in range(2 * NH, 3 * NH)])) if full else None)
            if full:
                allgather(nc, P, [(T["yT"][c], T["yTg"][c]) for c in range(NH, 2 * NH)])
                xh_src = I["xh"] if l == 0 else T["xh1"]
                xh_dst = T["xh1"] if l == 0 else T["xh2"]
                phase_E(nc, P, l, S, xh_src, xh_dst, I, T)
                xg = T["x1g"] if l == 0 else T["x2g"]
                xg_toks = [Tok() for _ in range(S // XC)]
                xg_pre = (lambda xh_dst=xh_dst, xg=xg, toks=xg_toks: start_allgather(
                    P, [(xh_dst[tc * XC:(tc + 1) * XC, :], xg[tc]) for tc in range(S // XC)], toks))
                if l == DEPTH - 1:
                    phase_F(nc, P, S, I, T, out, pre=xg_pre, xg_toks=xg_toks)
    return nc


def kernel(**inputs):
    inp = {k: np.asarray(v) for k, v in inputs.items()}
    B, S, _ = inp["x"].shape
    nc = build(S)
    in_maps = [pack_inputs(inp, c // 2, c % 2) for c in range(2 * B)]
    res = run_bass_kernel_spmd(nc, in_maps, core_ids=list(range(2 * B)))
    out = np.empty((B, S, D), np.float32)
    for c in range(2 * B):
        out[c // 2, :, (c % 2) * 512:(c % 2 + 1) * 512] = np.asarray(res.results[c]["out"], dtype=np.float32)
    return out
```
